# Optimizing a Trainium2 kernel written in Bass

```python
import jax, jax.numpy as jnp
from jax import lax
import numpy as np

D_MODEL = 1024
BATCH = 16
SEQ = 2048
DEPTH = 2
DEC_BATCH = 32
DEC_SEQ = 8
PAST_LEN = 16384
PAGE_SIZE = 128

GM_WIDTH = 1024
GM_GROUPS = 8
GM_GROUP_CH = GM_WIDTH // GM_GROUPS
CHUNK = 128
HEAD_DIM = 64
HEADS_PER_GROUP = 8
DIL_CONFIGS = ((128, 1), (512, 4), (2048, 16))
N_DIL = 3
ATT_QKV_WIDTH = N_DIL * HEADS_PER_GROUP * HEAD_DIM
ATT_OUT_WIDTH = HEADS_PER_GROUP * HEAD_DIM
BAND = 128
ROPE_THETA = 10000.0
RMS_EPS = 1e-6
LN_EPS = 1e-5
NEG = -1e30

kernel_name = 'dilated_gmlp_hybrid_step'


def _proj_sizes():
    return (GM_WIDTH, GM_WIDTH, GM_WIDTH, ATT_QKV_WIDTH, ATT_QKV_WIDTH, ATT_QKV_WIDTH,
            ATT_OUT_WIDTH, D_MODEL, D_MODEL)


def split_projection(proj):
    parts, start = [], 0
    for size in _proj_sizes():
        parts.append(proj[..., start:start + size])
        start += size
    return parts


def rms_norm(x, g):
    xf = x.astype(jnp.float32)
    y = xf * lax.rsqrt(jnp.mean(xf * xf, axis=-1, keepdims=True) + RMS_EPS)
    return (y * g.astype(jnp.float32)).astype(x.dtype)


def layer_norm(x, g, b):
    xf = x.astype(jnp.float32)
    mu = jnp.mean(xf, axis=-1, keepdims=True)
    var = jnp.mean(jnp.square(xf - mu), axis=-1, keepdims=True)
    y = (xf - mu) * lax.rsqrt(var + LN_EPS) * g.astype(jnp.float32) + b.astype(jnp.float32)
    return y.astype(x.dtype)


def rope(x, pos):
    half = HEAD_DIM // 2
    inv_freq = ROPE_THETA ** (-jnp.arange(half, dtype=jnp.float32) / half)
    ang = pos.astype(jnp.float32)[:, None] * inv_freq[None, :]
    ang = ang.reshape(ang.shape[:1] + (1,) * (x.ndim - 3) + (half,))
    cos, sin = jnp.cos(ang), jnp.sin(ang)
    xf = x.astype(jnp.float32)
    x1, x2 = xf[..., :half], xf[..., half:]
    return jnp.concatenate([x1 * cos - x2 * sin, x2 * cos + x1 * sin], axis=-1).astype(x.dtype)


def dilated_band_attention(q, k, v, dil, n_back):
    B, S, H, hd = q.shape
    n = S // dil
    nb = -(-n // BAND)
    n_pad = nb * BAND

    def to_sub(a):
        return a.reshape(B, n, dil, H, hd).transpose(0, 2, 1, 3, 4)

    qs = jnp.pad(to_sub(q), ((0, 0), (0, 0), (0, n_pad - n), (0, 0), (0, 0)))
    kv_pad = ((0, 0), (0, 0), (BAND, n_pad - n), (0, 0), (0, 0))
    ks = jnp.pad(to_sub(k), kv_pad)
    vs = jnp.pad(to_sub(v), kv_pad)
    qb = qs.reshape(B, dil, nb, BAND, H, hd)

    def band(a):
        prev = a[:, :, :n_pad].reshape(B, dil, nb, BAND, H, hd)
        cur = a[:, :, BAND:].reshape(B, dil, nb, BAND, H, hd)
        return jnp.concatenate([prev, cur], axis=3)

    kb, vb = band(ks), band(vs)
    s = jnp.einsum('brnqhd,brnkhd->brnhqk', qb, kb,
                   preferred_element_type=jnp.float32) * (HEAD_DIM ** -0.5)
    blk = jnp.arange(nb)[:, None, None]
    qi = jnp.arange(BAND)[None, :, None]
    kj = jnp.arange(2 * BAND)[None, None, :]
    dist = qi + BAND - kj
    key_pos = blk * BAND - BAND + kj
    valid = (dist >= 0) & (dist <= n_back) & (key_pos >= 0)
    s = jnp.where(valid[None, None, :, None], s, NEG)
    lse = jax.nn.logsumexp(s, axis=-1)
    p = jnp.exp(s - lse[..., None])
    o = jnp.einsum('brnhqk,brnkhd->brnqhd', p.astype(vb.dtype), vb,
                   preferred_element_type=jnp.float32)
    o = o.reshape(B, dil, n_pad, H, hd)[:, :, :n].transpose(0, 2, 1, 3, 4).reshape(B, S, H, hd)
    lse = lse.transpose(0, 1, 2, 4, 3).reshape(B, dil, n_pad, H)[:, :, :n]
    lse = lse.transpose(0, 2, 1, 3).reshape(B, S, H)
    return o, lse


def dilated_gather_attention(q, k_all, v_all, dil, n_back):
    T = q.shape[1]
    n_prev = k_all.shape[1] - T
    idx = n_prev + jnp.arange(T)[:, None] - dil * jnp.arange(n_back + 1)[None, :]
    valid = idx >= 0
    idx = jnp.maximum(idx, 0)
    kg = jnp.take(k_all, idx, axis=1)
    vg = jnp.take(v_all, idx, axis=1)
    s = jnp.einsum('bthd,btjhd->bhtj', q, kg,
                   preferred_element_type=jnp.float32) * (HEAD_DIM ** -0.5)
    s = jnp.where(valid[None, None], s, NEG)
    lse = jax.nn.logsumexp(s, axis=-1)
    p = jnp.exp(s - lse[..., None])
    o = jnp.einsum('bhtj,btjhd->bthd', p.astype(vg.dtype), vg,
                   preferred_element_type=jnp.float32)
    return o, lse.transpose(0, 2, 1)


def combine_dilations(outs, lses):
    w = jax.nn.softmax(jnp.stack(lses, axis=0), axis=0)
    return jnp.sum(w[..., None] * jnp.stack(outs, axis=0), axis=0)


def gmlp_spatial_prompt(vn, wm, bs):
    B, S, _ = vn.shape
    vc = vn.reshape(B, S // CHUNK, CHUNK, GM_GROUPS, GM_GROUP_CH)
    y = jnp.einsum('gts,bcsgd->bctgd', wm.astype(vn.dtype), vc) + bs.T[:, :, None].astype(vn.dtype)
    return y.reshape(B, S, GM_WIDTH)


def gmlp_spatial_sample(vn, wm, bs):
    B, T, _ = vn.shape
    vc = vn.reshape(B, T, GM_GROUPS, GM_GROUP_CH)
    y = jnp.einsum('gts,bsgd->btgd', wm[:, :T, :T].astype(vn.dtype), vc) + bs.T[:T, :, None].astype(vn.dtype)
    return y.reshape(B, T, GM_WIDTH)


def mixer_layer(x, c, pos, att_core, gm_spatial, w_ada, b_ada, norm_g, w_in,
                gm_ln_g, gm_ln_b, w_gm_out, w_att_out, w_o):
    Bx, S, _ = x.shape
    mod = jnp.dot(jax.nn.silu(c), w_ada) + b_ada
    shift, scale, gate = jnp.split(mod, 3, axis=-1)
    h = rms_norm(x, norm_g) * (1.0 + scale[:, None]) + shift[:, None]
    u, v, z_a, q, k, val, z_b, g_a, g_b = split_projection(jnp.dot(h, w_in))
    vn = layer_norm(jax.nn.gelu(v), gm_ln_g, gm_ln_b)
    y_a = jax.nn.gelu(u) * gm_spatial(vn) * jax.nn.silu(z_a)
    shp = (Bx, S, N_DIL, HEADS_PER_GROUP, HEAD_DIM)
    q = rope(q.reshape(shp), pos)
    k = rope(k.reshape(shp), pos)
    val = val.reshape(shp)
    y_b = att_core(q, k, val).astype(x.dtype).reshape(Bx, S, ATT_OUT_WIDTH) * jax.nn.silu(z_b)
    merged = (jax.nn.sigmoid(g_a) * jnp.dot(y_a, w_gm_out)
              + jax.nn.sigmoid(g_b) * jnp.dot(y_b, w_att_out))
    x = x + gate[:, None] * jnp.dot(merged, w_o)
    return x, k, val, vn


def setup_inputs(seed: int = 0) -> dict:
    key = jax.random.key(seed)
    ks = jax.random.split(key, 24)
    f32 = jnp.float32

    def nrm(k, shape, scale):
        return scale * jax.random.normal(k, shape, f32)

    in_width = sum(_proj_sizes())
    cshape = lambda win: (DEPTH, DEC_BATCH, min(win, PAST_LEN), 2, HEADS_PER_GROUP, HEAD_DIM)
    return {
        'x_prompt': nrm(ks[0], (BATCH, SEQ, D_MODEL), 1.0),
        'x_sample': nrm(ks[1], (DEC_BATCH, DEC_SEQ, D_MODEL), 1.0),
        'cache_kv_w128': nrm(ks[2], cshape(128), 1.0),
        'cache_kv_w512': nrm(ks[3], cshape(512), 1.0),
        'cache_kv_w2048': nrm(ks[4], cshape(2048), 1.0),
        'c_prompt': nrm(ks[5], (BATCH, D_MODEL), 1.0),
        'c_sample': nrm(ks[6], (DEC_BATCH, D_MODEL), 1.0),
        'w_ada': nrm(ks[7], (DEPTH, D_MODEL, 3 * D_MODEL), 0.5 * D_MODEL ** -0.5),
        'b_ada': nrm(ks[8], (DEPTH, 3 * D_MODEL), 0.01),
        'norm_g': 1.0 + nrm(ks[9], (DEPTH, D_MODEL), 0.05),
        'w_in': nrm(ks[10], (DEPTH, D_MODEL, in_width), D_MODEL ** -0.5),
        'gm_ln_g': 1.0 + nrm(ks[11], (DEPTH, GM_WIDTH), 0.05),
        'gm_ln_b': nrm(ks[12], (DEPTH, GM_WIDTH), 0.02),
        'gm_ws': nrm(ks[13], (DEPTH, GM_GROUPS, CHUNK, CHUNK), CHUNK ** -0.5),
        'gm_bs': 1.0 + nrm(ks[14], (DEPTH, GM_GROUPS, CHUNK), 0.1),
        'w_gm_out': nrm(ks[15], (DEPTH, GM_WIDTH, D_MODEL), GM_WIDTH ** -0.5),
        'w_att_out': nrm(ks[16], (DEPTH, ATT_OUT_WIDTH, D_MODEL), ATT_OUT_WIDTH ** -0.5),
        'w_o': nrm(ks[17], (DEPTH, D_MODEL, D_MODEL), D_MODEL ** -0.5),
        'final_g': 1.0 + nrm(ks[18], (D_MODEL,), 0.05),
    }


def reference(x_prompt, x_sample, cache_kv_w128, cache_kv_w512, cache_kv_w2048,
              c_prompt, c_sample, w_ada, b_ada, norm_g, w_in, gm_ln_g, gm_ln_b,
              gm_ws, gm_bs, w_gm_out, w_att_out, w_o, final_g):
    caches = (cache_kv_w128, cache_kv_w512, cache_kv_w2048)
    S = x_prompt.shape[1]
    T = x_sample.shape[1]
    pos_p = jnp.arange(S, dtype=jnp.int32)
    pos_s = PAST_LEN + jnp.arange(T, dtype=jnp.int32)
    causal = jnp.tril(jnp.ones((CHUNK, CHUNK), dtype=bool))
    xp, xs = x_prompt, x_sample
    new_p = [[] for _ in DIL_CONFIGS]
    new_s = [[] for _ in DIL_CONFIGS]
    gm_v_rows = []
    for l in range(DEPTH):
        wm = jnp.where(causal, gm_ws[l], 0.0)
        bs = gm_bs[l]
        lw = (w_ada[l], b_ada[l], norm_g[l], w_in[l], gm_ln_g[l], gm_ln_b[l],
              w_gm_out[l], w_att_out[l], w_o[l])

        def att_prompt(q, k, v):
            outs, lses = [], []
            for g, (win, dil) in enumerate(DIL_CONFIGS):
                o, lse = dilated_band_attention(q[:, :, g], k[:, :, g], v[:, :, g], dil, win // dil)
                outs.append(o)
                lses.append(lse)
            return combine_dilations(outs, lses)

        def att_sample(q, k, v, layer=l):
            outs, lses = [], []
            for g, (win, dil) in enumerate(DIL_CONFIGS):
                buf = caches[g][layer]
                k_all = jnp.concatenate([buf[:, :, 0], k[:, :, g]], axis=1)
                v_all = jnp.concatenate([buf[:, :, 1], v[:, :, g]], axis=1)
                o, lse = dilated_gather_attention(q[:, :, g], k_all, v_all, dil, win // dil)
                outs.append(o)
                lses.append(lse)
            return combine_dilations(outs, lses)

        xp, kp, vp, _ = mixer_layer(xp, c_prompt, pos_p, att_prompt,
                                    lambda vn: gmlp_spatial_prompt(vn, wm, bs), *lw)
        xs, ks_, vs_, vn_s = mixer_layer(xs, c_sample, pos_s, att_sample,
                                         lambda vn: gmlp_spatial_sample(vn, wm, bs), *lw)
        for g, (win, dil) in enumerate(DIL_CONFIGS):
            n_keep = min(win, S)
            new_p[g].append(jnp.stack([kp[:, S - n_keep:, g], vp[:, S - n_keep:, g]], axis=2))
            new_s[g].append(jnp.stack([ks_[:, :, g], vs_[:, :, g]], axis=2))
        gm_v_rows.append(vn_s)

    y_prompt = rms_norm(xp, final_g)
    y_sample = rms_norm(xs, final_g)
    kv_w128_p = jnp.stack(new_p[0])
    kv_w512_p = jnp.stack(new_p[1])
    kv_w2048_p = jnp.stack(new_p[2])
    kv_w128_s = jnp.stack(new_s[0])
    kv_w512_s = jnp.stack(new_s[1])
    kv_w2048_s = jnp.stack(new_s[2])
    gm_v_s = jnp.stack(gm_v_rows)
    return (y_prompt, y_sample, kv_w128_p, kv_w512_p, kv_w2048_p,
            kv_w128_s, kv_w512_s, kv_w2048_s, gm_v_s)
```

```python
import numpy as np
from contextlib import ExitStack
import concourse.bass as bass
import concourse.mybir as mybir
from concourse.bass_utils import run_bass_kernel_spmd

F32, BF16 = mybir.dt.float32, mybir.dt.bfloat16
AF = mybir.ActivationFunctionType
ALU = mybir.AluOpType

D = 1024
S = 2048
DEPTH = 2
T = 8
PAST = 16384
NCORES = 8
INW = 10240
OU, OV, OZA, OQ, OK_, OVAL, OZB, OGA, OGB = 0, 1024, 2048, 3072, 4608, 6144, 7680, 8192, 9216
DILS = ((128, 1), (512, 4), (2048, 16))
ENG = ("pe", "act", "dve", "pool", "sp")
EPOCH = 30000
NDS = 24


class Buf:
    __slots__ = ("w", "r")

    def __init__(self):
        self.w = None
        self.r = {}


class Sched:
    def __init__(self, nc, es):
        self.nc = nc
        self.eh = {"pe": nc.tensor, "act": nc.scalar, "dve": nc.vector, "pool": nc.gpsimd, "sp": nc.sync}
        self.cnt = {e: 0 for e in ENG}
        self.sems = {e: [es.enter_context(nc.semaphore(f"s_{e}{i}")) for i in range(3)] for e in ENG}
        self.dsem = [es.enter_context(nc.semaphore(f"d{i}")) for i in range(2 * NDS)]
        self.dtgt = [0] * (2 * NDS)
        self.rr = {"sp": 0, "pool": 0}
        self.waited = {}

    def _waits(self, eng, deps):
        out = []
        for tok in deps:
            if tok[0] == "e":
                key = (eng, "e", tok[1])
                val = (tok[2], tok[3])
                sem = self.sems[tok[1]][tok[2]]
                v = tok[3]
            else:
                key = (eng, "d", tok[1])
                val = (0, tok[2])
                sem = self.dsem[tok[1]]
                v = tok[2]
            if self.waited.get(key, (-1, -1)) >= val:
                continue
            self.waited[key] = val
            out.append((sem, v))
        return out

    def op(self, eng, fn, reads=(), writes=(), dma=False):
        deps = []
        for b in reads:
            if b.w is not None:
                deps.append(b.w)
        for b in writes:
            if b.w is not None:
                deps.append(b.w)
            deps.extend(b.r.values())
        if dma:
            k = self.rr[eng] % NDS + (NDS if eng == "pool" else 0)
            self.rr[eng] += 1
            old = self.dtgt[k]
            self.dtgt[k] += 16
            if old > 0:
                deps.append(("d", k, old))
            tok = ("d", k, self.dtgt[k])
            sem, inc = self.dsem[k], 16
        else:
            c = self.cnt[eng]
            self.cnt[eng] += 1
            ep, v = divmod(c, EPOCH)
            tok = ("e", eng, ep, v + 1)
            sem, inc = self.sems[eng][ep], 1
        waits = self._waits(eng, deps)

        e = self.eh[eng]
        for s_, v_ in waits:
            e.wait_ge(s_, v_)
        fn(e).then_inc(sem, inc)
        rk = ("e", eng) if not dma else ("d", tok[1])
        for b in reads:
            b.r[rk] = tok
        for b in writes:
            b.w = tok
            b.r = {}
        return tok

    def finish(self):
        deps = [("d", k, self.dtgt[k]) for k in range(2 * NDS) if self.dtgt[k] > 0]
        for e in ENG:
            if e != "sp" and self.cnt[e] > 0:
                ep, v = divmod(self.cnt[e] - 1, EPOCH)
                deps.append(("e", e, ep, v + 1))
        waits = self._waits("sp", deps)

        for s_, v_ in waits:
            self.eh["sp"].wait_ge(s_, v_)


def _consts():
    half = 32
    inv = (10000.0 ** (-np.arange(half, dtype=np.float32) / np.float32(half))).astype(np.float32)

    def tab(pos):
        ang = (pos.astype(np.float32)[:, None] * inv[None, :]).astype(np.float32)
        c = np.cos(ang.astype(np.float64)).astype(np.float32)
        s = np.sin(ang.astype(np.float64)).astype(np.float32)
        return np.concatenate([c, c], 1), np.concatenate([-s, s], 1)

    cc, ss = tab(np.arange(S))
    ccs, sss = tab(PAST + np.arange(T))
    k = np.arange(128)[:, None]
    q = np.arange(128)[None, :]
    prev = (k >= q).astype(np.float32)
    cur = (k <= q).astype(np.float32)
    mb = np.concatenate([prev, cur, prev, cur], 1)
    mg = np.zeros((4, 128, 512), np.float32)
    for tq in range(4):
        m = (np.arange(128)[:, None] <= (32 * tq + np.arange(32))[None, :]).astype(np.float32)
        mg[tq] = np.tile(m, (1, 16))
    tri = (np.arange(128)[:, None] >= np.arange(128)[None, :]).astype(np.float32)
    tt = np.arange(T)[None, :]
    rows = np.arange(128)[:, None]
    ms = np.zeros((13, 128, T), np.float32)
    ms[0] = (rows >= tt)
    for r in range(4):
        ms[1 + r] = ((tt % 4) == r) & ((tt < 4) | (rows >= 1))
    for r in range(8):
        ms[5 + r] = (tt == r) & (rows >= 0)
    tk = np.arange(T)[:, None]
    mn = np.zeros((3, T, T), np.float32)
    mn[0] = (tk <= tt)
    mn[1] = (tk <= tt) & (((tt - tk) % 4) == 0)
    mn[2] = (tk == tt)
    return dict(
        c_cc=np.ascontiguousarray(cc), c_ss=np.ascontiguousarray(ss),
        c_ccs=np.ascontiguousarray(np.tile(ccs, (4, 1))), c_sss=np.ascontiguousarray(np.tile(sss, (4, 1))),
        c_mb=mb, c_mg=mg, c_tri=tri, c_id=np.eye(128, dtype=np.float32),
        c_ms=np.ascontiguousarray(ms), c_mn=np.ascontiguousarray(mn),
    )


DBG_STOP = ''
DBG_NG = 99
DBG_SAMPLE = True
DBG_SSTOP = ''
DBG_SA = 9


def build(NBP=2, NBS=4):
    nc = bass.Bass("TRN2", target_bir_lowering=False)
    NC6 = NBP + NBS
    TS = NBS * T

    def din(name, shape, dt=F32):
        return nc.dram_tensor(name, list(shape), dt, kind="ExternalInput").ap()

    def dout(name, shape):
        return nc.dram_tensor(name, list(shape), F32, kind="ExternalOutput").ap()

    def dint(name, shape, dt):
        return nc.dram_tensor(name, list(shape), dt, kind="Internal").ap()

    x_p = din("x_p", [NBP, S, D]); x_s = din("x_s", [TS, D])
    c_all = din("c_all", [NC6, D])
    ck = [din("ck0", [DEPTH, NBS, 128, 2, 512]), din("ck1", [DEPTH, NBS, 512, 2, 512]),
          din("ck2", [DEPTH, NBS, 2048, 2, 512])]
    w_ada = din("w_ada", [DEPTH, D, 3 * D]); b_ada = din("b_ada", [DEPTH, 3 * D])
    norm_g = din("norm_g", [DEPTH, D]); w_in = din("w_in", [DEPTH, D, INW])
    gm_ln_g = din("gm_ln_g", [DEPTH, D]); gm_ln_b = din("gm_ln_b", [DEPTH, D])
    gm_ws = din("gm_ws", [DEPTH, 8, 128, 128]); gm_bs = din("gm_bs", [DEPTH, 8, 128])
    w_gm = din("w_gm_out", [DEPTH, D, D]); w_att = din("w_att_out", [DEPTH, 512, D])
    w_o = din("w_o", [DEPTH, D, D]); final_g = din("final_g", [D])
    c_cc = din("c_cc", [S, 64]); c_ss = din("c_ss", [S, 64])
    c_ccs = din("c_ccs", [32, 64]); c_sss = din("c_sss", [32, 64])
    c_mb = din("c_mb", [128, 512]); c_mg = din("c_mg", [4, 128, 512]); c_tri = din("c_tri", [128, 128])
    c_id = din("c_id", [128, 128]); c_ms = din("c_ms", [13, 128, T]); c_mn = din("c_mn", [3, T, T])

    y_p = dout("y_p", [NBP, S, D]); y_s = dout("y_s", [TS, D])
    kvp = [dout("kvp0", [DEPTH, NBP, 128, 2, 512]), dout("kvp1", [DEPTH, NBP, 512, 2, 512]),
           dout("kvp2", [DEPTH, NBP, 2048, 2, 512])]
    kvs = [dout(f"kvs{g}", [DEPTH, TS, 2, 512]) for g in range(3)]
    gmv = dout("gmv", [DEPTH, TS, D])

    WIN = dint("WIN", [DEPTH, D, INW], BF16)
    WGM = dint("WGM", [DEPTH, 8, 128, 8, 128], BF16)
    WATT = dint("WATT", [DEPTH, 8, 128, 4, 128], BF16)
    WO = dint("WO", [DEPTH, 8, 128, 8, 128], BF16)
    XS = dint("XS", [8, 128, S], F32)
    KH = dint("KH", [12, 128, S], BF16)
    VH = dint("VH", [S, 3, 4, 192], BF16)
    WMT = dint("WMT", [DEPTH, 8, 128, 128], BF16)

    es = ExitStack()
    with es:
        def sb(name, shape, dt=F32):
            return es.enter_context(nc.sbuf_tensor(name, list(shape), dt))

        sc = Sched(nc, es)
        PS = [es.enter_context(nc.psum_tensor(f"ps{i}", [128, 512], F32)) for i in range(8)]
        PSB = [Buf() for _ in range(8)]

        ident_f = sb("ident_f", [128, 128]); ident_b = sb("ident_b", [128, 128], BF16)
        MB = sb("MB", [128, 512], BF16); MG = sb("MG", [128, 4, 512], BF16)
        TRI = sb("TRI", [128, 128])
        wmT = sb("wmT", [128, DEPTH, 8, 128], BF16)
        SV = sb("SV", [128, 128]); SVT = sb("SVT", [128, 72])
        modT = sb("modT", [128, DEPTH, 24, NC6])
        Am = sb("Am", [128, DEPTH, 8, NC6]); G4 = sb("G4", [128, DEPTH, 8, NC6])
        cT = sb("cT", [128, 8, NC6]); scT = sb("scT", [128, 8, NC6], BF16)
        LG = sb("LG", [128, D]); LB = sb("LB", [128, D]); BSR = sb("BSR", [128, D])
        xT = sb("xT", [128, 8, 512]); hT = sb("hT", [128, 8, 512], BF16)
        rstd = sb("rstd", [128, 512])
        NWB = 4
        WB = [sb(f"WB{i}", [128, 8, 512], BF16) for i in range(NWB)]
        NWS = 5
        WBS = [sb(f"WBS{i}", [128, 8, 128], BF16) for i in range(NWS)]
        NT = 8
        T32 = [sb(f"T32_{i}", [128, 512]) for i in range(NT)]
        NTB = 5
        TB = [sb(f"TB_{i}", [128, 512], BF16) for i in range(NTB)]
        XL = [sb(f"XL{i}", [128, D]) for i in range(2)]
        QT = sb("QT", [128, 12, 512], BF16); KT = sb("KT", [128, 12, 512], BF16)
        MGT = QT[:, 0:8, :]
        BIG = KT[:, 0:8, :]
        MBf = XL[0][:, 0:512]
        call_t = XL[1][0:NC6, :]
        KHt = [sb("KH0", [128, 640], BF16), sb("KH1", [128, 1024], BF16), sb("KH2", [128, 2048], BF16)]
        VAs = [sb(f"VA{i}", [128, 29, 192], BF16) for i in range(2)]
        vn = sb("vn", [128, 4, D], BF16)
        VST = [sb(f"VST{i}", [128, 4, 192], BF16) for i in range(2)]
        ybT = sb("ybT", [128, 4, 512], BF16)
        CC = sb("CC", [128, 4, 64]); SSn = sb("SSn", [128, 4, 64])
        st6 = sb("st6", [128, 4, 2, 6]); mv = sb("mv", [128, 4, 2]); lnr = sb("lnr", [128, 4])
        xsT = sb("xsT", [128, 8, TS]); hsT = sb("hsT", [128, 8, TS], BF16)
        QTs = sb("QTs", [128, 12, TS], BF16); KTs = sb("KTs", [128, 12, TS], BF16)
        CT = XL
        KTc = sb("KTc", [128, 4, 128], BF16); VAc = sb("VAc", [128, 4, 192], BF16)
        KTcB = sb("KTcB", [128, 4, 128], BF16); VAcB = sb("VAcB", [128, 4, 192], BF16); EsB = sb("EsB", [128, 64], BF16)
        MS = sb("MS", [128, 13, T], BF16); MN = sb("MN", [T, 3, T], BF16)
        MSf = sb("MSf", [128, 13, T]); MNf = sb("MNf", [T, 3, T])
        CCs = sb("CCs", [32, 64]); SSs = sb("SSs", [32, 64])
        WS = sb("WS", [32, 8, 32], BF16)
        ybs = sb("ybs", [128, 4, TS], BF16)
        Es = sb("Es", [128, 64], BF16); Esn = sb("Esn", [T, 64], BF16)

        B = {}

        ALIAS = {"MGT": "QT", "BIG": "KT", "MBf": "XL0", "call": "XL1", "CT0": "XL0", "CT1": "XL1"}

        def bf(name):
            name = ALIAS.get(name, name)
            if name not in B:
                B[name] = Buf()
            return B[name]

        t32b = [Buf() for _ in range(NT)]
        tbb = [Buf() for _ in range(NTB)]
        t32i = [0]
        tbi = [0]

        def t32():
            i = t32i[0] % NT
            t32i[0] += 1
            return T32[i], t32b[i]

        def tb16():
            i = tbi[0] % NTB
            tbi[0] += 1
            return TB[i], tbb[i]

        wbb = [Buf() for _ in range(NWB)]
        wbi = [0]
        wsb = [Buf() for _ in range(NWS)]
        wsi = [0]

        def pe(fn, r, w):
            return sc.op("pe", fn, r, w)

        def act(fn, r, w):
            return sc.op("act", fn, r, w)

        def dve(fn, r, w):
            return sc.op("dve", fn, r, w)

        def pool(fn, r, w):
            return sc.op("pool", fn, r, w)

        def dma_in(out, in_, r, w):
            return sc.op("sp", lambda e: e.dma_start(out=out, in_=in_), r, w, dma=True)

        def dma_g(out, in_, r, w):
            return sc.op("pool", lambda e: e.dma_start(out=out, in_=in_), r, w, dma=True)

        def bcast_rows(ap1d, n):
            return bass.AP(tensor=ap1d.tensor, offset=ap1d.offset, ap=[[0, 128], [1, n]])

        dma_in(ident_f[:], c_id[:, :], [], [bf("ident_f")])
        pool(lambda e: e.tensor_copy(out=ident_b[:], in_=ident_f[:]), [bf("ident_f")], [bf("ident_b")])
        dma_in(MBf, c_mb[:, :], [], [bf("MBf")])
        pool(lambda e: e.tensor_copy(out=MB[:], in_=MBf), [bf("MBf")], [bf("MB")])
        for tq in range(4):
            dma_in(MBf, c_mg[tq], [], [bf("MBf")])
            pool(lambda e, tq=tq: e.tensor_copy(out=MG[:, tq, :], in_=MBf), [bf("MBf")], [bf("MG")])
        dma_in(TRI[:], c_tri[:, :], [], [bf("TRI")])
        dma_in(MSf[:], c_ms.rearrange("k p t -> p k t"), [], [bf("MSf")])
        pool(lambda e: e.tensor_copy(out=MS[:], in_=MSf[:]), [bf("MSf")], [bf("MS")])
        dma_in(MNf[:], c_mn.rearrange("k p t -> p k t"), [], [bf("MNf")])
        pool(lambda e: e.tensor_copy(out=MN[:], in_=MNf[:]), [bf("MNf")], [bf("MN")])
        dma_in(CCs[:], c_ccs[:, :], [], [bf("CCs")])
        dma_in(SSs[:], c_sss[:, :], [], [bf("SSs")])
        for i in range(2):
            pool(lambda e, i=i: e.memset(VAs[i][:], 1.0), [], [bf(f"VA{i}")])
            pool(lambda e, i=i: e.memset(VST[i][:], 1.0), [], [bf(f"VST{i}")])
        vsti = [0]
        pool(lambda e: e.memset(VAc[:], 1.0), [], [bf("VAc0")])
        pool(lambda e: e.memset(VAcB[:], 1.0), [], [bf("VAc1")])
        VAnB1 = sb("VAnB", [T, 3, 4, 192], BF16)
        VAnB = [VAnB1 for _ in range(max(NBS, 1))]
        pool(lambda e: e.memset(VAnB1[:], 1.0), [], [bf("VAnB")])
        pool(lambda e: e.memset(WS[:], 0.0), [], [bf("WS")])
        pool(lambda e: e.memset(KHt[2][:], 0.0), [], [bf("KH2")])

        wconv = bf("wconv")
        for l in range(DEPTH):
            for kc in range(8):
                dma_g(WIN[l, kc * 128:(kc + 1) * 128, :], w_in[l, kc * 128:(kc + 1) * 128, :], [], [])
                dma_g(bass.AP(tensor=WGM.tensor, offset=WGM[l, 0, 0, kc, 0].offset, ap=[[8 * 128, 128], [128 * 8 * 128, 8], [1, 128]]),
                      w_gm[l, kc * 128:(kc + 1) * 128, :].rearrange("p (o j) -> p o j", j=128), [], [])
                dma_g(bass.AP(tensor=WO.tensor, offset=WO[l, 0, 0, kc, 0].offset, ap=[[8 * 128, 128], [128 * 8 * 128, 8], [1, 128]]),
                      w_o[l, kc * 128:(kc + 1) * 128, :].rearrange("p (o j) -> p o j", j=128), [], [])
            for kc in range(4):
                dma_g(bass.AP(tensor=WATT.tensor, offset=WATT[l, 0, 0, kc, 0].offset, ap=[[4 * 128, 128], [128 * 4 * 128, 8], [1, 128]]),
                      w_att[l, kc * 128:(kc + 1) * 128, :].rearrange("p (o j) -> p o j", j=128), [], [])
        conv_deps = [("d", k, sc.dtgt[k]) for k in range(2 * NDS) if sc.dtgt[k] > 0]
        wts = sc._waits("pool", conv_deps)

        for s_, v_ in wts:
            nc.gpsimd.wait_ge(s_, v_)
        pool(lambda e: e.memset(Es[:], 0.0), [], [wconv])

        dma_in(SV[0:48, :], b_ada.rearrange("l (j p) -> (l j) p", p=128), [], [bf("SV")])
        dma_in(SV[48:64, :], norm_g.rearrange("l (j p) -> (l j) p", p=128), [], [bf("SV")])
        dma_in(SV[64:72, :], final_g.rearrange("(j p) -> j p", p=128), [], [bf("SV")])
        pe(lambda e: e.transpose(out=PS[0][:, 0:72], in_=SV[0:72, :], identity=ident_f[0:72, 0:72]),
           [bf("SV"), bf("ident_f")], [PSB[0]])
        dve(lambda e: e.tensor_copy(out=SVT[:], in_=PS[0][:, 0:72]), [PSB[0]], [bf("SVT")])

        for l in range(DEPTH):
            for g in range(8):
                tt_, tt_b = t32()
                dma_in(tt_[:, 0:128], gm_ws[l, g], [], [tt_b])
                dve(lambda e, tt_=tt_: e.tensor_tensor(out=tt_[:, 128:256], in0=tt_[:, 0:128], in1=TRI[:], op=ALU.mult),
                    [tt_b, bf("TRI")], [tt_b])
                pe(lambda e, tt_=tt_: e.transpose(out=PS[1][:, 0:128], in_=tt_[:, 128:256], identity=ident_f[:]),
                   [tt_b, bf("ident_f")], [PSB[1]])
                dve(lambda e, l=l, g=g: e.tensor_copy(out=wmT[:, l, g, :], in_=PS[1][:, 0:128]), [PSB[1]], [bf("wmT")])
        dma_g(WMT.rearrange("l g s t -> s l g t"), wmT[:], [bf("wmT")], [bf("WMT")])

        dma_in(call_t, c_all[:, :], [], [bf("call")])
        for kc in range(8):
            pe(lambda e, kc=kc: e.transpose(out=PS[0][:, kc * 8:kc * 8 + NC6], in_=call_t[:, kc * 128:(kc + 1) * 128],
                                            identity=ident_f[0:NC6, 0:NC6]), [bf("call"), bf("ident_f")], [PSB[0]])
        ps0v = PS[0][:, 0:64].rearrange("p (k i) -> p k i", i=8)[:, :, 0:NC6]
        dve(lambda e: e.tensor_copy(out=cT[:], in_=ps0v), [PSB[0]], [bf("cT")])
        tt_, tt_b = t32()
        ttv = tt_[:, 0:8 * NC6].rearrange("p (k i) -> p k i", i=NC6)
        act(lambda e: e.activation(out=ttv, in_=cT[:], func=AF.Tanh, scale=0.5), [bf("cT")], [tt_b])
        dve(lambda e: e.scalar_tensor_tensor(out=ttv, in0=ttv, scalar=1.0, in1=cT[:], op0=ALU.add, op1=ALU.mult),
            [tt_b, bf("cT")], [tt_b])
        dve(lambda e: e.tensor_scalar(out=scT[:], in0=ttv, scalar1=0.5, scalar2=None, op0=ALU.mult), [tt_b], [bf("scT")])
        for l in range(DEPTH):
            for cb in range(6):
                i = wbi[0] % NWB
                wbi[0] += 1
                src = bass.AP(tensor=w_ada.tensor, offset=w_ada[l, 0, cb * 512].offset,
                              ap=[[3 * D, 128], [128 * 3 * D, 8], [1, 512]])
                dma_g(WB[i][:], src, [], [wbb[i]])
                for ch in range(4):
                    j = cb * 4 + ch

                    def mm(e, i=i, ch=ch):
                        ins = None
                        for kc in range(8):
                            ins = e.matmul(PS[2][:, 0:NC6], lhsT=WB[i][:, kc, ch * 128:(ch + 1) * 128], rhs=scT[:, kc, :],
                                           start=(kc == 0), stop=(kc == 7))
                        return ins
                    pe(mm, [wbb[i], bf("scT")], [PSB[2]])
                    dve(lambda e, l=l, j=j: e.tensor_scalar(out=modT[:, l, j, :], in0=PS[2][:, 0:NC6],
                                                            scalar1=SVT[:, l * 24 + j:l * 24 + j + 1], scalar2=None, op0=ALU.add),
                        [PSB[2], bf("SVT")], [bf("modT")])
            for kc in range(8):
                dve(lambda e, l=l, kc=kc: e.tensor_scalar(out=Am[:, l, kc, :], in0=modT[:, l, 8 + kc, :], scalar1=1.0,
                                                          scalar2=SVT[:, 48 + l * 8 + kc:48 + l * 8 + kc + 1],
                                                          op0=ALU.add, op1=ALU.mult), [bf("modT"), bf("SVT")], [bf("Am")])
                dve(lambda e, l=l, kc=kc: e.tensor_scalar(out=G4[:, l, kc, :], in0=modT[:, l, 16 + kc, :], scalar1=0.25,
                                                          scalar2=None, op0=ALU.mult), [bf("modT")], [bf("G4")])

        def wblock(src3, l, k0, nk, col0, ncols=512):
            if ncols == 128:
                i = wsi[0] % NWS
                wsi[0] += 1
                dma_in(WBS[i][:, 0:nk, :], src3[l, col0 // 128], [wconv], [wsb[i]])
                return WBS[i], wsb[i]
            ncol_total = src3.shape[2]
            src = bass.AP(tensor=src3.tensor, offset=src3[l, k0 * 128, col0].offset,
                          ap=[[ncol_total, 128], [128 * ncol_total, nk], [1, ncols]])
            i = wbi[0] % NWB
            wbi[0] += 1
            dma_in(WB[i][:, 0:nk, 0:ncols], src, [wconv], [wbb[i]])
            return WB[i], wbb[i]

        def rms_rstd(src, srcb, n, gsel):
            act(lambda e: e.activation(out=BIG[:, :, 0:n], in_=src, func=AF.Square), [srcb], [bf("BIG")])

            def mm(e):
                ins = None
                for kc in range(8):
                    ins = e.matmul(PS[7][:, 0:n], lhsT=gsel, rhs=BIG[:, kc, 0:n], start=(kc == 0), stop=(kc == 7))
                return ins
            pe(mm, [bf("BIG"), bf("ones")], [PSB[7]])
            dve(lambda e: e.tensor_scalar(out=rstd[:, 0:n], in0=PS[7][:, 0:n], scalar1=1.0 / D, scalar2=1e-6,
                                          op0=ALU.mult, op1=ALU.add), [PSB[7]], [bf("rstd")])
            act(lambda e: e.activation(out=rstd[:, 0:n], in_=rstd[:, 0:n], func=AF.Ln), [bf("rstd")], [bf("rstd")])
            act(lambda e: e.activation(out=rstd[:, 0:n], in_=rstd[:, 0:n], func=AF.Exp, scale=-0.5), [bf("rstd")], [bf("rstd")])

        ones_b = sb("ones_b", [128, 128], BF16)
        pool(lambda e: e.memset(ones_b[:], 1.0), [], [bf("ones")])
        for tb_ in range(S // 128):
            dma_g(bass.AP(tensor=VH.tensor, offset=VH[tb_ * 128, 0, 0, 64].offset, ap=[[2304, 128], [192, 12], [1, 64]]),
                  bass.AP(tensor=ones_b[:].tensor, offset=ones_b[:].offset, ap=[list(ones_b[:].ap[0]), [0, 12], [1, 64]]),
                  [bf("ones")], [bf("VH")])

        def load_layer_vecs(l):
            dma_in(LG[:], bcast_rows(gm_ln_g[l], D), [], [bf("LG")])
            dma_in(LB[:], bcast_rows(gm_ln_b[l], D), [], [bf("LB")])
            dma_in(BSR[:], bcast_rows(gm_bs[l].rearrange("g t -> (g t)"), D), [], [bf("BSR")])

        def make_h(xsrc, xb_, n, l, cidx_of_col):
            dst = hT if n == 512 else hsT
            dstb = bf("hT") if n == 512 else bf("hsT")
            for kc in range(8):
                tt_, tt_b = t32()
                dve(lambda e, kc=kc, tt_=tt_: e.tensor_tensor(out=tt_[:, 0:n], in0=xsrc[:, kc, 0:n], in1=rstd[:, 0:n], op=ALU.mult),
                    [xb_, bf("rstd")], [tt_b])
                for (c0, c1, ci) in cidx_of_col:
                    act(lambda e, kc=kc, tt_=tt_, c0=c0, c1=c1, ci=ci: e.activation(
                        out=dst[:, kc, c0:c1], in_=tt_[:, c0:c1], func=AF.Identity, scale=Am[:, l, kc, ci:ci + 1],
                        bias=modT[:, l, kc, ci:ci + 1]),
                        [tt_b, bf("Am"), bf("modT")], [dstb])

        def rope_block(ps_ap, cc_ap, ss_ap, npart, want_f32):
            m1, m1b = t32()
            m2, m2b = t32()
            p3 = ps_ap.rearrange("p (h d) -> p h d", d=64)
            m13 = m1[0:npart, :].rearrange("p (h d) -> p h d", d=64)
            m23 = m2[0:npart, :].rearrange("p (h d) -> p h d", d=64)
            ccb = bass.AP(tensor=cc_ap.tensor, offset=cc_ap.offset, ap=[list(cc_ap.ap[0]), [0, 8], [1, 64]])
            ss1 = bass.AP(tensor=ss_ap.tensor, offset=ss_ap.offset, ap=[list(ss_ap.ap[0]), [0, 8], [1, 32]])
            ss2 = bass.AP(tensor=ss_ap.tensor, offset=ss_ap.offset + 32, ap=[list(ss_ap.ap[0]), [0, 8], [1, 32]])
            return m1, m1b, m2, m2b, p3, m13, m23, ccb, ss1, ss2

        cur_l = [-1]

        def prompt_group(b, l, tg):
            T0 = tg * 512
            ci = b
            if cur_l[0] != l:
                load_layer_vecs(l)
                cur_l[0] = l
            xTb = bf("xT")
            if l == 0:
                for ti in range(4):
                    xl = XL[ti % 2]
                    xlb = bf(f"XL{ti % 2}")
                    dma_in(xl[:], x_p[b, T0 + ti * 128:T0 + (ti + 1) * 128, :], [], [xlb])
                    for hf in range(2):
                        pi = 4 + hf

                        def tr(e, xl=xl, hf=hf, pi=pi):
                            ins = None
                            for k in range(4):
                                kc = hf * 4 + k
                                ins = e.transpose(out=PS[pi][:, k * 128:(k + 1) * 128], in_=xl[:, kc * 128:(kc + 1) * 128],
                                                  identity=ident_f[:])
                            return ins
                        pe(tr, [xlb, bf("ident_f")], [PSB[pi]])
                        act(lambda e, hf=hf, pi=pi, ti=ti: e.activation(
                            out=xT[:, hf * 4:hf * 4 + 4, ti * 128:(ti + 1) * 128],
                            in_=PS[pi][:, :].rearrange("p (k t) -> p k t", t=128), func=AF.Copy), [PSB[pi]], [xTb])
            else:
                dma_in(xT[:], XS[:, :, T0:T0 + 512].rearrange("k p t -> p k t"), [bf("XS")], [xTb])
            if DBG_STOP == 'X':
                return
            rms_rstd(xT[:], xTb, 512, ones_b[:])
            make_h(xT, xTb, 512, l, [(0, 512, ci)])
            dma_in(CC[:], c_cc[T0:T0 + 512, :].rearrange("(i p) d -> p i d", p=128), [], [bf("CC")])
            dma_in(SSn[:], c_ss[T0:T0 + 512, :].rearrange("(i p) d -> p i d", p=128), [], [bf("SS")])
            if DBG_STOP == 'H':
                return
            qk_pending = []
            for cb in range(6):
                isk = cb >= 3
                g = cb % 3
                W, Wb = wblock(WIN, l, 0, 8, OQ + cb * 512)
                for ti in range(4):
                    pi = ti % 4

                    def mm(e, W=W, ti=ti, pi=pi):
                        ins = None
                        for kc in range(8):
                            ins = e.matmul(PS[pi][:, :], lhsT=hT[:, kc, ti * 128:(ti + 1) * 128], rhs=W[:, kc, :],
                                           start=(kc == 0), stop=(kc == 7))
                        return ins
                    pe(mm, [Wb, bf("hT")], [PSB[pi]])
                    m1, m1b, m2, m2b, p3, m13, m23, ccb, ss1, ss2 = rope_block(PS[pi][:, :], CC[:, ti, :], SSn[:, ti, :], 128, isk)
                    dve(lambda e, m13=m13, p3=p3, ccb=ccb: e.tensor_tensor(out=m13, in0=p3, in1=ccb, op=ALU.mult),
                        [PSB[pi], bf("CC")], [m1b])
                    dve(lambda e, m23=m23, p3=p3, ss1=ss1: e.tensor_tensor(out=m23[:, :, 0:32], in0=p3[:, :, 32:64], in1=ss1, op=ALU.mult),
                        [PSB[pi], bf("SS")], [m2b])
                    dve(lambda e, m23=m23, p3=p3, ss2=ss2: e.tensor_tensor(out=m23[:, :, 32:64], in0=p3[:, :, 0:32], in1=ss2, op=ALU.mult),
                        [PSB[pi], bf("SS")], [m2b])
                    ob, obb = tb16()
                    if isk:
                        pool(lambda e, m1=m1, m2=m2: e.tensor_tensor(out=m1[:], in0=m1[:], in1=m2[:], op=ALU.add), [m1b, m2b], [m1b])
                        act(lambda e, ob=ob, m1=m1: e.activation(out=ob[:], in_=m1[:], func=AF.Copy), [m1b], [obb])
                        keep = DILS[g][0]
                        t_lo = T0 + ti * 128
                        if t_lo >= S - keep:
                            r0 = t_lo - (S - keep)
                            dma_g(kvp[g][l, b, r0:r0 + 128, 0, :], m1[:], [m1b], [])
                    else:
                        pool(lambda e, m1=m1, m2=m2, ob=ob: e.tensor_tensor(out=ob[:], in0=m1[:], in1=m2[:], op=ALU.add), [m1b, m2b], [obb])
                    pj = 4 + (ti % 2)

                    def tr_and_evac(ob=ob, obb=obb, pj=pj, isk=isk, g=g, ti=ti):
                        def tr(e):
                            ins = None
                            pv = PS[pj][:, :].bitcast(BF16)
                            for c in range(4):
                                ins = e.transpose(out=pv[:, c * 128:(c + 1) * 128], in_=ob[:, c * 128:(c + 1) * 128], identity=ident_b[:])
                            return ins
                        pe(tr, [obb, bf("ident_b")], [PSB[pj]])
                        dst = KT if isk else QT
                        dstb = bf("KT") if isk else bf("QT")
                        act(lambda e: e.activation(
                            out=dst[:, g * 4:g * 4 + 4, ti * 128:(ti + 1) * 128],
                            in_=PS[pj][:, :].bitcast(BF16)[:, 0:512].rearrange("p (c t) -> p c t", t=128), func=AF.Copy),
                            [PSB[pj]], [dstb])
                    qk_pending.append(tr_and_evac)
                    while len(qk_pending) > 2:
                        qk_pending.pop(0)()
            while qk_pending:
                qk_pending.pop(0)()
            dma_g(KH[:, :, T0:T0 + 512].rearrange("c p t -> p c t"), KT[:], [bf("KT")], [bf("KH")])
            if DBG_STOP == 'QK':
                return
            for g in range(3):
                W, Wb = wblock(WIN, l, 0, 8, OVAL + g * 512)
                for ti in range(4):
                    pi = ti % 4

                    def mm(e, W=W, ti=ti, pi=pi):
                        ins = None
                        for kc in range(8):
                            ins = e.matmul(PS[pi][:, :], lhsT=hT[:, kc, ti * 128:(ti + 1) * 128], rhs=W[:, kc, :],
                                           start=(kc == 0), stop=(kc == 7))
                        return ins
                    pe(mm, [Wb, bf("hT")], [PSB[pi]])
                    vf, vfb = t32()
                    act(lambda e, vf=vf, pi=pi: e.activation(out=vf[:], in_=PS[pi][:, :], func=AF.Copy), [PSB[pi]], [vfb])
                    t_lo = T0 + ti * 128
                    keep = DILS[g][0]
                    if t_lo >= S - keep:
                        r0 = t_lo - (S - keep)
                        dma_g(kvp[g][l, b, r0:r0 + 128, 1, :], vf[:], [vfb], [])
                    vi = vsti[0] % 2
                    vsti[0] += 1
                    vst = VST[vi]
                    vdst = bass.AP(tensor=vst[:].tensor, offset=vst[:].offset, ap=[list(vst[:].ap[0]), [192, 4], [128, 2], [1, 64]])
                    dve(lambda e, vdst=vdst, vf=vf: e.tensor_copy(out=vdst, in_=vf[:, :].rearrange("p (c h d) -> p c h d", h=2, d=64)),
                        [vfb], [bf(f"VST{vi}")])
                    dma_g(VH[t_lo:t_lo + 128, g, :, :], vst[:], [bf(f"VST{vi}")], [bf("VH")])
            if DBG_STOP == 'V':
                return
            lo0 = max(0, T0 - 128)
            lo1 = max(0, T0 - 512)

            def att_load_k(c):
                dma_in(KHt[0][:, 0:T0 + 512 - lo0], KH[0 * 4 + c, :, lo0:T0 + 512], [bf("KH")], [bf("KH0")])
                dma_in(KHt[1][:, 0:T0 + 512 - lo1], KH[1 * 4 + c, :, lo1:T0 + 512], [bf("KH")], [bf("KH1")])
                dma_in(KHt[2][:, 0:T0 + 512], KH[2 * 4 + c, :, 0:T0 + 512], [bf("KH")], [bf("KH2")])

            def att_load_v(c):
                VA = VAs[c % 2]
                vab = bf(f"VA{c % 2}")
                blk0 = max(0, tg * 4 - 1)
                nb = tg * 4 + 4 - blk0
                kb0 = blk0 - (tg * 4 - 1)
                src = bass.AP(tensor=VH.tensor, offset=VH[blk0 * 128, 0, c, 0].offset, ap=[[2304, 128], [128 * 2304, nb], [1, 192]])
                dma_in(VA[:, kb0:kb0 + nb, :], src, [bf("VH")], [vab])
                for bi in ((0, 1) if tg > 0 else (1,)):
                    src = bass.AP(tensor=VH.tensor, offset=VH[(tg - 1 + bi) * 512, 1, c, 0].offset, ap=[[4 * 2304, 128], [2304, 4], [1, 192]])
                    dma_in(VA[:, 5 + 4 * bi:9 + 4 * bi, :], src, [bf("VH")], [vab])
                np_ = 32 * (tg + 1)
                src = bass.AP(tensor=VH.tensor, offset=VH[0, 2, c, 0].offset, ap=[[16 * 2304, np_], [2304, 16], [1, 192]])
                dma_in(VA[0:np_, 13:29, :], src, [bf("VH")], [vab])

            sbanks = [6, 7, 2, 3]
            ucnt = [0]
            pending = []

            def flush(keep):
                while len(pending) > keep:
                    pending.pop(0)()

            def att_pair(c):
                VA = VAs[c % 2]
                vab = bf(f"VA{c % 2}")
                for hh in range(2):
                    rows = slice(hh * 64, hh * 64 + 64)
                    acc = PS[4 + hh]
                    accb = PSB[4 + hh]
                    vcol = slice(0, 128) if hh == 0 else slice(64, 192)
                    first = [True]

                    def pv_mm(e, slot, esrc, outcols, first=first, VA=VA, vcol=vcol):
                        ins = e.matmul(outcols, lhsT=VA[:, slot, vcol], rhs=esrc, start=first[0], stop=False, skip_group_check=True)
                        first[0] = False
                        return ins
                    for g in range(2):
                        for qp in range(2):
                            pi = sbanks[ucnt[0] % 4]
                            ucnt[0] += 1
                            units = []
                            for u in range(2):
                                qi = qp * 2 + u
                                if g == 0:
                                    qcols = QT[rows, 0 * 4 + c, qi * 128:(qi + 1) * 128]
                                    has_prev = (tg * 4 + qi) > 0
                                    kprev = KHt[0][rows, (T0 - lo0) + (qi - 1) * 128:(T0 - lo0) + qi * 128] if has_prev else None
                                    kcur = KHt[0][rows, (T0 - lo0) + qi * 128:(T0 - lo0) + (qi + 1) * 128]
                                    sprev, scur = qi, qi + 1
                                    ocols = acc[:, qi * 128:(qi + 1) * 128]
                                else:
                                    r = qi
                                    qcols = QT[rows, 1 * 4 + c, r:512:4]
                                    has_prev = tg > 0
                                    kprev = KHt[1][rows, (T0 - lo1) - 512 + r:(T0 - lo1):4] if has_prev else None
                                    kcur = KHt[1][rows, (T0 - lo1) + r:(T0 - lo1) + 512:4]
                                    sprev, scur = 5 + r, 9 + r
                                    ocols = acc[:, r:512:4]
                                units.append((qcols, has_prev, kprev, kcur, sprev, scur, ocols))

                            def smm(e, units=units, pi=pi):
                                ins = None
                                for u, (qcols, has_prev, kprev, kcur, sprev, scur, ocols) in enumerate(units):
                                    if has_prev:
                                        ins = e.matmul(PS[pi][:, u * 256:u * 256 + 128], lhsT=kprev, rhs=qcols, start=True, stop=True)
                                    ins = e.matmul(PS[pi][:, u * 256 + 128:u * 256 + 256], lhsT=kcur, rhs=qcols, start=True, stop=True)
                                return ins
                            pe(smm, [bf("QT"), bf(f"KH{g}")], [PSB[pi]])
                            c0 = 0 if units[0][1] else 128
                            E, Eb = tb16()
                            act(lambda e, E=E, pi=pi, c0=c0: e.activation(out=E[:, c0:512], in_=PS[pi][:, c0:512], func=AF.Exp, scale=0.125),
                                [PSB[pi]], [Eb])
                            dve(lambda e, E=E, c0=c0: e.tensor_tensor(out=E[:, c0:512], in0=E[:, c0:512], in1=MB[:, c0:512], op=ALU.mult),
                                [Eb, bf("MB")], [Eb])

                            def pmm(e, units=units, E=E, pv_mm=pv_mm):
                                ins = None
                                for u, (qcols, has_prev, kprev, kcur, sprev, scur, ocols) in enumerate(units):
                                    if has_prev:
                                        ins = pv_mm(e, sprev, E[:, u * 256:u * 256 + 128], ocols)
                                    ins = pv_mm(e, scur, E[:, u * 256 + 128:u * 256 + 256], ocols)
                                return ins
                            pending.append(lambda pmm=pmm, Eb=Eb, vab=vab, accb=accb: pe(pmm, [Eb, vab], [accb]))
                            flush(1)
                    pi = sbanks[ucnt[0] % 4]
                    ucnt[0] += 1

                    def smm2(e, pi=pi):
                        ins = None
                        for r in range(16):
                            ins = e.matmul(PS[pi][:, r * 32:(r + 1) * 32], lhsT=KHt[2][rows, r:2048:16],
                                           rhs=QT[rows, 2 * 4 + c, r:512:16], start=True, stop=True)
                        return ins
                    pe(smm2, [bf("QT"), bf("KH2")], [PSB[pi]])
                    E, Eb = tb16()
                    act(lambda e, E=E, pi=pi: e.activation(out=E[:], in_=PS[pi][:, :], func=AF.Exp, scale=0.125), [PSB[pi]], [Eb])
                    dve(lambda e, E=E: e.tensor_tensor(out=E[:], in0=E[:], in1=MG[:, tg, :], op=ALU.mult), [Eb, bf("MG")], [Eb])

                    def pmm2(e, E=E, pv_mm=pv_mm, acc=acc):
                        ins = None
                        for r in range(16):
                            ins = pv_mm(e, 13 + r, E[:, r * 32:(r + 1) * 32], acc[:, r:512:16])
                        return ins
                    pending.append(lambda pmm2=pmm2, Eb=Eb, vab=vab, accb=accb: pe(pmm2, [Eb, vab], [accb]))
                    urow = rows
                    zrow = slice(64, 128) if hh == 0 else slice(0, 64)

                    def norm(acc=acc, accb=accb, urow=urow, zrow=zrow, c=c):
                        rz, rzb = t32()
                        act(lambda e: e.activation(out=rz[urow, :], in_=acc[zrow, :], func=AF.Ln), [accb], [rzb])
                        act(lambda e: e.activation(out=rz[urow, :], in_=rz[urow, :], func=AF.Exp, scale=-1.0), [rzb], [rzb])
                        dve(lambda e: e.tensor_tensor(out=ybT[urow, c, :], in0=acc[urow, :], in1=rz[urow, :], op=ALU.mult),
                            [accb, rzb], [bf("ybT")])
                    pending.append(norm)

            branch_a1(l, 512, 4, hT, bf("hT"), None)
            att_load_v(0)
            att_load_k(0)
            branch_a2(l, 512, hT, bf("hT"), None)
            for c in range(4):
                if c + 1 < 4:
                    att_load_v(c + 1)
                att_pair(c)
                flush(0)
                if c + 1 < 4:
                    att_load_k(c + 1)
            if DBG_STOP == 'ATT':
                return
            store_oc = None
            if l == 0:
                store_oc = lambda oc: dma_g(XS[oc, :, T0:T0 + 512], xT[:, oc, :], [xTb], [bf("XS")])
            merge_out(l, 512, hT, bf("hT"), xT, xTb, ybT, bf("ybT"), [(0, 512, ci)], store_oc)
            if DBG_STOP == 'BM':
                return
            if l == 1:
                final_out(xT, xTb, 512, lambda ti: y_p[b, T0 + ti * 128:T0 + (ti + 1) * 128, :], 4, 128)

        def final_out(xsrc, xb_, n, dst_of_tile, ntiles, npart):
            rms_rstd(xsrc[:, :, 0:n], xb_, n, ones_b[:])
            for kc in range(8):
                dve(lambda e, kc=kc: e.scalar_tensor_tensor(out=xsrc[:, kc, 0:n], in0=xsrc[:, kc, 0:n], scalar=SVT[:, 64 + kc:65 + kc],
                                                            in1=rstd[:, 0:n], op0=ALU.mult, op1=ALU.mult), [xb_, bf("rstd"), bf("SVT")], [xb_])
            for ti in range(ntiles):
                xl = XL[ti % 2]
                xlb = bf(f"XL{ti % 2}")
                for hf in range(2):
                    pi = 4 + hf

                    def tr(e, hf=hf, pi=pi, ti=ti):
                        ins = None
                        for k in range(4):
                            kc = hf * 4 + k
                            ins = e.transpose(out=PS[pi][0:npart, k * 128:(k + 1) * 128], in_=xsrc[:, kc, ti * npart:(ti + 1) * npart],
                                              identity=ident_f[:])
                        return ins
                    pe(tr, [xb_, bf("ident_f")], [PSB[pi]])
                    act(lambda e, hf=hf, pi=pi, xl=xl: e.activation(out=xl[0:npart, hf * 512:(hf + 1) * 512], in_=PS[pi][0:npart, :], func=AF.Copy),
                        [PSB[pi]], [xlb])
                dma_g(dst_of_tile(ti), xl[0:npart, :], [xlb], [])

        def branch_a_and_merge(l, n, ntile, hsrc, hb_, xsrc, xb_, ybsrc, ybb, cidx_of_col, sample):
            branch_a1(l, n, ntile, hsrc, hb_, sample)
            branch_a2(l, n, hsrc, hb_, sample)
            merge_out(l, n, hsrc, hb_, xsrc, xb_, ybsrc, ybb, cidx_of_col)

        def branch_a1(l, n, ntile, hsrc, hb_, sample):
            npart = 128 if sample is None else n
            Ws = [wblock(WIN, l, 0, 8, OV + hf * 512) for hf in range(2)]
            for t0 in range(0, ntile, 2):
                tiles = [t for t in (t0, t0 + 1) if t < ntile]
                k = len(tiles)
                for ti in tiles:
                    gv = XL[ti % 2]
                    gvb = bf(f"XL{ti % 2}")
                    for hf in range(2):
                        W, Wb = Ws[hf]
                        pi = (ti % 2) * 2 + hf

                        def mm(e, W=W, ti=ti, pi=pi):
                            ins = None
                            for kc in range(8):
                                ins = e.matmul(PS[pi][0:npart, :], lhsT=hsrc[:, kc, ti * npart:(ti + 1) * npart], rhs=W[:, kc, :],
                                               start=(kc == 0), stop=(kc == 7))
                            return ins
                        pe(mm, [Wb, hb_], [PSB[pi]])
                        act(lambda e, gv=gv, hf=hf, pi=pi: e.activation(out=gv[0:npart, hf * 512:(hf + 1) * 512], in_=PS[pi][0:npart, :],
                                                                       func=AF.Gelu_apprx_tanh), [PSB[pi]], [gvb])
                        dve(lambda e, gv=gv, hf=hf, ti=ti: e.bn_stats(out=st6[0:npart, ti, hf, :], in_=gv[0:npart, hf * 512:(hf + 1) * 512]),
                            [gvb], [bf("st6")])
                    dve(lambda e, ti=ti: e.bn_aggr(out=mv[0:npart, ti, :], in_=st6[0:npart, ti, :, :]), [bf("st6")], [bf("mv")])
                dve(lambda e: e.tensor_scalar(out=lnr[0:npart, t0:t0 + k], in0=mv[0:npart, t0:t0 + k, 1], scalar1=1e-5, scalar2=None, op0=ALU.add),
                    [bf("mv")], [bf("lnr")])
                act(lambda e: e.activation(out=lnr[0:npart, t0:t0 + k], in_=lnr[0:npart, t0:t0 + k], func=AF.Ln), [bf("lnr")], [bf("lnr")])
                act(lambda e: e.activation(out=lnr[0:npart, t0:t0 + k], in_=lnr[0:npart, t0:t0 + k], func=AF.Exp, scale=-0.5),
                    [bf("lnr")], [bf("lnr")])
                for ti in tiles:
                    gv = XL[ti % 2]
                    gvb = bf(f"XL{ti % 2}")
                    dve(lambda e, gv=gv, ti=ti: e.tensor_scalar(out=gv[0:npart, :], in0=gv[0:npart, :], scalar1=mv[0:npart, ti, 0:1],
                                                                scalar2=lnr[0:npart, ti:ti + 1], op0=ALU.subtract, op1=ALU.mult),
                        [gvb, bf("mv"), bf("lnr")], [gvb])
                    pool(lambda e, gv=gv: e.tensor_tensor(out=gv[0:npart, :], in0=gv[0:npart, :], in1=LG[0:npart, :], op=ALU.mult), [gvb, bf("LG")], [gvb])
                    if sample is None:
                        pool(lambda e, gv=gv, ti=ti: e.tensor_tensor(out=vn[:, ti, :], in0=gv[:, :], in1=LB[:, :], op=ALU.add), [gvb, bf("LB")], [bf("vn")])
                    else:
                        pool(lambda e, gv=gv: e.tensor_tensor(out=gv[0:npart, :], in0=gv[0:npart, :], in1=LB[0:npart, :], op=ALU.add), [gvb, bf("LB")], [gvb])
                        dma_g(gmv[l, :, :], gv[0:npart, :], [gvb], [])
                        act(lambda e, gv=gv: e.activation(out=vn[0:npart, 0, :], in_=gv[0:npart, :], func=AF.Copy), [gvb], [bf("vn")])

        def branch_a2(l, n, hsrc, hb_, sample):
            yab = bf("BIG")
            Wu = None
            for g in range(8):
                if g % 4 == 0:
                    Wu = wblock(WIN, l, 0, 8, OU + (g // 4) * 512)
                    Wz = wblock(WIN, l, 0, 8, OZA + (g // 4) * 512)
                ch = g % 4
                bu, bz, bs_ = (0, 1, 2) if g % 2 == 0 else (3, 6, 7)

                def mmf(e, Wt, pi, ch=ch):
                    ins = None
                    for kc in range(8):
                        ins = e.matmul(PS[pi][:, 0:n], lhsT=Wt[:, kc, ch * 128:(ch + 1) * 128], rhs=hsrc[:, kc, 0:n],
                                       start=(kc == 0), stop=(kc == 7))
                    return ins
                pe(lambda e, W=Wu[0]: mmf(e, W, bu), [Wu[1], hb_], [PSB[bu]])
                pe(lambda e, W=Wz[0]: mmf(e, W, bz), [Wz[1], hb_], [PSB[bz]])

                def spm(e, g=g):
                    ins = None
                    if sample is None:
                        for ti in range(4):
                            ins = e.matmul(PS[bs_][:, ti * 128:(ti + 1) * 128], lhsT=vn[:, ti, g * 128:(g + 1) * 128], rhs=wmT[:, l, g, :],
                                           start=True, stop=True)
                    else:
                        ins = e.matmul(PS[bs_][:, 0:n], lhsT=vn[0:n, 0, g * 128:(g + 1) * 128], rhs=WS[0:n, g, 0:n], start=True, stop=True)
                    return ins
                pe(spm, [bf("vn"), bf("wmT"), bf("WS")], [PSB[bs_]])
                gu, gub = t32()
                tz, tzb = t32()
                spb, spbb = t32()
                act(lambda e, gu=gu: e.activation(out=gu[:, 0:n], in_=PS[bu][:, 0:n], func=AF.Gelu_apprx_tanh), [PSB[bu]], [gub])
                act(lambda e, tz=tz: e.activation(out=tz[:, 0:n], in_=PS[bz][:, 0:n], func=AF.Tanh, scale=0.5), [PSB[bz]], [tzb])
                dve(lambda e, tz=tz: e.scalar_tensor_tensor(out=tz[:, 0:n], in0=tz[:, 0:n], scalar=1.0, in1=PS[bz][:, 0:n], op0=ALU.add, op1=ALU.mult),
                    [tzb, PSB[bz]], [tzb])
                if sample is None:
                    bsv = bass.AP(tensor=BSR[:].tensor, offset=BSR[0, g * 128].offset, ap=[list(BSR[:].ap[0]), [0, 4], [1, 128]])
                    dve(lambda e, spb=spb, bsv=bsv: e.tensor_tensor(out=spb[:, :].rearrange("p (a t) -> p a t", t=128),
                                                                    in0=PS[bs_][:, :].rearrange("p (a t) -> p a t", t=128), in1=bsv, op=ALU.add),
                        [PSB[bs_], bf("BSR")], [spbb])
                else:
                    bsv = bass.AP(tensor=BSR[:].tensor, offset=BSR[0, g * 128].offset, ap=[list(BSR[:].ap[0]), [0, n // T], [1, T]])
                    dve(lambda e, spb=spb, bsv=bsv: e.tensor_tensor(out=spb[:, 0:n].rearrange("p (a t) -> p a t", t=T),
                                                                    in0=PS[bs_][:, 0:n].rearrange("p (a t) -> p a t", t=T), in1=bsv, op=ALU.add),
                        [PSB[bs_], bf("BSR")], [spbb])
                dve(lambda e, gu=gu, spb=spb: e.tensor_tensor(out=gu[:, 0:n], in0=gu[:, 0:n], in1=spb[:, 0:n], op=ALU.mult), [gub, spbb], [gub])
                dve(lambda e, gu=gu, tz=tz, g=g: e.tensor_tensor(out=BIG[:, g, 0:n], in0=gu[:, 0:n], in1=tz[:, 0:n], op=ALU.mult), [gub, tzb], [yab])

        def merge_out(l, n, hsrc, hb_, xsrc, xb_, ybsrc, ybb, cidx_of_col, after_oc=None):
            yab = bf("BIG")
            Wzb = wblock(WIN, l, 0, 8, OZB)
            for c in range(4):
                bq = c % 2
                pe(lambda e, c=c, bq=bq: mmf_generic(e, Wzb[0], bq, c, hsrc, n, 8), [Wzb[1], hb_], [PSB[bq]])
                tz, tzb = t32()
                act(lambda e, tz=tz, bq=bq: e.activation(out=tz[:, 0:n], in_=PS[bq][:, 0:n], func=AF.Tanh, scale=0.5), [PSB[bq]], [tzb])
                dve(lambda e, tz=tz, bq=bq: e.scalar_tensor_tensor(out=tz[:, 0:n], in0=tz[:, 0:n], scalar=1.0, in1=PS[bq][:, 0:n], op0=ALU.add, op1=ALU.mult),
                    [tzb, PSB[bq]], [tzb])
                dve(lambda e, tz=tz, c=c: e.tensor_tensor(out=ybsrc[:, c, 0:n], in0=ybsrc[:, c, 0:n], in1=tz[:, 0:n], op=ALU.mult), [tzb, ybb], [ybb])
            for oc in range(8):
                if oc % 4 == 0:
                    Wga = wblock(WIN, l, 0, 8, OGA + (oc // 4) * 512)
                    Wgb = wblock(WIN, l, 0, 8, OGB + (oc // 4) * 512)
                ch = oc % 4
                Wg1 = wblock(WGM, l, 0, 8, oc * 128, 128)
                Wa1 = wblock(WATT, l, 0, 4, oc * 128, 128)
                b0 = 0 if oc % 2 == 0 else 4
                pe(lambda e, ch=ch, W=Wga[0], b0=b0: mmf_generic(e, W, b0, ch, hsrc, n, 8), [Wga[1], hb_], [PSB[b0]])
                pe(lambda e, ch=ch, W=Wgb[0], b0=b0: mmf_generic(e, W, b0 + 1, ch, hsrc, n, 8), [Wgb[1], hb_], [PSB[b0 + 1]])
                pe(lambda e, W=Wg1[0], b0=b0: mmf_generic(e, W, b0 + 2, 0, BIG, n, 8), [Wg1[1], yab], [PSB[b0 + 2]])
                pe(lambda e, W=Wa1[0], b0=b0: mmf_generic(e, W, b0 + 3, 0, ybsrc, n, 4), [Wa1[1], ybb], [PSB[b0 + 3]])
                ta, tab_ = t32()
                tb_, tbb_ = t32()
                act(lambda e, ta=ta, b0=b0: e.activation(out=ta[:, 0:n], in_=PS[b0][:, 0:n], func=AF.Tanh, scale=0.5), [PSB[b0]], [tab_])
                act(lambda e, tb_=tb_, b0=b0: e.activation(out=tb_[:, 0:n], in_=PS[b0 + 1][:, 0:n], func=AF.Tanh, scale=0.5), [PSB[b0 + 1]], [tbb_])
                dve(lambda e, ta=ta, b0=b0: e.scalar_tensor_tensor(out=ta[:, 0:n], in0=ta[:, 0:n], scalar=1.0, in1=PS[b0 + 2][:, 0:n], op0=ALU.add, op1=ALU.mult),
                    [tab_, PSB[b0 + 2]], [tab_])
                dve(lambda e, tb_=tb_, b0=b0: e.scalar_tensor_tensor(out=tb_[:, 0:n], in0=tb_[:, 0:n], scalar=1.0, in1=PS[b0 + 3][:, 0:n], op0=ALU.add, op1=ALU.mult),
                    [tbb_, PSB[b0 + 3]], [tbb_])
                dve(lambda e, ta=ta, tb_=tb_, oc=oc: e.tensor_tensor(out=MGT[:, oc, 0:n], in0=ta[:, 0:n], in1=tb_[:, 0:n], op=ALU.add),
                     [tab_, tbb_], [bf("MGT")])
            for oc in range(8):
                Wo1 = wblock(WO, l, 0, 8, oc * 128, 128)
                bo = oc % 4
                pe(lambda e, W=Wo1[0], bo=bo: mmf_generic(e, W, bo, 0, MGT, n, 8), [Wo1[1], bf("MGT")], [PSB[bo]])
                for (c0, c1, ci) in cidx_of_col:
                    dve(lambda e, oc=oc, c0=c0, c1=c1, ci=ci, bo=bo: e.scalar_tensor_tensor(
                        out=xsrc[:, oc, c0:c1], in0=PS[bo][:, c0:c1], scalar=G4[:, l, oc, ci:ci + 1], in1=xsrc[:, oc, c0:c1],
                        op0=ALU.mult, op1=ALU.add), [PSB[bo], bf("G4"), xb_], [xb_])
                if after_oc is not None:
                    after_oc(oc)

        def mmf_generic(e, Wt, pi, ch, src, n, nk):
            ins = None
            for kc in range(nk):
                ins = e.matmul(PS[pi][:, 0:n], lhsT=Wt[:, kc, ch * 128:(ch + 1) * 128], rhs=src[:, kc, 0:n],
                               start=(kc == 0), stop=(kc == nk - 1))
            return ins

        def sample_layer(l):
            n = TS
            xb_ = bf("xsT")
            if cur_l[0] != l:
                load_layer_vecs(l)
                cur_l[0] = l
            if l == 0:
                xl = XL[0]
                xlb = bf("XL0")
                dma_in(xl[0:n, :], x_s[:, :], [], [xlb])
                for hf in range(2):
                    pi = 4 + hf

                    def tr(e, hf=hf, pi=pi):
                        ins = None
                        for k in range(4):
                            kc = hf * 4 + k
                            ins = e.transpose(out=PS[pi][:, k * 128:k * 128 + n], in_=xl[0:n, kc * 128:(kc + 1) * 128], identity=ident_f[0:n, 0:n])
                        return ins
                    pe(tr, [xlb, bf("ident_f")], [PSB[pi]])
                    act(lambda e, hf=hf, pi=pi: e.activation(out=xsT[:, hf * 4:hf * 4 + 4, :],
                                                            in_=PS[pi][:, :].rearrange("p (k t) -> p k t", t=128)[:, :, 0:n], func=AF.Copy),
                        [PSB[pi]], [xb_])
            for bs in range(NBS):
                dma_in(WS[bs * T:(bs + 1) * T, :, bs * T:(bs + 1) * T], WMT[l, :, 0:T, 0:T].rearrange("g s t -> s g t"), [bf("WMT")], [bf("WS")])
            if DBG_SSTOP == 'SX':
                return
            cols = [(bs * T, (bs + 1) * T, NBP + bs) for bs in range(NBS)]
            rms_rstd(xsT[:], xb_, n, ones_b[:])
            make_h(xsT, xb_, n, l, cols)
            hb_ = bf("hsT")
            if DBG_SSTOP == 'SH':
                return
            for cb in range(6):
                isk = cb >= 3
                g = cb % 3
                W, Wb = wblock(WIN, l, 0, 8, OQ + cb * 512)

                def mm(e, W=W):
                    ins = None
                    for kc in range(8):
                        ins = e.matmul(PS[0][0:n, :], lhsT=hsT[:, kc, :], rhs=W[:, kc, :], start=(kc == 0), stop=(kc == 7))
                    return ins
                pe(mm, [Wb, hb_], [PSB[0]])
                m1, m1b, m2, m2b, p3, m13, m23, ccb, ss1, ss2 = rope_block(PS[0][0:n, :], CCs[0:n, :], SSs[0:n, :], n, isk)
                dve(lambda e, m13=m13, p3=p3, ccb=ccb: e.tensor_tensor(out=m13, in0=p3, in1=ccb, op=ALU.mult), [PSB[0], bf("CCs")], [m1b])
                dve(lambda e, m23=m23, p3=p3, ss1=ss1: e.tensor_tensor(out=m23[:, :, 0:32], in0=p3[:, :, 32:64], in1=ss1, op=ALU.mult), [PSB[0], bf("SSs")], [m2b])
                dve(lambda e, m23=m23, p3=p3, ss2=ss2: e.tensor_tensor(out=m23[:, :, 32:64], in0=p3[:, :, 0:32], in1=ss2, op=ALU.mult), [PSB[0], bf("SSs")], [m2b])
                ob, obb = tb16()
                dve(lambda e, m1=m1, m2=m2: e.tensor_tensor(out=m1[0:n, :], in0=m1[0:n, :], in1=m2[0:n, :], op=ALU.add), [m1b, m2b], [m1b])
                act(lambda e, ob=ob, m1=m1: e.activation(out=ob[0:n, :], in_=m1[0:n, :], func=AF.Copy), [m1b], [obb])
                if isk:
                    dma_g(kvs[g][l, :, 0, :], m1[0:n, :], [m1b], [])

                def tr(e, ob=ob):
                    ins = None
                    pv = PS[4][:, :].bitcast(BF16)
                    for c in range(4):
                        ins = e.transpose(out=pv[:, c * 128:c * 128 + n], in_=ob[0:n, c * 128:(c + 1) * 128], identity=ident_b[0:n, 0:n])
                    return ins
                pe(tr, [obb, bf("ident_b")], [PSB[4]])
                dst = KTs if isk else QTs
                dstb = bf("KTs") if isk else bf("QTs")
                act(lambda e, dst=dst, g=g: e.activation(out=dst[:, g * 4:g * 4 + 4, :],
                                                         in_=PS[4][:, :].bitcast(BF16)[:, 0:512].rearrange("p (c t) -> p c t", t=128)[:, :, 0:n],
                                                         func=AF.Copy), [PSB[4]], [dstb])
            if DBG_SSTOP == 'SQK':
                return
            for g in range(3):
                W, Wb = wblock(WIN, l, 0, 8, OVAL + g * 512)

                def mm(e, W=W):
                    ins = None
                    for kc in range(8):
                        ins = e.matmul(PS[0][0:n, :], lhsT=hsT[:, kc, :], rhs=W[:, kc, :], start=(kc == 0), stop=(kc == 7))
                    return ins
                pe(mm, [Wb, hb_], [PSB[0]])
                vf, vfb = t32()
                act(lambda e, vf=vf: e.activation(out=vf[0:n, :], in_=PS[0][0:n, :], func=AF.Copy), [PSB[0]], [vfb])
                dma_g(kvs[g][l, :, 1, :], vf[0:n, :], [vfb], [])
            if DBG_SSTOP == 'SV':
                return
            for bs in range(NBS):
                sample_attention(l, bs)
            if DBG_SSTOP == 'SATT':
                return
            branch_a_and_merge(l, n, 1, hsT, hb_, xsT, xb_, ybs, bf("ybs"), cols, True)
            if l == DEPTH - 1:
                final_out(xsT, xb_, n, lambda ti: y_s[:, :], 1, n)


        def sample_attention(l, bs):
            n = TS
            hb_ = bf("hsT")
            for g in range(3):
                W, Wb = wblock(WIN, l, 0, 8, OVAL + g * 512)

                def mm2(e, W=W):
                    ins = None
                    for kc in range(8):
                        ins = e.matmul(PS[1][0:T, :], lhsT=hsT[:, kc, bs * T:(bs + 1) * T], rhs=W[:, kc, :], start=(kc == 0), stop=(kc == 7))
                    return ins
                pe(mm2, [Wb, hb_], [PSB[1]])
                vdst = bass.AP(tensor=VAnB1[:].tensor, offset=VAnB1[0, g, 0, 0].offset, ap=[list(VAnB1[:].ap[0]), [192, 4], [128, 2], [1, 64]])
                act(lambda e, vdst=vdst: e.activation(out=vdst, in_=PS[1][0:T, :].rearrange("p (c h d) -> p c h d", h=2, d=64), func=AF.Copy),
                    [PSB[1]], [bf("VAnB")])
            acc = PS[5]
            accb = PSB[5]
            first = [True]
            tiles = [(0, 0, 0)] + [(1, r, 1 + r) for r in range(4)] + [(2, r, 5 + r) for r in range(8)]
            spend = []
            for idx, (g, r, mi) in enumerate(tiles):
                par = idx % 2
                d = DILS[g][1]
                ct = CT[par]
                ctb = bf(f"CT{par}")
                ktc = (KTc, KTcB)[par]
                vac = (VAc, VAcB)[par]
                es = (Es, EsB)[par]
                ktb, vab_, esb = bf(f"KTc{par}"), bf(f"VAc{par}"), bf(f"Es{par}")
                pt = 2 + par
                be, bo = (7, 6) if par == 0 else (1, 0)
                src = bass.AP(tensor=ck[g].tensor, offset=ck[g][l, bs, r, 0, 0].offset, ap=[[d * 1024, 128], [1, 1024]])
                dma_in(ct[:], src, [], [ctb])

                def tr(e, ct=ct, pt=pt):
                    ins = None
                    for c in range(4):
                        ins = e.transpose(out=PS[pt][:, c * 128:(c + 1) * 128], in_=ct[:, c * 128:(c + 1) * 128], identity=ident_f[:])
                    return ins
                pe(tr, [ctb, bf("ident_f")], [PSB[pt]])
                act(lambda e, ktc=ktc, pt=pt: e.activation(out=ktc[:], in_=PS[pt][:, :].rearrange("p (c t) -> p c t", t=128), func=AF.Copy),
                    [PSB[pt]], [ktb])
                vdst = bass.AP(tensor=vac[:].tensor, offset=vac[0, 0, 0].offset, ap=[list(vac[:].ap[0]), [192, 4], [128, 2], [1, 64]])
                dve(lambda e, ct=ct, vdst=vdst: e.tensor_copy(out=vdst, in_=ct[:, 512:1024].rearrange("p (c h d) -> p c h d", h=2, d=64)),
                    [ctb], [vab_])

                def rest(g=g, mi=mi, ktc=ktc, vac=vac, es=es, ktb=ktb, vab_=vab_, esb=esb, be=be, bo=bo):
                    def smm(e):
                        ins = None
                        for h in (0, 2, 4, 6, 1, 3, 5, 7):
                            rows = slice((h % 2) * 64, (h % 2) * 64 + 64)
                            ins = e.matmul(PS[be if h % 2 == 0 else bo][:, (h // 2) * T:(h // 2 + 1) * T], lhsT=ktc[rows, h // 2, :],
                                           rhs=QTs[rows, g * 4 + h // 2, bs * T:(bs + 1) * T], start=True, stop=True)
                        return ins
                    pe(smm, [ktb, bf("QTs")], [PSB[be], PSB[bo]])
                    es4 = es[:, :].rearrange("p (c h t) -> p c h t", h=2, t=T)
                    for hh in range(2):
                        bk = be if hh == 0 else bo
                        act(lambda e, hh=hh, bk=bk: e.activation(out=es4[:, :, hh, :], in_=PS[bk][:, 0:4 * T].rearrange("p (c t) -> p c t", t=T),
                                                                 func=AF.Exp, scale=0.125), [PSB[bk]], [esb])
                    msk = bass.AP(tensor=MS[:].tensor, offset=MS[0, mi, 0].offset, ap=[list(MS[:].ap[0]), [0, 8], [1, T]])
                    dve(lambda e: e.tensor_tensor(out=es[:, :].rearrange("p (h t) -> p h t", t=T), in0=es[:, :].rearrange("p (h t) -> p h t", t=T),
                                                  in1=msk, op=ALU.mult), [esb, bf("MS")], [esb])

                    def pmm(e):
                        ins = None
                        for h in range(8):
                            vcol = slice(0, 128) if h % 2 == 0 else slice(64, 192)
                            ins = e.matmul(acc[:, h * T:(h + 1) * T], lhsT=vac[:, h // 2, vcol], rhs=es[:, h * T:(h + 1) * T], start=first[0], stop=False,
                                           skip_group_check=True)
                            first[0] = False
                        return ins
                    pe(pmm, [vab_, esb], [accb])
                spend.append(rest)
                while len(spend) > 1:
                    spend.pop(0)()
            while spend:
                spend.pop(0)()
            if DBG_SA <= 3:
                return
            for g in range(3):
                def smm(e, g=g):
                    ins = None
                    for h in (0, 2, 4, 6, 1, 3, 5, 7):
                        rows = slice((h % 2) * 64, (h % 2) * 64 + 64)
                        ins = e.matmul(PS[7 - (h % 2)][0:T, (h // 2) * T:(h // 2 + 1) * T], lhsT=KTs[rows, g * 4 + h // 2, bs * T:(bs + 1) * T],
                                       rhs=QTs[rows, g * 4 + h // 2, bs * T:(bs + 1) * T], start=True, stop=True)
                    return ins
                pe(smm, [bf("KTs"), bf("QTs")], [PSB[7], PSB[6]])
                if DBG_SA == 35:
                    continue
                Esn4 = Esn[:, :].rearrange("p (c h t) -> p c h t", h=2, t=T)
                for hh in range(2):
                    act(lambda e, hh=hh: e.activation(out=Esn4[:, :, hh, :], in_=PS[7 - hh][0:T, 0:4 * T].rearrange("p (c t) -> p c t", t=T),
                                                      func=AF.Exp, scale=0.125), [PSB[7 - hh]], [bf("Esn")])
                msk = bass.AP(tensor=MN[:].tensor, offset=MN[0, g, 0].offset, ap=[list(MN[:].ap[0]), [0, 8], [1, T]])
                dve(lambda e, msk=msk: e.tensor_tensor(out=Esn[:, :].rearrange("p (h t) -> p h t", t=T), in0=Esn[:, :].rearrange("p (h t) -> p h t", t=T),
                                                       in1=msk, op=ALU.mult), [bf("Esn"), bf("MN")], [bf("Esn")])
                if DBG_SA == 36:
                    continue

                def pmm(e, g=g):
                    ins = None
                    for h in range(8):
                        vcol = slice(0, 128) if h % 2 == 0 else slice(64, 192)
                        ins = e.matmul(acc[:, h * T:(h + 1) * T], lhsT=VAnB[bs][:, g, h // 2, vcol], rhs=Esn[:, h * T:(h + 1) * T], start=False, stop=False,
                                       skip_group_check=True)
                    return ins
                pe(pmm, [bf("VAnB"), bf("Esn")], [accb])
            if DBG_SA <= 4 or DBG_SA in (35, 36):
                return
            rz, rzb = t32()
            for hh in range(2):
                urow = slice(hh * 64, hh * 64 + 64)
                zrow = slice(64, 128) if hh == 0 else slice(0, 64)
                a3 = acc[:, 0:64].rearrange("p (c h t) -> p c h t", h=2, t=T)
                r3 = rz[:, 0:32].rearrange("p (c t) -> p c t", t=T)
                act(lambda e, urow=urow, zrow=zrow, hh=hh: e.activation(out=r3[urow], in_=a3[zrow, :, hh, :], func=AF.Ln), [accb], [rzb])
                act(lambda e, urow=urow: e.activation(out=r3[urow], in_=r3[urow], func=AF.Exp, scale=-1.0), [rzb], [rzb])
                dve(lambda e, urow=urow, hh=hh: e.tensor_tensor(out=ybs[urow, :, bs * T:(bs + 1) * T], in0=a3[urow, :, hh, :], in1=r3[urow], op=ALU.mult),
                    [accb, rzb], [bf("ybs")])

        ng = 0
        for b in range(NBP):
            for l in range(DEPTH):
                for tg in range(4):
                    if ng < DBG_NG:
                        prompt_group(b, l, tg)
                    ng += 1
        if NBS > 0 and DBG_SAMPLE:
            for l in range(DEPTH):
                sample_layer(l)
        sc.finish()

    return nc


_NC_CACHE = {}


def _run(inputs, NBP, NBS, ncores):
    key = (NBP, NBS)
    if key not in _NC_CACHE:
        _NC_CACHE[key] = build(NBP, NBS)
    nc = _NC_CACHE[key]
    f = lambda a: np.ascontiguousarray(np.asarray(a, dtype=np.float32))
    cst = _consts()
    shared = {k: f(inputs[k]) for k in ("w_ada", "b_ada", "norm_g", "w_in", "gm_ln_g", "gm_ln_b", "gm_ws", "gm_bs",
                                        "w_gm_out", "w_att_out", "w_o", "final_g")}
    shared.update(cst)
    xp, xs = f(inputs["x_prompt"]), f(inputs["x_sample"])
    cp, cs = f(inputs["c_prompt"]), f(inputs["c_sample"])
    caches = [f(inputs["cache_kv_w128"]), f(inputs["cache_kv_w512"]), f(inputs["cache_kv_w2048"])]
    in_maps = []
    for i in range(ncores):
        m = dict(shared)
        m["x_p"] = np.ascontiguousarray(xp[i * NBP:(i + 1) * NBP])
        m["x_s"] = np.ascontiguousarray(xs[i * NBS:(i + 1) * NBS].reshape(NBS * T, D))
        m["c_all"] = np.ascontiguousarray(np.concatenate([cp[i * NBP:(i + 1) * NBP], cs[i * NBS:(i + 1) * NBS]], 0))
        for g in range(3):
            cg = caches[g][:, i * NBS:(i + 1) * NBS]
            m[f"ck{g}"] = np.ascontiguousarray(cg.reshape(DEPTH, NBS, cg.shape[2], 2, 512))
        in_maps.append(m)
    res = run_bass_kernel_spmd(nc, in_maps, core_ids=list(range(ncores)))
    R = res.results
    y_p = np.concatenate([r["y_p"] for r in R], 0)
    y_s = np.concatenate([r["y_s"].reshape(NBS, T, D) for r in R], 0)
    outs = [y_p, y_s]
    for g in range(3):
        keep = DILS[g][0]
        outs.append(np.concatenate([r[f"kvp{g}"].reshape(DEPTH, NBP, keep, 2, 8, 64) for r in R], 1))
    for g in range(3):
        outs.append(np.concatenate([r[f"kvs{g}"].reshape(DEPTH, NBS, T, 2, 8, 64) for r in R], 1))
    outs.append(np.concatenate([r["gmv"].reshape(DEPTH, NBS, T, D) for r in R], 1))
    return tuple(np.ascontiguousarray(o.astype(np.float32)) for o in outs)


def kernel(**inputs):
    return _run(inputs, 2, 4, NCORES)
```

```python
import numpy as np
from contextlib import ExitStack
import concourse.bass as bass
import concourse.mybir as mybir
from concourse.bass_utils import run_bass_kernel_spmd

F32, BF16 = mybir.dt.float32, mybir.dt.bfloat16
AF = mybir.ActivationFunctionType
ALU = mybir.AluOpType

D = 1024
S = 2048
DEPTH = 2
T = 8
PAST = 16384
NCORES = 8
INW = 10240
OU, OV, OZA, OQ, OK_, OVAL, OZB, OGA, OGB = 0, 1024, 2048, 3072, 4608, 6144, 7680, 8192, 9216
DILS = ((128, 1), (512, 4), (2048, 16))
ENG = ("pe", "act", "dve", "pool", "sp")
EPOCH = 30000
NDS = 24


class Buf:
    __slots__ = ("w", "r")

    def __init__(self):
        self.w = None
        self.r = {}


class Sched:
    def __init__(self, nc, es):
        self.nc = nc
        self.eh = {"pe": nc.tensor, "act": nc.scalar, "dve": nc.vector, "pool": nc.gpsimd, "sp": nc.sync}
        self.cnt = {e: 0 for e in ENG}
        self.sems = {e: [es.enter_context(nc.semaphore(f"s_{e}{i}")) for i in range(3)] for e in ENG}
        self.dsem = [es.enter_context(nc.semaphore(f"d{i}")) for i in range(2 * NDS)]
        self.dtgt = [0] * (2 * NDS)
        self.rr = {"sp": 0, "pool": 0}
        self.waited = {}

    def _waits(self, eng, deps):
        out = []
        for tok in deps:
            if tok[0] == "e":
                key = (eng, "e", tok[1])
                val = (tok[2], tok[3])
                sem = self.sems[tok[1]][tok[2]]
                v = tok[3]
            else:
                key = (eng, "d", tok[1])
                val = (0, tok[2])
                sem = self.dsem[tok[1]]
                v = tok[2]
            if self.waited.get(key, (-1, -1)) >= val:
                continue
            self.waited[key] = val
            out.append((sem, v))
        return out

    def op(self, eng, fn, reads=(), writes=(), dma=False):
        deps = []
        for b in reads:
            if b.w is not None:
                deps.append(b.w)
        for b in writes:
            if b.w is not None:
                deps.append(b.w)
            deps.extend(b.r.values())
        if dma:
            k = self.rr[eng] % NDS + (NDS if eng == "pool" else 0)
            self.rr[eng] += 1
            old = self.dtgt[k]
            self.dtgt[k] += 16
            if old > 0:
                deps.append(("d", k, old))
            tok = ("d", k, self.dtgt[k])
            sem, inc = self.dsem[k], 16
        else:
            c = self.cnt[eng]
            self.cnt[eng] += 1
            ep, v = divmod(c, EPOCH)
            tok = ("e", eng, ep, v + 1)
            sem, inc = self.sems[eng][ep], 1
        waits = self._waits(eng, deps)

        e = self.eh[eng]
        for s_, v_ in waits:
            e.wait_ge(s_, v_)
        fn(e).then_inc(sem, inc)
        rk = ("e", eng) if not dma else ("d", tok[1])
        for b in reads:
            b.r[rk] = tok
        for b in writes:
            b.w = tok
            b.r = {}
        return tok

    def finish(self):
        deps = [("d", k, self.dtgt[k]) for k in range(2 * NDS) if self.dtgt[k] > 0]
        for e in ENG:
            if e != "sp" and self.cnt[e] > 0:
                ep, v = divmod(self.cnt[e] - 1, EPOCH)
                deps.append(("e", e, ep, v + 1))
        waits = self._waits("sp", deps)

        for s_, v_ in waits:
            self.eh["sp"].wait_ge(s_, v_)


def _consts():
    half = 32
    inv = (10000.0 ** (-np.arange(half, dtype=np.float32) / np.float32(half))).astype(np.float32)

    def tab(pos):
        ang = (pos.astype(np.float32)[:, None] * inv[None, :]).astype(np.float32)
        c = np.cos(ang.astype(np.float64)).astype(np.float32)
        s = np.sin(ang.astype(np.float64)).astype(np.float32)
        return np.concatenate([c, c], 1), np.concatenate([-s, s], 1)

    cc, ss = tab(np.arange(S))
    ccs, sss = tab(PAST + np.arange(T))
    k = np.arange(128)[:, None]
    q = np.arange(128)[None, :]
    prev = (k >= q).astype(np.float32)
    cur = (k <= q).astype(np.float32)
    mb = np.concatenate([prev, cur, prev, cur], 1)
    mg = np.zeros((4, 128, 512), np.float32)
    for tq in range(4):
        m = (np.arange(128)[:, None] <= (32 * tq + np.arange(32))[None, :]).astype(np.float32)
        mg[tq] = np.tile(m, (1, 16))
    tri = (np.arange(128)[:, None] >= np.arange(128)[None, :]).astype(np.float32)
    tt = np.arange(T)[None, :]
    rows = np.arange(128)[:, None]
    ms = np.zeros((13, 128, T), np.float32)
    ms[0] = (rows >= tt)
    for r in range(4):
        ms[1 + r] = ((tt % 4) == r) & ((tt < 4) | (rows >= 1))
    for r in range(8):
        ms[5 + r] = (tt == r) & (rows >= 0)
    tk = np.arange(T)[:, None]
    mn = np.zeros((3, T, T), np.float32)
    mn[0] = (tk <= tt)
    mn[1] = (tk <= tt) & (((tt - tk) % 4) == 0)
    mn[2] = (tk == tt)
    return dict(
        c_cc=np.ascontiguousarray(cc), c_ss=np.ascontiguousarray(ss),
        c_ccs=np.ascontiguousarray(np.tile(ccs, (4, 1))), c_sss=np.ascontiguousarray(np.tile(sss, (4, 1))),
        c_mb=mb, c_mg=mg, c_tri=tri, c_id=np.eye(128, dtype=np.float32),
        c_ms=np.ascontiguousarray(ms), c_mn=np.ascontiguousarray(mn),
    )


DBG_STOP = ''
DBG_NG = 99
DBG_SAMPLE = True
DBG_SSTOP = ''
DBG_SA = 9


def build(NBP=2, NBS=4):
    nc = bass.Bass("TRN2", target_bir_lowering=False)
    NC6 = NBP + NBS
    TS = NBS * T

    def din(name, shape, dt=F32):
        return nc.dram_tensor(name, list(shape), dt, kind="ExternalInput").ap()

    def dout(name, shape):
        return nc.dram_tensor(name, list(shape), F32, kind="ExternalOutput").ap()

    def dint(name, shape, dt):
        return nc.dram_tensor(name, list(shape), dt, kind="Internal").ap()

    x_p = din("x_p", [NBP, S, D]); x_s = din("x_s", [TS, D])
    c_all = din("c_all", [NC6, D])
    ck = [din("ck0", [DEPTH, NBS, 128, 2, 512]), din("ck1", [DEPTH, NBS, 512, 2, 512]),
          din("ck2", [DEPTH, NBS, 2048, 2, 512])]
    w_ada = din("w_ada", [DEPTH, D, 3 * D]); b_ada = din("b_ada", [DEPTH, 3 * D])
    norm_g = din("norm_g", [DEPTH, D]); w_in = din("w_in", [DEPTH, D, INW])
    gm_ln_g = din("gm_ln_g", [DEPTH, D]); gm_ln_b = din("gm_ln_b", [DEPTH, D])
    gm_ws = din("gm_ws", [DEPTH, 8, 128, 128]); gm_bs = din("gm_bs", [DEPTH, 8, 128])
    w_gm = din("w_gm_out", [DEPTH, D, D]); w_att = din("w_att_out", [DEPTH, 512, D])
    w_o = din("w_o", [DEPTH, D, D]); final_g = din("final_g", [D])
    c_cc = din("c_cc", [S, 64]); c_ss = din("c_ss", [S, 64])
    c_ccs = din("c_ccs", [32, 64]); c_sss = din("c_sss", [32, 64])
    c_mb = din("c_mb", [128, 512]); c_mg = din("c_mg", [4, 128, 512]); c_tri = din("c_tri", [128, 128])
    c_id = din("c_id", [128, 128]); c_ms = din("c_ms", [13, 128, T]); c_mn = din("c_mn", [3, T, T])

    y_p = dout("y_p", [NBP, S, D]); y_s = dout("y_s", [TS, D])
    kvp = [dout("kvp0", [DEPTH, NBP, 128, 2, 512]), dout("kvp1", [DEPTH, NBP, 512, 2, 512]),
           dout("kvp2", [DEPTH, NBP, 2048, 2, 512])]
    kvs = [dout(f"kvs{g}", [DEPTH, TS, 2, 512]) for g in range(3)]
    gmv = dout("gmv", [DEPTH, TS, D])

    WIN = dint("WIN", [DEPTH, D, INW], BF16)
    WGM = dint("WGM", [DEPTH, 8, 128, 8, 128], BF16)
    WATT = dint("WATT", [DEPTH, 8, 128, 4, 128], BF16)
    WO = dint("WO", [DEPTH, 8, 128, 8, 128], BF16)
    XS = dint("XS", [8, 128, S], F32)
    KH = dint("KH", [12, 128, S], BF16)
    VH = dint("VH", [S, 3, 4, 192], BF16)
    WMT = dint("WMT", [DEPTH, 8, 128, 128], BF16)

    es = ExitStack()
    with es:
        def sb(name, shape, dt=F32):
            return es.enter_context(nc.sbuf_tensor(name, list(shape), dt))

        sc = Sched(nc, es)
        PS = [es.enter_context(nc.psum_tensor(f"ps{i}", [128, 512], F32)) for i in range(8)]
        PSB = [Buf() for _ in range(8)]

        ident_f = sb("ident_f", [128, 128]); ident_b = sb("ident_b", [128, 128], BF16)
        MB = sb("MB", [128, 512], BF16); MG = sb("MG", [128, 4, 512], BF16)
        TRI = sb("TRI", [128, 128])
        wmT = sb("wmT", [128, DEPTH, 8, 128], BF16)
        SV = sb("SV", [128, 128]); SVT = sb("SVT", [128, 72])
        modT = sb("modT", [128, DEPTH, 24, NC6])
        Am = sb("Am", [128, DEPTH, 8, NC6]); G4 = sb("G4", [128, DEPTH, 8, NC6])
        cT = sb("cT", [128, 8, NC6]); scT = sb("scT", [128, 8, NC6], BF16)
        LG = sb("LG", [128, D]); LB = sb("LB", [128, D]); BSR = sb("BSR", [128, D])
        xT = sb("xT", [128, 8, 512]); hT = sb("hT", [128, 8, 512], BF16)
        rstd = sb("rstd", [128, 512])
        NWB = 4
        WB = [sb(f"WB{i}", [128, 8, 512], BF16) for i in range(NWB)]
        NWS = 5
        WBS = [sb(f"WBS{i}", [128, 8, 128], BF16) for i in range(NWS)]
        NT = 8
        T32 = [sb(f"T32_{i}", [128, 512]) for i in range(NT)]
        NTB = 5
        TB = [sb(f"TB_{i}", [128, 512], BF16) for i in range(NTB)]
        XL = [sb(f"XL{i}", [128, D]) for i in range(2)]
        QT = sb("QT", [128, 12, 512], BF16); KT = sb("KT", [128, 12, 512], BF16)
        MGT = QT[:, 0:8, :]
        BIG = KT[:, 0:8, :]
        MBf = XL[0][:, 0:512]
        call_t = XL[1][0:NC6, :]
        KHt = [sb("KH0", [128, 640], BF16), sb("KH1", [128, 1024], BF16), sb("KH2", [128, 2048], BF16)]
        VAs = [sb(f"VA{i}", [128, 29, 192], BF16) for i in range(2)]
        vn = sb("vn", [128, 4, D], BF16)
        VST = [sb(f"VST{i}", [128, 4, 192], BF16) for i in range(2)]
        ybT = sb("ybT", [128, 4, 512], BF16)
        CC = sb("CC", [128, 4, 64]); SSn = sb("SSn", [128, 4, 64])
        st6 = sb("st6", [128, 4, 2, 6]); mv = sb("mv", [128, 4, 2]); lnr = sb("lnr", [128, 4])
        xsT = sb("xsT", [128, 8, TS]); hsT = sb("hsT", [128, 8, TS], BF16)
        QTs = sb("QTs", [128, 12, TS], BF16); KTs = sb("KTs", [128, 12, TS], BF16)
        CT = XL
        KTc = sb("KTc", [128, 4, 128], BF16); VAc = sb("VAc", [128, 4, 192], BF16)
        KTcB = sb("KTcB", [128, 4, 128], BF16); VAcB = sb("VAcB", [128, 4, 192], BF16); EsB = sb("EsB", [128, 64], BF16)
        MS = sb("MS", [128, 13, T], BF16); MN = sb("MN", [T, 3, T], BF16)
        MSf = sb("MSf", [128, 13, T]); MNf = sb("MNf", [T, 3, T])
        CCs = sb("CCs", [32, 64]); SSs = sb("SSs", [32, 64])
        WS = sb("WS", [32, 8, 32], BF16)
        ybs = sb("ybs", [128, 4, TS], BF16)
        Es = sb("Es", [128, 64], BF16); Esn = sb("Esn", [T, 64], BF16)

        B = {}

        ALIAS = {"MGT": "QT", "BIG": "KT", "MBf": "XL0", "call": "XL1", "CT0": "XL0", "CT1": "XL1"}

        def bf(name):
            name = ALIAS.get(name, name)
            if name not in B:
                B[name] = Buf()
            return B[name]

        t32b = [Buf() for _ in range(NT)]
        tbb = [Buf() for _ in range(NTB)]
        t32i = [0]
        tbi = [0]

        def t32():
            i = t32i[0] % NT
            t32i[0] += 1
            return T32[i], t32b[i]

        def tb16():
            i = tbi[0] % NTB
            tbi[0] += 1
            return TB[i], tbb[i]

        wbb = [Buf() for _ in range(NWB)]
        wbi = [0]
        wsb = [Buf() for _ in range(NWS)]
        wsi = [0]

        def pe(fn, r, w):
            return sc.op("pe", fn, r, w)

        def act(fn, r, w):
            return sc.op("act", fn, r, w)

        def dve(fn, r, w):
            return sc.op("dve", fn, r, w)

        def pool(fn, r, w):
            return sc.op("pool", fn, r, w)

        def dma_in(out, in_, r, w):
            return sc.op("sp", lambda e: e.dma_start(out=out, in_=in_), r, w, dma=True)

        def dma_g(out, in_, r, w):
            return sc.op("pool", lambda e: e.dma_start(out=out, in_=in_), r, w, dma=True)

        def bcast_rows(ap1d, n):
            return bass.AP(tensor=ap1d.tensor, offset=ap1d.offset, ap=[[0, 128], [1, n]])

        dma_in(ident_f[:], c_id[:, :], [], [bf("ident_f")])
        pool(lambda e: e.tensor_copy(out=ident_b[:], in_=ident_f[:]), [bf("ident_f")], [bf("ident_b")])
        dma_in(MBf, c_mb[:, :], [], [bf("MBf")])
        pool(lambda e: e.tensor_copy(out=MB[:], in_=MBf), [bf("MBf")], [bf("MB")])
        for tq in range(4):
            dma_in(MBf, c_mg[tq], [], [bf("MBf")])
            pool(lambda e, tq=tq: e.tensor_copy(out=MG[:, tq, :], in_=MBf), [bf("MBf")], [bf("MG")])
        dma_in(TRI[:], c_tri[:, :], [], [bf("TRI")])
        dma_in(MSf[:], c_ms.rearrange("k p t -> p k t"), [], [bf("MSf")])
        pool(lambda e: e.tensor_copy(out=MS[:], in_=MSf[:]), [bf("MSf")], [bf("MS")])
        dma_in(MNf[:], c_mn.rearrange("k p t -> p k t"), [], [bf("MNf")])
        pool(lambda e: e.tensor_copy(out=MN[:], in_=MNf[:]), [bf("MNf")], [bf("MN")])
        dma_in(CCs[:], c_ccs[:, :], [], [bf("CCs")])
        dma_in(SSs[:], c_sss[:, :], [], [bf("SSs")])
        for i in range(2):
            pool(lambda e, i=i: e.memset(VAs[i][:], 1.0), [], [bf(f"VA{i}")])
            pool(lambda e, i=i: e.memset(VST[i][:], 1.0), [], [bf(f"VST{i}")])
        vsti = [0]
        pool(lambda e: e.memset(VAc[:], 1.0), [], [bf("VAc0")])
        pool(lambda e: e.memset(VAcB[:], 1.0), [], [bf("VAc1")])
        VAnB1 = sb("VAnB", [T, 3, 4, 192], BF16)
        VAnB = [VAnB1 for _ in range(max(NBS, 1))]
        pool(lambda e: e.memset(VAnB1[:], 1.0), [], [bf("VAnB")])
        pool(lambda e: e.memset(WS[:], 0.0), [], [bf("WS")])
        pool(lambda e: e.memset(KHt[2][:], 0.0), [], [bf("KH2")])

        wconv = bf("wconv")
        for l in range(DEPTH):
            for kc in range(8):
                dma_g(WIN[l, kc * 128:(kc + 1) * 128, :], w_in[l, kc * 128:(kc + 1) * 128, :], [], [])
                dma_g(bass.AP(tensor=WGM.tensor, offset=WGM[l, 0, 0, kc, 0].offset, ap=[[8 * 128, 128], [128 * 8 * 128, 8], [1, 128]]),
                      w_gm[l, kc * 128:(kc + 1) * 128, :].rearrange("p (o j) -> p o j", j=128), [], [])
                dma_g(bass.AP(tensor=WO.tensor, offset=WO[l, 0, 0, kc, 0].offset, ap=[[8 * 128, 128], [128 * 8 * 128, 8], [1, 128]]),
                      w_o[l, kc * 128:(kc + 1) * 128, :].rearrange("p (o j) -> p o j", j=128), [], [])
            for kc in range(4):
                dma_g(bass.AP(tensor=WATT.tensor, offset=WATT[l, 0, 0, kc, 0].offset, ap=[[4 * 128, 128], [128 * 4 * 128, 8], [1, 128]]),
                      w_att[l, kc * 128:(kc + 1) * 128, :].rearrange("p (o j) -> p o j", j=128), [], [])
        conv_deps = [("d", k, sc.dtgt[k]) for k in range(2 * NDS) if sc.dtgt[k] > 0]
        wts = sc._waits("pool", conv_deps)

        for s_, v_ in wts:
            nc.gpsimd.wait_ge(s_, v_)
        pool(lambda e: e.memset(Es[:], 0.0), [], [wconv])

        dma_in(SV[0:48, :], b_ada.rearrange("l (j p) -> (l j) p", p=128), [], [bf("SV")])
        dma_in(SV[48:64, :], norm_g.rearrange("l (j p) -> (l j) p", p=128), [], [bf("SV")])
        dma_in(SV[64:72, :], final_g.rearrange("(j p) -> j p", p=128), [], [bf("SV")])
        pe(lambda e: e.transpose(out=PS[0][:, 0:72], in_=SV[0:72, :], identity=ident_f[0:72, 0:72]),
           [bf("SV"), bf("ident_f")], [PSB[0]])
        dve(lambda e: e.tensor_copy(out=SVT[:], in_=PS[0][:, 0:72]), [PSB[0]], [bf("SVT")])

        for l in range(DEPTH):
            for g in range(8):
                tt_, tt_b = t32()
                dma_in(tt_[:, 0:128], gm_ws[l, g], [], [tt_b])
                dve(lambda e, tt_=tt_: e.tensor_tensor(out=tt_[:, 128:256], in0=tt_[:, 0:128], in1=TRI[:], op=ALU.mult),
                    [tt_b, bf("TRI")], [tt_b])
                pe(lambda e, tt_=tt_: e.transpose(out=PS[1][:, 0:128], in_=tt_[:, 128:256], identity=ident_f[:]),
                   [tt_b, bf("ident_f")], [PSB[1]])
                dve(lambda e, l=l, g=g: e.tensor_copy(out=wmT[:, l, g, :], in_=PS[1][:, 0:128]), [PSB[1]], [bf("wmT")])
        dma_g(WMT.rearrange("l g s t -> s l g t"), wmT[:], [bf("wmT")], [bf("WMT")])

        dma_in(call_t, c_all[:, :], [], [bf("call")])
        for kc in range(8):
            pe(lambda e, kc=kc: e.transpose(out=PS[0][:, kc * 8:kc * 8 + NC6], in_=call_t[:, kc * 128:(kc + 1) * 128],
                                            identity=ident_f[0:NC6, 0:NC6]), [bf("call"), bf("ident_f")], [PSB[0]])
        ps0v = PS[0][:, 0:64].rearrange("p (k i) -> p k i", i=8)[:, :, 0:NC6]
        dve(lambda e: e.tensor_copy(out=cT[:], in_=ps0v), [PSB[0]], [bf("cT")])
        tt_, tt_b = t32()
        ttv = tt_[:, 0:8 * NC6].rearrange("p (k i) -> p k i", i=NC6)
        act(lambda e: e.activation(out=ttv, in_=cT[:], func=AF.Tanh, scale=0.5), [bf("cT")], [tt_b])
        dve(lambda e: e.scalar_tensor_tensor(out=ttv, in0=ttv, scalar=1.0, in1=cT[:], op0=ALU.add, op1=ALU.mult),
            [tt_b, bf("cT")], [tt_b])
        dve(lambda e: e.tensor_scalar(out=scT[:], in0=ttv, scalar1=0.5, scalar2=None, op0=ALU.mult), [tt_b], [bf("scT")])
        for l in range(DEPTH):
            for cb in range(6):
                i = wbi[0] % NWB
                wbi[0] += 1
                src = bass.AP(tensor=w_ada.tensor, offset=w_ada[l, 0, cb * 512].offset,
                              ap=[[3 * D, 128], [128 * 3 * D, 8], [1, 512]])
                dma_g(WB[i][:], src, [], [wbb[i]])
                for ch in range(4):
                    j = cb * 4 + ch

                    def mm(e, i=i, ch=ch):
                        ins = None
                        for kc in range(8):
                            ins = e.matmul(PS[2][:, 0:NC6], lhsT=WB[i][:, kc, ch * 128:(ch + 1) * 128], rhs=scT[:, kc, :],
                                           start=(kc == 0), stop=(kc == 7))
                        return ins
                    pe(mm, [wbb[i], bf("scT")], [PSB[2]])
                    dve(lambda e, l=l, j=j: e.tensor_scalar(out=modT[:, l, j, :], in0=PS[2][:, 0:NC6],
                                                            scalar1=SVT[:, l * 24 + j:l * 24 + j + 1], scalar2=None, op0=ALU.add),
                        [PSB[2], bf("SVT")], [bf("modT")])
            for kc in range(8):
                dve(lambda e, l=l, kc=kc: e.tensor_scalar(out=Am[:, l, kc, :], in0=modT[:, l, 8 + kc, :], scalar1=1.0,
                                                          scalar2=SVT[:, 48 + l * 8 + kc:48 + l * 8 + kc + 1],
                                                          op0=ALU.add, op1=ALU.mult), [bf("modT"), bf("SVT")], [bf("Am")])
                dve(lambda e, l=l, kc=kc: e.tensor_scalar(out=G4[:, l, kc, :], in0=modT[:, l, 16 + kc, :], scalar1=0.25,
                                                          scalar2=None, op0=ALU.mult), [bf("modT")], [bf("G4")])

        def wblock(src3, l, k0, nk, col0, ncols=512):
            if ncols == 128:
                i = wsi[0] % NWS
                wsi[0] += 1
                dma_in(WBS[i][:, 0:nk, :], src3[l, col0 // 128], [wconv], [wsb[i]])
                return WBS[i], wsb[i]
            ncol_total = src3.shape[2]
            src = bass.AP(tensor=src3.tensor, offset=src3[l, k0 * 128, col0].offset,
                          ap=[[ncol_total, 128], [128 * ncol_total, nk], [1, ncols]])
            i = wbi[0] % NWB
            wbi[0] += 1
            dma_in(WB[i][:, 0:nk, 0:ncols], src, [wconv], [wbb[i]])
            return WB[i], wbb[i]

        def rms_rstd(src, srcb, n, gsel):
            act(lambda e: e.activation(out=BIG[:, :, 0:n], in_=src, func=AF.Square), [srcb], [bf("BIG")])

            def mm(e):
                ins = None
                for kc in range(8):
                    ins = e.matmul(PS[7][:, 0:n], lhsT=gsel, rhs=BIG[:, kc, 0:n], start=(kc == 0), stop=(kc == 7))
                return ins
            pe(mm, [bf("BIG"), bf("ones")], [PSB[7]])
            dve(lambda e: e.tensor_scalar(out=rstd[:, 0:n], in0=PS[7][:, 0:n], scalar1=1.0 / D, scalar2=1e-6,
                                          op0=ALU.mult, op1=ALU.add), [PSB[7]], [bf("rstd")])
            act(lambda e: e.activation(out=rstd[:, 0:n], in_=rstd[:, 0:n], func=AF.Ln), [bf("rstd")], [bf("rstd")])
            act(lambda e: e.activation(out=rstd[:, 0:n], in_=rstd[:, 0:n], func=AF.Exp, scale=-0.5), [bf("rstd")], [bf("rstd")])

        ones_b = sb("ones_b", [128, 128], BF16)
        pool(lambda e: e.memset(ones_b[:], 1.0), [], [bf("ones")])
        for tb_ in range(S // 128):
            dma_g(bass.AP(tensor=VH.tensor, offset=VH[tb_ * 128, 0, 0, 64].offset, ap=[[2304, 128], [192, 12], [1, 64]]),
                  bass.AP(tensor=ones_b[:].tensor, offset=ones_b[:].offset, ap=[list(ones_b[:].ap[0]), [0, 12], [1, 64]]),
                  [bf("ones")], [bf("VH")])

        def load_layer_vecs(l):
            dma_in(LG[:], bcast_rows(gm_ln_g[l], D), [], [bf("LG")])
            dma_in(LB[:], bcast_rows(gm_ln_b[l], D), [], [bf("LB")])
            dma_in(BSR[:], bcast_rows(gm_bs[l].rearrange("g t -> (g t)"), D), [], [bf("BSR")])

        def make_h(xsrc, xb_, n, l, cidx_of_col):
            dst = hT if n == 512 else hsT
            dstb = bf("hT") if n == 512 else bf("hsT")
            for kc in range(8):
                tt_, tt_b = t32()
                dve(lambda e, kc=kc, tt_=tt_: e.tensor_tensor(out=tt_[:, 0:n], in0=xsrc[:, kc, 0:n], in1=rstd[:, 0:n], op=ALU.mult),
                    [xb_, bf("rstd")], [tt_b])
                for (c0, c1, ci) in cidx_of_col:
                    dve(lambda e, kc=kc, tt_=tt_, c0=c0, c1=c1, ci=ci: e.tensor_scalar(
                        out=dst[:, kc, c0:c1], in0=tt_[:, c0:c1], scalar1=Am[:, l, kc, ci:ci + 1],
                        scalar2=modT[:, l, kc, ci:ci + 1], op0=ALU.mult, op1=ALU.add),
                        [tt_b, bf("Am"), bf("modT")], [dstb])

        def rope_block(ps_ap, cc_ap, ss_ap, npart, want_f32):
            m1, m1b = t32()
            m2, m2b = t32()
            p3 = ps_ap.rearrange("p (h d) -> p h d", d=64)
            m13 = m1[0:npart, :].rearrange("p (h d) -> p h d", d=64)
            m23 = m2[0:npart, :].rearrange("p (h d) -> p h d", d=64)
            ccb = bass.AP(tensor=cc_ap.tensor, offset=cc_ap.offset, ap=[list(cc_ap.ap[0]), [0, 8], [1, 64]])
            ss1 = bass.AP(tensor=ss_ap.tensor, offset=ss_ap.offset, ap=[list(ss_ap.ap[0]), [0, 8], [1, 32]])
            ss2 = bass.AP(tensor=ss_ap.tensor, offset=ss_ap.offset + 32, ap=[list(ss_ap.ap[0]), [0, 8], [1, 32]])
            return m1, m1b, m2, m2b, p3, m13, m23, ccb, ss1, ss2

        cur_l = [-1]

        def prompt_group(b, l, tg):
            T0 = tg * 512
            ci = b
            if cur_l[0] != l:
                load_layer_vecs(l)
                cur_l[0] = l
            xTb = bf("xT")
            if l == 0:
                for ti in range(4):
                    xl = XL[ti % 2]
                    xlb = bf(f"XL{ti % 2}")
                    dma_in(xl[:], x_p[b, T0 + ti * 128:T0 + (ti + 1) * 128, :], [], [xlb])
                    for hf in range(2):
                        pi = 4 + hf

                        def tr(e, xl=xl, hf=hf, pi=pi):
                            ins = None
                            for k in range(4):
                                kc = hf * 4 + k
                                ins = e.transpose(out=PS[pi][:, k * 128:(k + 1) * 128], in_=xl[:, kc * 128:(kc + 1) * 128],
                                                  identity=ident_f[:])
                            return ins
                        pe(tr, [xlb, bf("ident_f")], [PSB[pi]])
                        act(lambda e, hf=hf, pi=pi, ti=ti: e.activation(
                            out=xT[:, hf * 4:hf * 4 + 4, ti * 128:(ti + 1) * 128],
                            in_=PS[pi][:, :].rearrange("p (k t) -> p k t", t=128), func=AF.Copy), [PSB[pi]], [xTb])
            else:
                dma_in(xT[:], XS[:, :, T0:T0 + 512].rearrange("k p t -> p k t"), [bf("XS")], [xTb])
            if DBG_STOP == 'X':
                return
            rms_rstd(xT[:], xTb, 512, ones_b[:])
            make_h(xT, xTb, 512, l, [(0, 512, ci)])
            dma_in(CC[:], c_cc[T0:T0 + 512, :].rearrange("(i p) d -> p i d", p=128), [], [bf("CC")])
            dma_in(SSn[:], c_ss[T0:T0 + 512, :].rearrange("(i p) d -> p i d", p=128), [], [bf("SS")])
            if DBG_STOP == 'H':
                return
            qk_pending = []
            for cb in range(6):
                isk = cb >= 3
                g = cb % 3
                W, Wb = wblock(WIN, l, 0, 8, OQ + cb * 512)
                for ti in range(4):
                    pi = ti % 4

                    def mm(e, W=W, ti=ti, pi=pi):
                        ins = None
                        for kc in range(8):
                            ins = e.matmul(PS[pi][:, :], lhsT=hT[:, kc, ti * 128:(ti + 1) * 128], rhs=W[:, kc, :],
                                           start=(kc == 0), stop=(kc == 7))
                        return ins
                    pe(mm, [Wb, bf("hT")], [PSB[pi]])
                    m1, m1b, m2, m2b, p3, m13, m23, ccb, ss1, ss2 = rope_block(PS[pi][:, :], CC[:, ti, :], SSn[:, ti, :], 128, isk)
                    dve(lambda e, m13=m13, p3=p3, ccb=ccb: e.tensor_tensor(out=m13, in0=p3, in1=ccb, op=ALU.mult),
                        [PSB[pi], bf("CC")], [m1b])
                    dve(lambda e, m23=m23, p3=p3, ss1=ss1: e.tensor_tensor(out=m23[:, :, 0:32], in0=p3[:, :, 32:64], in1=ss1, op=ALU.mult),
                        [PSB[pi], bf("SS")], [m2b])
                    dve(lambda e, m23=m23, p3=p3, ss2=ss2: e.tensor_tensor(out=m23[:, :, 32:64], in0=p3[:, :, 0:32], in1=ss2, op=ALU.mult),
                        [PSB[pi], bf("SS")], [m2b])
                    ob, obb = tb16()
                    if isk:
                        pool(lambda e, m1=m1, m2=m2: e.tensor_tensor(out=m1[:], in0=m1[:], in1=m2[:], op=ALU.add), [m1b, m2b], [m1b])
                        act(lambda e, ob=ob, m1=m1: e.activation(out=ob[:], in_=m1[:], func=AF.Copy), [m1b], [obb])
                        keep = DILS[g][0]
                        t_lo = T0 + ti * 128
                        if t_lo >= S - keep:
                            r0 = t_lo - (S - keep)
                            dma_g(kvp[g][l, b, r0:r0 + 128, 0, :], m1[:], [m1b], [])
                    else:
                        pool(lambda e, m1=m1, m2=m2, ob=ob: e.tensor_tensor(out=ob[:], in0=m1[:], in1=m2[:], op=ALU.add), [m1b, m2b], [obb])
                    pj = 4 + (ti % 2)

                    def tr_and_evac(ob=ob, obb=obb, pj=pj, isk=isk, g=g, ti=ti):
                        def tr(e):
                            ins = None
                            pv = PS[pj][:, :].bitcast(BF16)
                            for c in range(4):
                                ins = e.transpose(out=pv[:, c * 128:(c + 1) * 128], in_=ob[:, c * 128:(c + 1) * 128], identity=ident_b[:])
                            return ins
                        pe(tr, [obb, bf("ident_b")], [PSB[pj]])
                        dst = KT if isk else QT
                        dstb = bf("KT") if isk else bf("QT")
                        act(lambda e: e.activation(
                            out=dst[:, g * 4:g * 4 + 4, ti * 128:(ti + 1) * 128],
                            in_=PS[pj][:, :].bitcast(BF16)[:, 0:512].rearrange("p (c t) -> p c t", t=128), func=AF.Copy),
                            [PSB[pj]], [dstb])
                    qk_pending.append(tr_and_evac)
                    while len(qk_pending) > 2:
                        qk_pending.pop(0)()
            while qk_pending:
                qk_pending.pop(0)()
            dma_g(KH[:, :, T0:T0 + 512].rearrange("c p t -> p c t"), KT[:], [bf("KT")], [bf("KH")])
            if DBG_STOP == 'QK':
                return
            for g in range(3):
                W, Wb = wblock(WIN, l, 0, 8, OVAL + g * 512)
                for ti in range(4):
                    pi = ti % 4

                    def mm(e, W=W, ti=ti, pi=pi):
                        ins = None
                        for kc in range(8):
                            ins = e.matmul(PS[pi][:, :], lhsT=hT[:, kc, ti * 128:(ti + 1) * 128], rhs=W[:, kc, :],
                                           start=(kc == 0), stop=(kc == 7))
                        return ins
                    pe(mm, [Wb, bf("hT")], [PSB[pi]])
                    vf, vfb = t32()
                    act(lambda e, vf=vf, pi=pi: e.activation(out=vf[:], in_=PS[pi][:, :], func=AF.Copy), [PSB[pi]], [vfb])
                    t_lo = T0 + ti * 128
                    keep = DILS[g][0]
                    if t_lo >= S - keep:
                        r0 = t_lo - (S - keep)
                        dma_g(kvp[g][l, b, r0:r0 + 128, 1, :], vf[:], [vfb], [])
                    vi = vsti[0] % 2
                    vsti[0] += 1
                    vst = VST[vi]
                    vdst = bass.AP(tensor=vst[:].tensor, offset=vst[:].offset, ap=[list(vst[:].ap[0]), [192, 4], [128, 2], [1, 64]])
                    dve(lambda e, vdst=vdst, vf=vf: e.tensor_copy(out=vdst, in_=vf[:, :].rearrange("p (c h d) -> p c h d", h=2, d=64)),
                        [vfb], [bf(f"VST{vi}")])
                    dma_g(VH[t_lo:t_lo + 128, g, :, :], vst[:], [bf(f"VST{vi}")], [bf("VH")])
            if DBG_STOP == 'V':
                return
            lo0 = max(0, T0 - 128)
            lo1 = max(0, T0 - 512)

            def att_load_k(c):
                dma_in(KHt[0][:, 0:T0 + 512 - lo0], KH[0 * 4 + c, :, lo0:T0 + 512], [bf("KH")], [bf("KH0")])
                dma_in(KHt[1][:, 0:T0 + 512 - lo1], KH[1 * 4 + c, :, lo1:T0 + 512], [bf("KH")], [bf("KH1")])
                dma_in(KHt[2][:, 0:T0 + 512], KH[2 * 4 + c, :, 0:T0 + 512], [bf("KH")], [bf("KH2")])

            def att_load_v(c):
                VA = VAs[c % 2]
                vab = bf(f"VA{c % 2}")
                blk0 = max(0, tg * 4 - 1)
                nb = tg * 4 + 4 - blk0
                kb0 = blk0 - (tg * 4 - 1)
                src = bass.AP(tensor=VH.tensor, offset=VH[blk0 * 128, 0, c, 0].offset, ap=[[2304, 128], [128 * 2304, nb], [1, 192]])
                dma_in(VA[:, kb0:kb0 + nb, :], src, [bf("VH")], [vab])
                for bi in ((0, 1) if tg > 0 else (1,)):
                    src = bass.AP(tensor=VH.tensor, offset=VH[(tg - 1 + bi) * 512, 1, c, 0].offset, ap=[[4 * 2304, 128], [2304, 4], [1, 192]])
                    dma_in(VA[:, 5 + 4 * bi:9 + 4 * bi, :], src, [bf("VH")], [vab])
                np_ = 32 * (tg + 1)
                src = bass.AP(tensor=VH.tensor, offset=VH[0, 2, c, 0].offset, ap=[[16 * 2304, np_], [2304, 16], [1, 192]])
                dma_in(VA[0:np_, 13:29, :], src, [bf("VH")], [vab])

            sbanks = [6, 7, 2, 3]
            ucnt = [0]
            pending = []

            def flush(keep):
                while len(pending) > keep:
                    pending.pop(0)()

            def att_pair(c):
                VA = VAs[c % 2]
                vab = bf(f"VA{c % 2}")
                for hh in range(2):
                    rows = slice(hh * 64, hh * 64 + 64)
                    acc = PS[4 + hh]
                    accb = PSB[4 + hh]
                    vcol = slice(0, 128) if hh == 0 else slice(64, 192)
                    first = [True]

                    def pv_mm(e, slot, esrc, outcols, first=first, VA=VA, vcol=vcol):
                        ins = e.matmul(outcols, lhsT=VA[:, slot, vcol], rhs=esrc, start=first[0], stop=False, skip_group_check=True)
                        first[0] = False
                        return ins
                    for g in range(2):
                        for qp in range(2):
                            pi = sbanks[ucnt[0] % 4]
                            ucnt[0] += 1
                            units = []
                            for u in range(2):
                                qi = qp * 2 + u
                                if g == 0:
                                    qcols = QT[rows, 0 * 4 + c, qi * 128:(qi + 1) * 128]
                                    has_prev = (tg * 4 + qi) > 0
                                    kprev = KHt[0][rows, (T0 - lo0) + (qi - 1) * 128:(T0 - lo0) + qi * 128] if has_prev else None
                                    kcur = KHt[0][rows, (T0 - lo0) + qi * 128:(T0 - lo0) + (qi + 1) * 128]
                                    sprev, scur = qi, qi + 1
                                    ocols = acc[:, qi * 128:(qi + 1) * 128]
                                else:
                                    r = qi
                                    qcols = QT[rows, 1 * 4 + c, r:512:4]
                                    has_prev = tg > 0
                                    kprev = KHt[1][rows, (T0 - lo1) - 512 + r:(T0 - lo1):4] if has_prev else None
                                    kcur = KHt[1][rows, (T0 - lo1) + r:(T0 - lo1) + 512:4]
                                    sprev, scur = 5 + r, 9 + r
                                    ocols = acc[:, r:512:4]
                                units.append((qcols, has_prev, kprev, kcur, sprev, scur, ocols))

                            def smm(e, units=units, pi=pi):
                                ins = None
                                for u, (qcols, has_prev, kprev, kcur, sprev, scur, ocols) in enumerate(units):
                                    if has_prev:
                                        ins = e.matmul(PS[pi][:, u * 256:u * 256 + 128], lhsT=kprev, rhs=qcols, start=True, stop=True)
                                    ins = e.matmul(PS[pi][:, u * 256 + 128:u * 256 + 256], lhsT=kcur, rhs=qcols, start=True, stop=True)
                                return ins
                            pe(smm, [bf("QT"), bf(f"KH{g}")], [PSB[pi]])
                            c0 = 0 if units[0][1] else 128
                            E, Eb = tb16()
                            act(lambda e, E=E, pi=pi, c0=c0: e.activation(out=E[:, c0:512], in_=PS[pi][:, c0:512], func=AF.Exp, scale=0.125),
                                [PSB[pi]], [Eb])
                            dve(lambda e, E=E, c0=c0: e.tensor_tensor(out=E[:, c0:512], in0=E[:, c0:512], in1=MB[:, c0:512], op=ALU.mult),
                                [Eb, bf("MB")], [Eb])

                            def pmm(e, units=units, E=E, pv_mm=pv_mm):
                                ins = None
                                for u, (qcols, has_prev, kprev, kcur, sprev, scur, ocols) in enumerate(units):
                                    if has_prev:
                                        ins = pv_mm(e, sprev, E[:, u * 256:u * 256 + 128], ocols)
                                    ins = pv_mm(e, scur, E[:, u * 256 + 128:u * 256 + 256], ocols)
                                return ins
                            pending.append(lambda pmm=pmm, Eb=Eb, vab=vab, accb=accb: pe(pmm, [Eb, vab], [accb]))
                            flush(1)
                    pi = sbanks[ucnt[0] % 4]
                    ucnt[0] += 1

                    def smm2(e, pi=pi):
                        ins = None
                        for r in range(16):
                            ins = e.matmul(PS[pi][:, r * 32:(r + 1) * 32], lhsT=KHt[2][rows, r:2048:16],
                                           rhs=QT[rows, 2 * 4 + c, r:512:16], start=True, stop=True)
                        return ins
                    pe(smm2, [bf("QT"), bf("KH2")], [PSB[pi]])
                    E, Eb = tb16()
                    act(lambda e, E=E, pi=pi: e.activation(out=E[:], in_=PS[pi][:, :], func=AF.Exp, scale=0.125), [PSB[pi]], [Eb])
                    dve(lambda e, E=E: e.tensor_tensor(out=E[:], in0=E[:], in1=MG[:, tg, :], op=ALU.mult), [Eb, bf("MG")], [Eb])

                    def pmm2(e, E=E, pv_mm=pv_mm, acc=acc):
                        ins = None
                        for r in range(16):
                            ins = pv_mm(e, 13 + r, E[:, r * 32:(r + 1) * 32], acc[:, r:512:16])
                        return ins
                    pending.append(lambda pmm2=pmm2, Eb=Eb, vab=vab, accb=accb: pe(pmm2, [Eb, vab], [accb]))
                    urow = rows
                    zrow = slice(64, 128) if hh == 0 else slice(0, 64)

                    def norm(acc=acc, accb=accb, urow=urow, zrow=zrow, c=c):
                        rz, rzb = t32()
                        act(lambda e: e.activation(out=rz[urow, :], in_=acc[zrow, :], func=AF.Ln), [accb], [rzb])
                        act(lambda e: e.activation(out=rz[urow, :], in_=rz[urow, :], func=AF.Exp, scale=-1.0), [rzb], [rzb])
                        dve(lambda e: e.tensor_tensor(out=ybT[urow, c, :], in0=acc[urow, :], in1=rz[urow, :], op=ALU.mult),
                            [accb, rzb], [bf("ybT")])
                    pending.append(norm)

            branch_a1(l, 512, 4, hT, bf("hT"), None)
            att_load_v(0)
            att_load_k(0)
            branch_a2(l, 512, hT, bf("hT"), None)
            for c in range(4):
                if c + 1 < 4:
                    att_load_v(c + 1)
                att_pair(c)
                flush(0)
                if c + 1 < 4:
                    att_load_k(c + 1)
            if DBG_STOP == 'ATT':
                return
            merge_out(l, 512, hT, bf("hT"), xT, xTb, ybT, bf("ybT"), [(0, 512, ci)])
            if DBG_STOP == 'BM':
                return
            if l == 0:
                dma_g(XS[:, :, T0:T0 + 512].rearrange("k p t -> p k t"), xT[:], [xTb], [bf("XS")])
            else:
                final_out(xT, xTb, 512, lambda ti: y_p[b, T0 + ti * 128:T0 + (ti + 1) * 128, :], 4, 128)

        def final_out(xsrc, xb_, n, dst_of_tile, ntiles, npart):
            rms_rstd(xsrc[:, :, 0:n], xb_, n, ones_b[:])
            for kc in range(8):
                dve(lambda e, kc=kc: e.scalar_tensor_tensor(out=xsrc[:, kc, 0:n], in0=xsrc[:, kc, 0:n], scalar=SVT[:, 64 + kc:65 + kc],
                                                            in1=rstd[:, 0:n], op0=ALU.mult, op1=ALU.mult), [xb_, bf("rstd"), bf("SVT")], [xb_])
            for ti in range(ntiles):
                xl = XL[ti % 2]
                xlb = bf(f"XL{ti % 2}")
                for hf in range(2):
                    pi = 4 + hf

                    def tr(e, hf=hf, pi=pi, ti=ti):
                        ins = None
                        for k in range(4):
                            kc = hf * 4 + k
                            ins = e.transpose(out=PS[pi][0:npart, k * 128:(k + 1) * 128], in_=xsrc[:, kc, ti * npart:(ti + 1) * npart],
                                              identity=ident_f[:])
                        return ins
                    pe(tr, [xb_, bf("ident_f")], [PSB[pi]])
                    act(lambda e, hf=hf, pi=pi, xl=xl: e.activation(out=xl[0:npart, hf * 512:(hf + 1) * 512], in_=PS[pi][0:npart, :], func=AF.Copy),
                        [PSB[pi]], [xlb])
                dma_g(dst_of_tile(ti), xl[0:npart, :], [xlb], [])

        def branch_a_and_merge(l, n, ntile, hsrc, hb_, xsrc, xb_, ybsrc, ybb, cidx_of_col, sample):
            branch_a1(l, n, ntile, hsrc, hb_, sample)
            branch_a2(l, n, hsrc, hb_, sample)
            merge_out(l, n, hsrc, hb_, xsrc, xb_, ybsrc, ybb, cidx_of_col)

        def branch_a1(l, n, ntile, hsrc, hb_, sample):
            npart = 128 if sample is None else n
            Ws = [wblock(WIN, l, 0, 8, OV + hf * 512) for hf in range(2)]
            for t0 in range(0, ntile, 2):
                tiles = [t for t in (t0, t0 + 1) if t < ntile]
                k = len(tiles)
                for ti in tiles:
                    gv = XL[ti % 2]
                    gvb = bf(f"XL{ti % 2}")
                    for hf in range(2):
                        W, Wb = Ws[hf]
                        pi = (ti % 2) * 2 + hf

                        def mm(e, W=W, ti=ti, pi=pi):
                            ins = None
                            for kc in range(8):
                                ins = e.matmul(PS[pi][0:npart, :], lhsT=hsrc[:, kc, ti * npart:(ti + 1) * npart], rhs=W[:, kc, :],
                                               start=(kc == 0), stop=(kc == 7))
                            return ins
                        pe(mm, [Wb, hb_], [PSB[pi]])
                        act(lambda e, gv=gv, hf=hf, pi=pi: e.activation(out=gv[0:npart, hf * 512:(hf + 1) * 512], in_=PS[pi][0:npart, :],
                                                                       func=AF.Gelu_apprx_tanh), [PSB[pi]], [gvb])
                        dve(lambda e, gv=gv, hf=hf, ti=ti: e.bn_stats(out=st6[0:npart, ti, hf, :], in_=gv[0:npart, hf * 512:(hf + 1) * 512]),
                            [gvb], [bf("st6")])
                    dve(lambda e, ti=ti: e.bn_aggr(out=mv[0:npart, ti, :], in_=st6[0:npart, ti, :, :]), [bf("st6")], [bf("mv")])
                dve(lambda e: e.tensor_scalar(out=lnr[0:npart, t0:t0 + k], in0=mv[0:npart, t0:t0 + k, 1], scalar1=1e-5, scalar2=None, op0=ALU.add),
                    [bf("mv")], [bf("lnr")])
                act(lambda e: e.activation(out=lnr[0:npart, t0:t0 + k], in_=lnr[0:npart, t0:t0 + k], func=AF.Ln), [bf("lnr")], [bf("lnr")])
                act(lambda e: e.activation(out=lnr[0:npart, t0:t0 + k], in_=lnr[0:npart, t0:t0 + k], func=AF.Exp, scale=-0.5),
                    [bf("lnr")], [bf("lnr")])
                for ti in tiles:
                    gv = XL[ti % 2]
                    gvb = bf(f"XL{ti % 2}")
                    dve(lambda e, gv=gv, ti=ti: e.tensor_scalar(out=gv[0:npart, :], in0=gv[0:npart, :], scalar1=mv[0:npart, ti, 0:1],
                                                                scalar2=lnr[0:npart, ti:ti + 1], op0=ALU.subtract, op1=ALU.mult),
                        [gvb, bf("mv"), bf("lnr")], [gvb])
                    pool(lambda e, gv=gv: e.tensor_tensor(out=gv[0:npart, :], in0=gv[0:npart, :], in1=LG[0:npart, :], op=ALU.mult), [gvb, bf("LG")], [gvb])
                    if sample is None:
                        pool(lambda e, gv=gv, ti=ti: e.tensor_tensor(out=vn[:, ti, :], in0=gv[:, :], in1=LB[:, :], op=ALU.add), [gvb, bf("LB")], [bf("vn")])
                    else:
                        pool(lambda e, gv=gv: e.tensor_tensor(out=gv[0:npart, :], in0=gv[0:npart, :], in1=LB[0:npart, :], op=ALU.add), [gvb, bf("LB")], [gvb])
                        dma_g(gmv[l, :, :], gv[0:npart, :], [gvb], [])
                        act(lambda e, gv=gv: e.activation(out=vn[0:npart, 0, :], in_=gv[0:npart, :], func=AF.Copy), [gvb], [bf("vn")])

        def branch_a2(l, n, hsrc, hb_, sample):
            yab = bf("BIG")
            Wu = None
            for g in range(8):
                if g % 4 == 0:
                    Wu = wblock(WIN, l, 0, 8, OU + (g // 4) * 512)
                    Wz = wblock(WIN, l, 0, 8, OZA + (g // 4) * 512)
                ch = g % 4
                bu, bz, bs_ = (0, 1, 2) if g % 2 == 0 else (3, 6, 7)

                def mmf(e, Wt, pi, ch=ch):
                    ins = None
                    for kc in range(8):
                        ins = e.matmul(PS[pi][:, 0:n], lhsT=Wt[:, kc, ch * 128:(ch + 1) * 128], rhs=hsrc[:, kc, 0:n],
                                       start=(kc == 0), stop=(kc == 7))
                    return ins
                pe(lambda e, W=Wu[0]: mmf(e, W, bu), [Wu[1], hb_], [PSB[bu]])
                pe(lambda e, W=Wz[0]: mmf(e, W, bz), [Wz[1], hb_], [PSB[bz]])

                def spm(e, g=g):
                    ins = None
                    if sample is None:
                        for ti in range(4):
                            ins = e.matmul(PS[bs_][:, ti * 128:(ti + 1) * 128], lhsT=vn[:, ti, g * 128:(g + 1) * 128], rhs=wmT[:, l, g, :],
                                           start=True, stop=True)
                    else:
                        ins = e.matmul(PS[bs_][:, 0:n], lhsT=vn[0:n, 0, g * 128:(g + 1) * 128], rhs=WS[0:n, g, 0:n], start=True, stop=True)
                    return ins
                pe(spm, [bf("vn"), bf("wmT"), bf("WS")], [PSB[bs_]])
                gu, gub = t32()
                tz, tzb = t32()
                spb, spbb = t32()
                act(lambda e, gu=gu: e.activation(out=gu[:, 0:n], in_=PS[bu][:, 0:n], func=AF.Gelu_apprx_tanh), [PSB[bu]], [gub])
                act(lambda e, tz=tz: e.activation(out=tz[:, 0:n], in_=PS[bz][:, 0:n], func=AF.Tanh, scale=0.5), [PSB[bz]], [tzb])
                dve(lambda e, tz=tz: e.scalar_tensor_tensor(out=tz[:, 0:n], in0=tz[:, 0:n], scalar=1.0, in1=PS[bz][:, 0:n], op0=ALU.add, op1=ALU.mult),
                    [tzb, PSB[bz]], [tzb])
                if sample is None:
                    bsv = bass.AP(tensor=BSR[:].tensor, offset=BSR[0, g * 128].offset, ap=[list(BSR[:].ap[0]), [0, 4], [1, 128]])
                    dve(lambda e, spb=spb, bsv=bsv: e.tensor_tensor(out=spb[:, :].rearrange("p (a t) -> p a t", t=128),
                                                                    in0=PS[bs_][:, :].rearrange("p (a t) -> p a t", t=128), in1=bsv, op=ALU.add),
                        [PSB[bs_], bf("BSR")], [spbb])
                else:
                    bsv = bass.AP(tensor=BSR[:].tensor, offset=BSR[0, g * 128].offset, ap=[list(BSR[:].ap[0]), [0, n // T], [1, T]])
                    dve(lambda e, spb=spb, bsv=bsv: e.tensor_tensor(out=spb[:, 0:n].rearrange("p (a t) -> p a t", t=T),
                                                                    in0=PS[bs_][:, 0:n].rearrange("p (a t) -> p a t", t=T), in1=bsv, op=ALU.add),
                        [PSB[bs_], bf("BSR")], [spbb])
                dve(lambda e, gu=gu, spb=spb: e.tensor_tensor(out=gu[:, 0:n], in0=gu[:, 0:n], in1=spb[:, 0:n], op=ALU.mult), [gub, spbb], [gub])
                dve(lambda e, gu=gu, tz=tz, g=g: e.tensor_tensor(out=BIG[:, g, 0:n], in0=gu[:, 0:n], in1=tz[:, 0:n], op=ALU.mult), [gub, tzb], [yab])

        def merge_out(l, n, hsrc, hb_, xsrc, xb_, ybsrc, ybb, cidx_of_col):
            yab = bf("BIG")
            Wzb = wblock(WIN, l, 0, 8, OZB)
            for c in range(4):
                bq = c % 2
                pe(lambda e, c=c, bq=bq: mmf_generic(e, Wzb[0], bq, c, hsrc, n, 8), [Wzb[1], hb_], [PSB[bq]])
                tz, tzb = t32()
                act(lambda e, tz=tz, bq=bq: e.activation(out=tz[:, 0:n], in_=PS[bq][:, 0:n], func=AF.Tanh, scale=0.5), [PSB[bq]], [tzb])
                dve(lambda e, tz=tz, bq=bq: e.scalar_tensor_tensor(out=tz[:, 0:n], in0=tz[:, 0:n], scalar=1.0, in1=PS[bq][:, 0:n], op0=ALU.add, op1=ALU.mult),
                    [tzb, PSB[bq]], [tzb])
                dve(lambda e, tz=tz, c=c: e.tensor_tensor(out=ybsrc[:, c, 0:n], in0=ybsrc[:, c, 0:n], in1=tz[:, 0:n], op=ALU.mult), [tzb, ybb], [ybb])
            for oc in range(8):
                if oc % 4 == 0:
                    Wga = wblock(WIN, l, 0, 8, OGA + (oc // 4) * 512)
                    Wgb = wblock(WIN, l, 0, 8, OGB + (oc // 4) * 512)
                ch = oc % 4
                Wg1 = wblock(WGM, l, 0, 8, oc * 128, 128)
                Wa1 = wblock(WATT, l, 0, 4, oc * 128, 128)
                b0 = 0 if oc % 2 == 0 else 4
                pe(lambda e, ch=ch, W=Wga[0], b0=b0: mmf_generic(e, W, b0, ch, hsrc, n, 8), [Wga[1], hb_], [PSB[b0]])
                pe(lambda e, ch=ch, W=Wgb[0], b0=b0: mmf_generic(e, W, b0 + 1, ch, hsrc, n, 8), [Wgb[1], hb_], [PSB[b0 + 1]])
                pe(lambda e, W=Wg1[0], b0=b0: mmf_generic(e, W, b0 + 2, 0, BIG, n, 8), [Wg1[1], yab], [PSB[b0 + 2]])
                pe(lambda e, W=Wa1[0], b0=b0: mmf_generic(e, W, b0 + 3, 0, ybsrc, n, 4), [Wa1[1], ybb], [PSB[b0 + 3]])
                ta, tab_ = t32()
                tb_, tbb_ = t32()
                act(lambda e, ta=ta, b0=b0: e.activation(out=ta[:, 0:n], in_=PS[b0][:, 0:n], func=AF.Tanh, scale=0.5), [PSB[b0]], [tab_])
                act(lambda e, tb_=tb_, b0=b0: e.activation(out=tb_[:, 0:n], in_=PS[b0 + 1][:, 0:n], func=AF.Tanh, scale=0.5), [PSB[b0 + 1]], [tbb_])
                dve(lambda e, ta=ta, b0=b0: e.scalar_tensor_tensor(out=ta[:, 0:n], in0=ta[:, 0:n], scalar=1.0, in1=PS[b0 + 2][:, 0:n], op0=ALU.add, op1=ALU.mult),
                    [tab_, PSB[b0 + 2]], [tab_])
                dve(lambda e, tb_=tb_, b0=b0: e.scalar_tensor_tensor(out=tb_[:, 0:n], in0=tb_[:, 0:n], scalar=1.0, in1=PS[b0 + 3][:, 0:n], op0=ALU.add, op1=ALU.mult),
                    [tbb_, PSB[b0 + 3]], [tbb_])
                dve(lambda e, ta=ta, tb_=tb_, oc=oc: e.tensor_tensor(out=MGT[:, oc, 0:n], in0=ta[:, 0:n], in1=tb_[:, 0:n], op=ALU.add),
                     [tab_, tbb_], [bf("MGT")])
            for oc in range(8):
                Wo1 = wblock(WO, l, 0, 8, oc * 128, 128)
                bo = oc % 4
                pe(lambda e, W=Wo1[0], bo=bo: mmf_generic(e, W, bo, 0, MGT, n, 8), [Wo1[1], bf("MGT")], [PSB[bo]])
                for (c0, c1, ci) in cidx_of_col:
                    dve(lambda e, oc=oc, c0=c0, c1=c1, ci=ci, bo=bo: e.scalar_tensor_tensor(
                        out=xsrc[:, oc, c0:c1], in0=PS[bo][:, c0:c1], scalar=G4[:, l, oc, ci:ci + 1], in1=xsrc[:, oc, c0:c1],
                        op0=ALU.mult, op1=ALU.add), [PSB[bo], bf("G4"), xb_], [xb_])

        def mmf_generic(e, Wt, pi, ch, src, n, nk):
            ins = None
            for kc in range(nk):
                ins = e.matmul(PS[pi][:, 0:n], lhsT=Wt[:, kc, ch * 128:(ch + 1) * 128], rhs=src[:, kc, 0:n],
                               start=(kc == 0), stop=(kc == nk - 1))
            return ins

        def sample_layer(l):
            n = TS
            xb_ = bf("xsT")
            if cur_l[0] != l:
                load_layer_vecs(l)
                cur_l[0] = l
            if l == 0:
                xl = XL[0]
                xlb = bf("XL0")
                dma_in(xl[0:n, :], x_s[:, :], [], [xlb])
                for hf in range(2):
                    pi = 4 + hf

                    def tr(e, hf=hf, pi=pi):
                        ins = None
                        for k in range(4):
                            kc = hf * 4 + k
                            ins = e.transpose(out=PS[pi][:, k * 128:k * 128 + n], in_=xl[0:n, kc * 128:(kc + 1) * 128], identity=ident_f[0:n, 0:n])
                        return ins
                    pe(tr, [xlb, bf("ident_f")], [PSB[pi]])
                    act(lambda e, hf=hf, pi=pi: e.activation(out=xsT[:, hf * 4:hf * 4 + 4, :],
                                                            in_=PS[pi][:, :].rearrange("p (k t) -> p k t", t=128)[:, :, 0:n], func=AF.Copy),
                        [PSB[pi]], [xb_])
            for bs in range(NBS):
                dma_in(WS[bs * T:(bs + 1) * T, :, bs * T:(bs + 1) * T], WMT[l, :, 0:T, 0:T].rearrange("g s t -> s g t"), [bf("WMT")], [bf("WS")])
            if DBG_SSTOP == 'SX':
                return
            cols = [(bs * T, (bs + 1) * T, NBP + bs) for bs in range(NBS)]
            rms_rstd(xsT[:], xb_, n, ones_b[:])
            make_h(xsT, xb_, n, l, cols)
            hb_ = bf("hsT")
            if DBG_SSTOP == 'SH':
                return
            for cb in range(6):
                isk = cb >= 3
                g = cb % 3
                W, Wb = wblock(WIN, l, 0, 8, OQ + cb * 512)

                def mm(e, W=W):
                    ins = None
                    for kc in range(8):
                        ins = e.matmul(PS[0][0:n, :], lhsT=hsT[:, kc, :], rhs=W[:, kc, :], start=(kc == 0), stop=(kc == 7))
                    return ins
                pe(mm, [Wb, hb_], [PSB[0]])
                m1, m1b, m2, m2b, p3, m13, m23, ccb, ss1, ss2 = rope_block(PS[0][0:n, :], CCs[0:n, :], SSs[0:n, :], n, isk)
                dve(lambda e, m13=m13, p3=p3, ccb=ccb: e.tensor_tensor(out=m13, in0=p3, in1=ccb, op=ALU.mult), [PSB[0], bf("CCs")], [m1b])
                dve(lambda e, m23=m23, p3=p3, ss1=ss1: e.tensor_tensor(out=m23[:, :, 0:32], in0=p3[:, :, 32:64], in1=ss1, op=ALU.mult), [PSB[0], bf("SSs")], [m2b])
                dve(lambda e, m23=m23, p3=p3, ss2=ss2: e.tensor_tensor(out=m23[:, :, 32:64], in0=p3[:, :, 0:32], in1=ss2, op=ALU.mult), [PSB[0], bf("SSs")], [m2b])
                ob, obb = tb16()
                dve(lambda e, m1=m1, m2=m2: e.tensor_tensor(out=m1[0:n, :], in0=m1[0:n, :], in1=m2[0:n, :], op=ALU.add), [m1b, m2b], [m1b])
                act(lambda e, ob=ob, m1=m1: e.activation(out=ob[0:n, :], in_=m1[0:n, :], func=AF.Copy), [m1b], [obb])
                if isk:
                    dma_g(kvs[g][l, :, 0, :], m1[0:n, :], [m1b], [])

                def tr(e, ob=ob):
                    ins = None
                    pv = PS[4][:, :].bitcast(BF16)
                    for c in range(4):
                        ins = e.transpose(out=pv[:, c * 128:c * 128 + n], in_=ob[0:n, c * 128:(c + 1) * 128], identity=ident_b[0:n, 0:n])
                    return ins
                pe(tr, [obb, bf("ident_b")], [PSB[4]])
                dst = KTs if isk else QTs
                dstb = bf("KTs") if isk else bf("QTs")
                act(lambda e, dst=dst, g=g: e.activation(out=dst[:, g * 4:g * 4 + 4, :],
                                                         in_=PS[4][:, :].bitcast(BF16)[:, 0:512].rearrange("p (c t) -> p c t", t=128)[:, :, 0:n],
                                                         func=AF.Copy), [PSB[4]], [dstb])
            if DBG_SSTOP == 'SQK':
                return
            for g in range(3):
                W, Wb = wblock(WIN, l, 0, 8, OVAL + g * 512)

                def mm(e, W=W):
                    ins = None
                    for kc in range(8):
                        ins = e.matmul(PS[0][0:n, :], lhsT=hsT[:, kc, :], rhs=W[:, kc, :], start=(kc == 0), stop=(kc == 7))
                    return ins
                pe(mm, [Wb, hb_], [PSB[0]])
                vf, vfb = t32()
                act(lambda e, vf=vf: e.activation(out=vf[0:n, :], in_=PS[0][0:n, :], func=AF.Copy), [PSB[0]], [vfb])
                dma_g(kvs[g][l, :, 1, :], vf[0:n, :], [vfb], [])
            if DBG_SSTOP == 'SV':
                return
            for bs in range(NBS):
                sample_attention(l, bs)
            if DBG_SSTOP == 'SATT':
                return
            branch_a_and_merge(l, n, 1, hsT, hb_, xsT, xb_, ybs, bf("ybs"), cols, True)
            if l == DEPTH - 1:
                final_out(xsT, xb_, n, lambda ti: y_s[:, :], 1, n)


        def sample_attention(l, bs):
            n = TS
            hb_ = bf("hsT")
            for g in range(3):
                W, Wb = wblock(WIN, l, 0, 8, OVAL + g * 512)

                def mm2(e, W=W):
                    ins = None
                    for kc in range(8):
                        ins = e.matmul(PS[1][0:T, :], lhsT=hsT[:, kc, bs * T:(bs + 1) * T], rhs=W[:, kc, :], start=(kc == 0), stop=(kc == 7))
                    return ins
                pe(mm2, [Wb, hb_], [PSB[1]])
                vdst = bass.AP(tensor=VAnB1[:].tensor, offset=VAnB1[0, g, 0, 0].offset, ap=[list(VAnB1[:].ap[0]), [192, 4], [128, 2], [1, 64]])
                act(lambda e, vdst=vdst: e.activation(out=vdst, in_=PS[1][0:T, :].rearrange("p (c h d) -> p c h d", h=2, d=64), func=AF.Copy),
                    [PSB[1]], [bf("VAnB")])
            acc = PS[5]
            accb = PSB[5]
            first = [True]
            tiles = [(0, 0, 0)] + [(1, r, 1 + r) for r in range(4)] + [(2, r, 5 + r) for r in range(8)]
            spend = []
            for idx, (g, r, mi) in enumerate(tiles):
                par = idx % 2
                d = DILS[g][1]
                ct = CT[par]
                ctb = bf(f"CT{par}")
                ktc = (KTc, KTcB)[par]
                vac = (VAc, VAcB)[par]
                es = (Es, EsB)[par]
                ktb, vab_, esb = bf(f"KTc{par}"), bf(f"VAc{par}"), bf(f"Es{par}")
                pt = 2 + par
                be, bo = (7, 6) if par == 0 else (1, 0)
                src = bass.AP(tensor=ck[g].tensor, offset=ck[g][l, bs, r, 0, 0].offset, ap=[[d * 1024, 128], [1, 1024]])
                dma_in(ct[:], src, [], [ctb])

                kb, kbb = tb16()
                dve(lambda e, kb=kb, ct=ct: e.tensor_copy(out=kb[:], in_=ct[:, 0:512]), [ctb], [kbb])

                def tr(e, kb=kb, pt=pt):
                    ins = None
                    pv = PS[pt][:, :].bitcast(BF16)
                    for c in range(4):
                        ins = e.transpose(out=pv[:, c * 128:(c + 1) * 128], in_=kb[:, c * 128:(c + 1) * 128], identity=ident_b[:])
                    return ins
                pe(tr, [kbb, bf("ident_b")], [PSB[pt]])
                act(lambda e, ktc=ktc, pt=pt: e.activation(out=ktc[:], in_=PS[pt][:, :].bitcast(BF16)[:, 0:512].rearrange("p (c t) -> p c t", t=128),
                                                           func=AF.Copy), [PSB[pt]], [ktb])
                vdst = bass.AP(tensor=vac[:].tensor, offset=vac[0, 0, 0].offset, ap=[list(vac[:].ap[0]), [192, 4], [128, 2], [1, 64]])
                dve(lambda e, ct=ct, vdst=vdst: e.tensor_copy(out=vdst, in_=ct[:, 512:1024].rearrange("p (c h d) -> p c h d", h=2, d=64)),
                    [ctb], [vab_])

                def rest(g=g, mi=mi, ktc=ktc, vac=vac, es=es, ktb=ktb, vab_=vab_, esb=esb, be=be, bo=bo):
                    def smm(e):
                        ins = None
                        for h in (0, 2, 4, 6, 1, 3, 5, 7):
                            rows = slice((h % 2) * 64, (h % 2) * 64 + 64)
                            ins = e.matmul(PS[be if h % 2 == 0 else bo][:, (h // 2) * T:(h // 2 + 1) * T], lhsT=ktc[rows, h // 2, :],
                                           rhs=QTs[rows, g * 4 + h // 2, bs * T:(bs + 1) * T], start=True, stop=True)
                        return ins
                    pe(smm, [ktb, bf("QTs")], [PSB[be], PSB[bo]])
                    es4 = es[:, :].rearrange("p (c h t) -> p c h t", h=2, t=T)
                    for hh in range(2):
                        bk = be if hh == 0 else bo
                        act(lambda e, hh=hh, bk=bk: e.activation(out=es4[:, :, hh, :], in_=PS[bk][:, 0:4 * T].rearrange("p (c t) -> p c t", t=T),
                                                                 func=AF.Exp, scale=0.125), [PSB[bk]], [esb])
                    msk = bass.AP(tensor=MS[:].tensor, offset=MS[0, mi, 0].offset, ap=[list(MS[:].ap[0]), [0, 8], [1, T]])
                    dve(lambda e: e.tensor_tensor(out=es[:, :].rearrange("p (h t) -> p h t", t=T), in0=es[:, :].rearrange("p (h t) -> p h t", t=T),
                                                  in1=msk, op=ALU.mult), [esb, bf("MS")], [esb])

                    def pmm(e):
                        ins = None
                        for h in range(8):
                            vcol = slice(0, 128) if h % 2 == 0 else slice(64, 192)
                            ins = e.matmul(acc[:, h * T:(h + 1) * T], lhsT=vac[:, h // 2, vcol], rhs=es[:, h * T:(h + 1) * T], start=first[0], stop=False,
                                           skip_group_check=True)
                            first[0] = False
                        return ins
                    pe(pmm, [vab_, esb], [accb])
                spend.append(rest)
                while len(spend) > 1:
                    spend.pop(0)()
            while spend:
                spend.pop(0)()
            if DBG_SA <= 3:
                return
            for g in range(3):
                def smm(e, g=g):
                    ins = None
                    for h in (0, 2, 4, 6, 1, 3, 5, 7):
                        rows = slice((h % 2) * 64, (h % 2) * 64 + 64)
                        ins = e.matmul(PS[7 - (h % 2)][0:T, (h // 2) * T:(h // 2 + 1) * T], lhsT=KTs[rows, g * 4 + h // 2, bs * T:(bs + 1) * T],
                                       rhs=QTs[rows, g * 4 + h // 2, bs * T:(bs + 1) * T], start=True, stop=True)
                    return ins
                pe(smm, [bf("KTs"), bf("QTs")], [PSB[7], PSB[6]])
                if DBG_SA == 35:
                    continue
                Esn4 = Esn[:, :].rearrange("p (c h t) -> p c h t", h=2, t=T)
                for hh in range(2):
                    act(lambda e, hh=hh: e.activation(out=Esn4[:, :, hh, :], in_=PS[7 - hh][0:T, 0:4 * T].rearrange("p (c t) -> p c t", t=T),
                                                      func=AF.Exp, scale=0.125), [PSB[7 - hh]], [bf("Esn")])
                msk = bass.AP(tensor=MN[:].tensor, offset=MN[0, g, 0].offset, ap=[list(MN[:].ap[0]), [0, 8], [1, T]])
                dve(lambda e, msk=msk: e.tensor_tensor(out=Esn[:, :].rearrange("p (h t) -> p h t", t=T), in0=Esn[:, :].rearrange("p (h t) -> p h t", t=T),
                                                       in1=msk, op=ALU.mult), [bf("Esn"), bf("MN")], [bf("Esn")])
                if DBG_SA == 36:
                    continue

                def pmm(e, g=g):
                    ins = None
                    for h in range(8):
                        vcol = slice(0, 128) if h % 2 == 0 else slice(64, 192)
                        ins = e.matmul(acc[:, h * T:(h + 1) * T], lhsT=VAnB[bs][:, g, h // 2, vcol], rhs=Esn[:, h * T:(h + 1) * T], start=False, stop=False,
                                       skip_group_check=True)
                    return ins
                pe(pmm, [bf("VAnB"), bf("Esn")], [accb])
            if DBG_SA <= 4 or DBG_SA in (35, 36):
                return
            rz, rzb = t32()
            for hh in range(2):
                urow = slice(hh * 64, hh * 64 + 64)
                zrow = slice(64, 128) if hh == 0 else slice(0, 64)
                a3 = acc[:, 0:64].rearrange("p (c h t) -> p c h t", h=2, t=T)
                r3 = rz[:, 0:32].rearrange("p (c t) -> p c t", t=T)
                act(lambda e, urow=urow, zrow=zrow, hh=hh: e.activation(out=r3[urow], in_=a3[zrow, :, hh, :], func=AF.Ln), [accb], [rzb])
                act(lambda e, urow=urow: e.activation(out=r3[urow], in_=r3[urow], func=AF.Exp, scale=-1.0), [rzb], [rzb])
                dve(lambda e, urow=urow, hh=hh: e.tensor_tensor(out=ybs[urow, :, bs * T:(bs + 1) * T], in0=a3[urow, :, hh, :], in1=r3[urow], op=ALU.mult),
                    [accb, rzb], [bf("ybs")])

        ng = 0
        for b in range(NBP):
            for l in range(DEPTH):
                for tg in range(4):
                    if ng < DBG_NG:
                        prompt_group(b, l, tg)
                    ng += 1
        if NBS > 0 and DBG_SAMPLE:
            for l in range(DEPTH):
                sample_layer(l)
        sc.finish()

    return nc


_NC_CACHE = {}


def _run(inputs, NBP, NBS, ncores):
    key = (NBP, NBS)
    if key not in _NC_CACHE:
        _NC_CACHE[key] = build(NBP, NBS)
    nc = _NC_CACHE[key]
    f = lambda a: np.ascontiguousarray(np.asarray(a, dtype=np.float32))
    cst = _consts()
    shared = {k: f(inputs[k]) for k in ("w_ada", "b_ada", "norm_g", "w_in", "gm_ln_g", "gm_ln_b", "gm_ws", "gm_bs",
                                        "w_gm_out", "w_att_out", "w_o", "final_g")}
    shared.update(cst)
    xp, xs = f(inputs["x_prompt"]), f(inputs["x_sample"])
    cp, cs = f(inputs["c_prompt"]), f(inputs["c_sample"])
    caches = [f(inputs["cache_kv_w128"]), f(inputs["cache_kv_w512"]), f(inputs["cache_kv_w2048"])]
    in_maps = []
    for i in range(ncores):
        m = dict(shared)
        m["x_p"] = np.ascontiguousarray(xp[i * NBP:(i + 1) * NBP])
        m["x_s"] = np.ascontiguousarray(xs[i * NBS:(i + 1) * NBS].reshape(NBS * T, D))
        m["c_all"] = np.ascontiguousarray(np.concatenate([cp[i * NBP:(i + 1) * NBP], cs[i * NBS:(i + 1) * NBS]], 0))
        for g in range(3):
            cg = caches[g][:, i * NBS:(i + 1) * NBS]
            m[f"ck{g}"] = np.ascontiguousarray(cg.reshape(DEPTH, NBS, cg.shape[2], 2, 512))
        in_maps.append(m)
    res = run_bass_kernel_spmd(nc, in_maps, core_ids=list(range(ncores)))
    R = res.results
    y_p = np.concatenate([r["y_p"] for r in R], 0)
    y_s = np.concatenate([r["y_s"].reshape(NBS, T, D) for r in R], 0)
    outs = [y_p, y_s]
    for g in range(3):
        keep = DILS[g][0]
        outs.append(np.concatenate([r[f"kvp{g}"].reshape(DEPTH, NBP, keep, 2, 8, 64) for r in R], 1))
    for g in range(3):
        outs.append(np.concatenate([r[f"kvs{g}"].reshape(DEPTH, NBS, T, 2, 8, 64) for r in R], 1))
    outs.append(np.concatenate([r["gmv"].reshape(DEPTH, NBS, T, D) for r in R], 1))
    return tuple(np.ascontiguousarray(o.astype(np.float32)) for o in outs)


def kernel(**inputs):
    return _run(inputs, 2, 4, NCORES)
```

```python
import numpy as np
from contextlib import ExitStack
import concourse.bass as bass
import concourse.mybir as mybir
from concourse.bass_utils import run_bass_kernel_spmd

F32, BF16 = mybir.dt.float32, mybir.dt.bfloat16
AF = mybir.ActivationFunctionType
ALU = mybir.AluOpType

D = 1024
S = 2048
DEPTH = 2
T = 8
PAST = 16384
NCORES = 8
INW = 10240
OU, OV, OZA, OQ, OK_, OVAL, OZB, OGA, OGB = 0, 1024, 2048, 3072, 4608, 6144, 7680, 8192, 9216
DILS = ((128, 1), (512, 4), (2048, 16))
ENG = ("pe", "act", "dve", "pool", "sp")
EPOCH = 30000
NDS = 24


class Buf:
    __slots__ = ("w", "r")

    def __init__(self):
        self.w = None
        self.r = {}


class Sched:
    def __init__(self, nc, es):
        self.nc = nc
        self.eh = {"pe": nc.tensor, "act": nc.scalar, "dve": nc.vector, "pool": nc.gpsimd, "sp": nc.sync}
        self.cnt = {e: 0 for e in ENG}
        self.sems = {e: [es.enter_context(nc.semaphore(f"s_{e}{i}")) for i in range(3)] for e in ENG}
        self.dsem = [es.enter_context(nc.semaphore(f"d{i}")) for i in range(2 * NDS)]
        self.dtgt = [0] * (2 * NDS)
        self.rr = {"sp": 0, "pool": 0}
        self.waited = {}

    def _waits(self, eng, deps):
        out = []
        for tok in deps:
            if tok[0] == "e":
                key = (eng, "e", tok[1])
                val = (tok[2], tok[3])
                sem = self.sems[tok[1]][tok[2]]
                v = tok[3]
            else:
                key = (eng, "d", tok[1])
                val = (0, tok[2])
                sem = self.dsem[tok[1]]
                v = tok[2]
            if self.waited.get(key, (-1, -1)) >= val:
                continue
            self.waited[key] = val
            out.append((sem, v))
        return out

    def op(self, eng, fn, reads=(), writes=(), dma=False):
        deps = []
        for b in reads:
            if b.w is not None:
                deps.append(b.w)
        for b in writes:
            if b.w is not None:
                deps.append(b.w)
            deps.extend(b.r.values())
        if dma:
            k = self.rr[eng] % NDS + (NDS if eng == "pool" else 0)
            self.rr[eng] += 1
            old = self.dtgt[k]
            self.dtgt[k] += 16
            if old > 0:
                deps.append(("d", k, old))
            tok = ("d", k, self.dtgt[k])
            sem, inc = self.dsem[k], 16
        else:
            c = self.cnt[eng]
            self.cnt[eng] += 1
            ep, v = divmod(c, EPOCH)
            tok = ("e", eng, ep, v + 1)
            sem, inc = self.sems[eng][ep], 1
        waits = self._waits(eng, deps)

        e = self.eh[eng]
        for s_, v_ in waits:
            e.wait_ge(s_, v_)
        fn(e).then_inc(sem, inc)
        rk = ("e", eng) if not dma else ("d", tok[1])
        for b in reads:
            b.r[rk] = tok
        for b in writes:
            b.w = tok
            b.r = {}
        return tok

    def finish(self):
        deps = [("d", k, self.dtgt[k]) for k in range(2 * NDS) if self.dtgt[k] > 0]
        for e in ENG:
            if e != "sp" and self.cnt[e] > 0:
                ep, v = divmod(self.cnt[e] - 1, EPOCH)
                deps.append(("e", e, ep, v + 1))
        waits = self._waits("sp", deps)

        for s_, v_ in waits:
            self.eh["sp"].wait_ge(s_, v_)


def _consts():
    half = 32
    inv = (10000.0 ** (-np.arange(half, dtype=np.float32) / np.float32(half))).astype(np.float32)

    def tab(pos):
        ang = (pos.astype(np.float32)[:, None] * inv[None, :]).astype(np.float32)
        c = np.cos(ang.astype(np.float64)).astype(np.float32)
        s = np.sin(ang.astype(np.float64)).astype(np.float32)
        return np.concatenate([c, c], 1), np.concatenate([-s, s], 1)

    cc, ss = tab(np.arange(S))
    ccs, sss = tab(PAST + np.arange(T))
    k = np.arange(128)[:, None]
    q = np.arange(128)[None, :]
    prev = (k >= q).astype(np.float32)
    cur = (k <= q).astype(np.float32)
    mb = np.concatenate([prev, cur, prev, cur], 1)
    mg = np.zeros((4, 128, 512), np.float32)
    for tq in range(4):
        m = (np.arange(128)[:, None] <= (32 * tq + np.arange(32))[None, :]).astype(np.float32)
        mg[tq] = np.tile(m, (1, 16))
    tri = (np.arange(128)[:, None] >= np.arange(128)[None, :]).astype(np.float32)
    tt = np.arange(T)[None, :]
    rows = np.arange(128)[:, None]
    ms = np.zeros((13, 128, T), np.float32)
    ms[0] = (rows >= tt)
    for r in range(4):
        ms[1 + r] = ((tt % 4) == r) & ((tt < 4) | (rows >= 1))
    for r in range(8):
        ms[5 + r] = (tt == r) & (rows >= 0)
    tk = np.arange(T)[:, None]
    mn = np.zeros((3, T, T), np.float32)
    mn[0] = (tk <= tt)
    mn[1] = (tk <= tt) & (((tt - tk) % 4) == 0)
    mn[2] = (tk == tt)
    return dict(
        c_cc=np.ascontiguousarray(cc), c_ss=np.ascontiguousarray(ss),
        c_ccs=np.ascontiguousarray(np.tile(ccs, (4, 1))), c_sss=np.ascontiguousarray(np.tile(sss, (4, 1))),
        c_mb=mb, c_mg=mg, c_tri=tri, c_id=np.eye(128, dtype=np.float32),
        c_ms=np.ascontiguousarray(ms), c_mn=np.ascontiguousarray(mn),
    )


DBG_STOP = ''
DBG_NG = 99
DBG_SAMPLE = True
DBG_SSTOP = ''
DBG_SA = 9


def build(NBP=2, NBS=4):
    nc = bass.Bass("TRN2", target_bir_lowering=False)
    NC6 = NBP + NBS
    TS = NBS * T

    def din(name, shape, dt=F32):
        return nc.dram_tensor(name, list(shape), dt, kind="ExternalInput").ap()

    def dout(name, shape):
        return nc.dram_tensor(name, list(shape), F32, kind="ExternalOutput").ap()

    def dint(name, shape, dt):
        return nc.dram_tensor(name, list(shape), dt, kind="Internal").ap()

    x_p = din("x_p", [NBP, S, D]); x_s = din("x_s", [TS, D])
    c_all = din("c_all", [NC6, D])
    ck = [din("ck0", [DEPTH, NBS, 128, 2, 512]), din("ck1", [DEPTH, NBS, 512, 2, 512]),
          din("ck2", [DEPTH, NBS, 2048, 2, 512])]
    w_ada = din("w_ada", [DEPTH, D, 3 * D]); b_ada = din("b_ada", [DEPTH, 3 * D])
    norm_g = din("norm_g", [DEPTH, D]); w_in = din("w_in", [DEPTH, D, INW])
    gm_ln_g = din("gm_ln_g", [DEPTH, D]); gm_ln_b = din("gm_ln_b", [DEPTH, D])
    gm_ws = din("gm_ws", [DEPTH, 8, 128, 128]); gm_bs = din("gm_bs", [DEPTH, 8, 128])
    w_gm = din("w_gm_out", [DEPTH, D, D]); w_att = din("w_att_out", [DEPTH, 512, D])
    w_o = din("w_o", [DEPTH, D, D]); final_g = din("final_g", [D])
    c_cc = din("c_cc", [S, 64]); c_ss = din("c_ss", [S, 64])
    c_ccs = din("c_ccs", [32, 64]); c_sss = din("c_sss", [32, 64])
    c_mb = din("c_mb", [128, 512]); c_mg = din("c_mg", [4, 128, 512]); c_tri = din("c_tri", [128, 128])
    c_id = din("c_id", [128, 128]); c_ms = din("c_ms", [13, 128, T]); c_mn = din("c_mn", [3, T, T])

    y_p = dout("y_p", [NBP, S, D]); y_s = dout("y_s", [TS, D])
    kvp = [dout("kvp0", [DEPTH, NBP, 128, 2, 512]), dout("kvp1", [DEPTH, NBP, 512, 2, 512]),
           dout("kvp2", [DEPTH, NBP, 2048, 2, 512])]
    kvs = [dout(f"kvs{g}", [DEPTH, TS, 2, 512]) for g in range(3)]
    gmv = dout("gmv", [DEPTH, TS, D])

    WIN = dint("WIN", [DEPTH, D, INW], BF16)
    WGM = dint("WGM", [DEPTH, 8, 128, 8, 128], BF16)
    WATT = dint("WATT", [DEPTH, 8, 128, 4, 128], BF16)
    WO = dint("WO", [DEPTH, 8, 128, 8, 128], BF16)
    XS = dint("XS", [8, 128, S], F32)
    KH = dint("KH", [12, 128, S], BF16)
    VH = dint("VH", [S, 3, 4, 192], BF16)
    WMT = dint("WMT", [DEPTH, 8, 128, 128], BF16)

    es = ExitStack()
    with es:
        def sb(name, shape, dt=F32):
            return es.enter_context(nc.sbuf_tensor(name, list(shape), dt))

        sc = Sched(nc, es)
        PS = [es.enter_context(nc.psum_tensor(f"ps{i}", [128, 512], F32)) for i in range(8)]
        PSB = [Buf() for _ in range(8)]

        ident_f = sb("ident_f", [128, 128]); ident_b = sb("ident_b", [128, 128], BF16)
        MB = sb("MB", [128, 512], BF16); MG = sb("MG", [128, 4, 512], BF16)
        TRI = sb("TRI", [128, 128])
        wmT = sb("wmT", [128, DEPTH, 8, 128], BF16)
        SV = sb("SV", [128, 128]); SVT = sb("SVT", [128, 72])
        modT = sb("modT", [128, DEPTH, 24, NC6])
        Am = sb("Am", [128, DEPTH, 8, NC6]); G4 = sb("G4", [128, DEPTH, 8, NC6])
        cT = sb("cT", [128, 8, NC6]); scT = sb("scT", [128, 8, NC6], BF16)
        LG = sb("LG", [128, D]); LB = sb("LB", [128, D]); BSR = sb("BSR", [128, D])
        xT = sb("xT", [128, 8, 512]); hT = sb("hT", [128, 8, 512], BF16)
        rstd = sb("rstd", [128, 512])
        NWB = 4
        WB = [sb(f"WB{i}", [128, 8, 512], BF16) for i in range(NWB)]
        NWS = 5
        WBS = [sb(f"WBS{i}", [128, 8, 128], BF16) for i in range(NWS)]
        NT = 8
        T32 = [sb(f"T32_{i}", [128, 512]) for i in range(NT)]
        NTB = 5
        TB = [sb(f"TB_{i}", [128, 512], BF16) for i in range(NTB)]
        XL = [sb(f"XL{i}", [128, D]) for i in range(2)]
        QT = sb("QT", [128, 12, 512], BF16); KT = sb("KT", [128, 12, 512], BF16)
        MGT = QT[:, 0:8, :]
        BIG = KT[:, 0:8, :]
        MBf = XL[0][:, 0:512]
        call_t = XL[1][0:NC6, :]
        KHt = [sb("KH0", [128, 640], BF16), sb("KH1", [128, 1024], BF16), sb("KH2", [128, 2048], BF16)]
        VAs = [sb(f"VA{i}", [128, 29, 192], BF16) for i in range(2)]
        vn = sb("vn", [128, 4, D], BF16)
        VST = [sb(f"VST{i}", [128, 4, 192], BF16) for i in range(2)]
        ybT = sb("ybT", [128, 4, 512], BF16)
        CC = sb("CC", [128, 4, 64]); SSn = sb("SSn", [128, 4, 64])
        st6 = sb("st6", [128, 4, 2, 6]); mv = sb("mv", [128, 4, 2]); lnr = sb("lnr", [128, 4])
        xsT = sb("xsT", [128, 8, TS]); hsT = sb("hsT", [128, 8, TS], BF16)
        QTs = sb("QTs", [128, 12, TS], BF16); KTs = sb("KTs", [128, 12, TS], BF16)
        CT = XL
        KTc = sb("KTc", [128, 4, 128], BF16); VAc = sb("VAc", [128, 4, 192], BF16)
        KTcB = sb("KTcB", [128, 4, 128], BF16); VAcB = sb("VAcB", [128, 4, 192], BF16); EsB = sb("EsB", [128, 64], BF16)
        MS = sb("MS", [128, 13, T], BF16); MN = sb("MN", [T, 3, T], BF16)
        MSf = sb("MSf", [128, 13, T]); MNf = sb("MNf", [T, 3, T])
        CCs = sb("CCs", [32, 64]); SSs = sb("SSs", [32, 64])
        WS = sb("WS", [32, 8, 32], BF16)
        ybs = sb("ybs", [128, 4, TS], BF16)
        Es = sb("Es", [128, 64], BF16); Esn = sb("Esn", [T, 64], BF16)

        B = {}

        ALIAS = {"MGT": "QT", "BIG": "KT", "MBf": "XL0", "call": "XL1", "CT0": "XL0", "CT1": "XL1"}

        def bf(name):
            name = ALIAS.get(name, name)
            if name not in B:
                B[name] = Buf()
            return B[name]

        t32b = [Buf() for _ in range(NT)]
        tbb = [Buf() for _ in range(NTB)]
        t32i = [0]
        tbi = [0]

        def t32():
            i = t32i[0] % NT
            t32i[0] += 1
            return T32[i], t32b[i]

        def tb16():
            i = tbi[0] % NTB
            tbi[0] += 1
            return TB[i], tbb[i]

        wbb = [Buf() for _ in range(NWB)]
        wbi = [0]
        wsb = [Buf() for _ in range(NWS)]
        wsi = [0]

        def pe(fn, r, w):
            return sc.op("pe", fn, r, w)

        def act(fn, r, w):
            return sc.op("act", fn, r, w)

        def dve(fn, r, w):
            return sc.op("dve", fn, r, w)

        def pool(fn, r, w):
            return sc.op("pool", fn, r, w)

        def dma_in(out, in_, r, w):
            return sc.op("sp", lambda e: e.dma_start(out=out, in_=in_), r, w, dma=True)

        def dma_g(out, in_, r, w):
            return sc.op("pool", lambda e: e.dma_start(out=out, in_=in_), r, w, dma=True)

        def bcast_rows(ap1d, n):
            return bass.AP(tensor=ap1d.tensor, offset=ap1d.offset, ap=[[0, 128], [1, n]])

        dma_in(ident_f[:], c_id[:, :], [], [bf("ident_f")])
        pool(lambda e: e.tensor_copy(out=ident_b[:], in_=ident_f[:]), [bf("ident_f")], [bf("ident_b")])
        dma_in(MBf, c_mb[:, :], [], [bf("MBf")])
        pool(lambda e: e.tensor_copy(out=MB[:], in_=MBf), [bf("MBf")], [bf("MB")])
        for tq in range(4):
            dma_in(MBf, c_mg[tq], [], [bf("MBf")])
            pool(lambda e, tq=tq: e.tensor_copy(out=MG[:, tq, :], in_=MBf), [bf("MBf")], [bf("MG")])
        dma_in(TRI[:], c_tri[:, :], [], [bf("TRI")])
        dma_in(MSf[:], c_ms.rearrange("k p t -> p k t"), [], [bf("MSf")])
        pool(lambda e: e.tensor_copy(out=MS[:], in_=MSf[:]), [bf("MSf")], [bf("MS")])
        dma_in(MNf[:], c_mn.rearrange("k p t -> p k t"), [], [bf("MNf")])
        pool(lambda e: e.tensor_copy(out=MN[:], in_=MNf[:]), [bf("MNf")], [bf("MN")])
        dma_in(CCs[:], c_ccs[:, :], [], [bf("CCs")])
        dma_in(SSs[:], c_sss[:, :], [], [bf("SSs")])
        for i in range(2):
            pool(lambda e, i=i: e.memset(VAs[i][:], 1.0), [], [bf(f"VA{i}")])
            pool(lambda e, i=i: e.memset(VST[i][:], 1.0), [], [bf(f"VST{i}")])
        vsti = [0]
        pool(lambda e: e.memset(VAc[:], 1.0), [], [bf("VAc0")])
        pool(lambda e: e.memset(VAcB[:], 1.0), [], [bf("VAc1")])
        VAnB1 = sb("VAnB", [T, 3, 4, 192], BF16)
        VAnB = [VAnB1 for _ in range(max(NBS, 1))]
        pool(lambda e: e.memset(VAnB1[:], 1.0), [], [bf("VAnB")])
        pool(lambda e: e.memset(WS[:], 0.0), [], [bf("WS")])
        pool(lambda e: e.memset(KHt[2][:], 0.0), [], [bf("KH2")])

        wconv = bf("wconv")
        for l in range(DEPTH):
            for kc in range(8):
                dma_g(WIN[l, kc * 128:(kc + 1) * 128, :], w_in[l, kc * 128:(kc + 1) * 128, :], [], [])
                dma_g(bass.AP(tensor=WGM.tensor, offset=WGM[l, 0, 0, kc, 0].offset, ap=[[8 * 128, 128], [128 * 8 * 128, 8], [1, 128]]),
                      w_gm[l, kc * 128:(kc + 1) * 128, :].rearrange("p (o j) -> p o j", j=128), [], [])
                dma_g(bass.AP(tensor=WO.tensor, offset=WO[l, 0, 0, kc, 0].offset, ap=[[8 * 128, 128], [128 * 8 * 128, 8], [1, 128]]),
                      w_o[l, kc * 128:(kc + 1) * 128, :].rearrange("p (o j) -> p o j", j=128), [], [])
            for kc in range(4):
                dma_g(bass.AP(tensor=WATT.tensor, offset=WATT[l, 0, 0, kc, 0].offset, ap=[[4 * 128, 128], [128 * 4 * 128, 8], [1, 128]]),
                      w_att[l, kc * 128:(kc + 1) * 128, :].rearrange("p (o j) -> p o j", j=128), [], [])
        conv_deps = [("d", k, sc.dtgt[k]) for k in range(2 * NDS) if sc.dtgt[k] > 0]
        wts = sc._waits("pool", conv_deps)

        for s_, v_ in wts:
            nc.gpsimd.wait_ge(s_, v_)
        pool(lambda e: e.memset(Es[:], 0.0), [], [wconv])

        dma_in(SV[0:48, :], b_ada.rearrange("l (j p) -> (l j) p", p=128), [], [bf("SV")])
        dma_in(SV[48:64, :], norm_g.rearrange("l (j p) -> (l j) p", p=128), [], [bf("SV")])
        dma_in(SV[64:72, :], final_g.rearrange("(j p) -> j p", p=128), [], [bf("SV")])
        pe(lambda e: e.transpose(out=PS[0][:, 0:72], in_=SV[0:72, :], identity=ident_f[0:72, 0:72]),
           [bf("SV"), bf("ident_f")], [PSB[0]])
        dve(lambda e: e.tensor_copy(out=SVT[:], in_=PS[0][:, 0:72]), [PSB[0]], [bf("SVT")])

        for l in range(DEPTH):
            for g in range(8):
                tt_, tt_b = t32()
                dma_in(tt_[:, 0:128], gm_ws[l, g], [], [tt_b])
                dve(lambda e, tt_=tt_: e.tensor_tensor(out=tt_[:, 128:256], in0=tt_[:, 0:128], in1=TRI[:], op=ALU.mult),
                    [tt_b, bf("TRI")], [tt_b])
                pe(lambda e, tt_=tt_: e.transpose(out=PS[1][:, 0:128], in_=tt_[:, 128:256], identity=ident_f[:]),
                   [tt_b, bf("ident_f")], [PSB[1]])
                dve(lambda e, l=l, g=g: e.tensor_copy(out=wmT[:, l, g, :], in_=PS[1][:, 0:128]), [PSB[1]], [bf("wmT")])
        dma_g(WMT.rearrange("l g s t -> s l g t"), wmT[:], [bf("wmT")], [bf("WMT")])

        dma_in(call_t, c_all[:, :], [], [bf("call")])
        for kc in range(8):
            pe(lambda e, kc=kc: e.transpose(out=PS[0][:, kc * 8:kc * 8 + NC6], in_=call_t[:, kc * 128:(kc + 1) * 128],
                                            identity=ident_f[0:NC6, 0:NC6]), [bf("call"), bf("ident_f")], [PSB[0]])
        ps0v = PS[0][:, 0:64].rearrange("p (k i) -> p k i", i=8)[:, :, 0:NC6]
        dve(lambda e: e.tensor_copy(out=cT[:], in_=ps0v), [PSB[0]], [bf("cT")])
        tt_, tt_b = t32()
        ttv = tt_[:, 0:8 * NC6].rearrange("p (k i) -> p k i", i=NC6)
        act(lambda e: e.activation(out=ttv, in_=cT[:], func=AF.Tanh, scale=0.5), [bf("cT")], [tt_b])
        dve(lambda e: e.scalar_tensor_tensor(out=ttv, in0=ttv, scalar=1.0, in1=cT[:], op0=ALU.add, op1=ALU.mult),
            [tt_b, bf("cT")], [tt_b])
        dve(lambda e: e.tensor_scalar(out=scT[:], in0=ttv, scalar1=0.5, scalar2=None, op0=ALU.mult), [tt_b], [bf("scT")])
        for l in range(DEPTH):
            for cb in range(6):
                i = wbi[0] % NWB
                wbi[0] += 1
                src = bass.AP(tensor=w_ada.tensor, offset=w_ada[l, 0, cb * 512].offset,
                              ap=[[3 * D, 128], [128 * 3 * D, 8], [1, 512]])
                dma_g(WB[i][:], src, [], [wbb[i]])
                for ch in range(4):
                    j = cb * 4 + ch

                    def mm(e, i=i, ch=ch):
                        ins = None
                        for kc in range(8):
                            ins = e.matmul(PS[2][:, 0:NC6], lhsT=WB[i][:, kc, ch * 128:(ch + 1) * 128], rhs=scT[:, kc, :],
                                           start=(kc == 0), stop=(kc == 7))
                        return ins
                    pe(mm, [wbb[i], bf("scT")], [PSB[2]])
                    dve(lambda e, l=l, j=j: e.tensor_scalar(out=modT[:, l, j, :], in0=PS[2][:, 0:NC6],
                                                            scalar1=SVT[:, l * 24 + j:l * 24 + j + 1], scalar2=None, op0=ALU.add),
                        [PSB[2], bf("SVT")], [bf("modT")])
            for kc in range(8):
                dve(lambda e, l=l, kc=kc: e.tensor_scalar(out=Am[:, l, kc, :], in0=modT[:, l, 8 + kc, :], scalar1=1.0,
                                                          scalar2=SVT[:, 48 + l * 8 + kc:48 + l * 8 + kc + 1],
                                                          op0=ALU.add, op1=ALU.mult), [bf("modT"), bf("SVT")], [bf("Am")])
                dve(lambda e, l=l, kc=kc: e.tensor_scalar(out=G4[:, l, kc, :], in0=modT[:, l, 16 + kc, :], scalar1=0.25,
                                                          scalar2=None, op0=ALU.mult), [bf("modT")], [bf("G4")])

        def wblock(src3, l, k0, nk, col0, ncols=512):
            if ncols == 128:
                i = wsi[0] % NWS
                wsi[0] += 1
                dma_in(WBS[i][:, 0:nk, :], src3[l, col0 // 128], [wconv], [wsb[i]])
                return WBS[i], wsb[i]
            ncol_total = src3.shape[2]
            src = bass.AP(tensor=src3.tensor, offset=src3[l, k0 * 128, col0].offset,
                          ap=[[ncol_total, 128], [128 * ncol_total, nk], [1, ncols]])
            i = wbi[0] % NWB
            wbi[0] += 1
            dma_in(WB[i][:, 0:nk, 0:ncols], src, [wconv], [wbb[i]])
            return WB[i], wbb[i]

        def rms_rstd(src, srcb, n, gsel):
            act(lambda e: e.activation(out=BIG[:, :, 0:n], in_=src, func=AF.Square), [srcb], [bf("BIG")])

            def mm(e):
                ins = None
                for kc in range(8):
                    ins = e.matmul(PS[7][:, 0:n], lhsT=gsel, rhs=BIG[:, kc, 0:n], start=(kc == 0), stop=(kc == 7))
                return ins
            pe(mm, [bf("BIG"), bf("ones")], [PSB[7]])
            dve(lambda e: e.tensor_scalar(out=rstd[:, 0:n], in0=PS[7][:, 0:n], scalar1=1.0 / D, scalar2=1e-6,
                                          op0=ALU.mult, op1=ALU.add), [PSB[7]], [bf("rstd")])
            act(lambda e: e.activation(out=rstd[:, 0:n], in_=rstd[:, 0:n], func=AF.Ln), [bf("rstd")], [bf("rstd")])
            act(lambda e: e.activation(out=rstd[:, 0:n], in_=rstd[:, 0:n], func=AF.Exp, scale=-0.5), [bf("rstd")], [bf("rstd")])

        ones_b = sb("ones_b", [128, 128], BF16)
        pool(lambda e: e.memset(ones_b[:], 1.0), [], [bf("ones")])
        for tb_ in range(S // 128):
            dma_g(bass.AP(tensor=VH.tensor, offset=VH[tb_ * 128, 0, 0, 64].offset, ap=[[2304, 128], [192, 12], [1, 64]]),
                  bass.AP(tensor=ones_b[:].tensor, offset=ones_b[:].offset, ap=[list(ones_b[:].ap[0]), [0, 12], [1, 64]]),
                  [bf("ones")], [bf("VH")])

        def load_layer_vecs(l):
            dma_in(LG[:], bcast_rows(gm_ln_g[l], D), [], [bf("LG")])
            dma_in(LB[:], bcast_rows(gm_ln_b[l], D), [], [bf("LB")])
            dma_in(BSR[:], bcast_rows(gm_bs[l].rearrange("g t -> (g t)"), D), [], [bf("BSR")])

        def make_h(xsrc, xb_, n, l, cidx_of_col):
            dst = hT if n == 512 else hsT
            dstb = bf("hT") if n == 512 else bf("hsT")
            for kc in range(8):
                tt_, tt_b = t32()
                dve(lambda e, kc=kc, tt_=tt_: e.tensor_tensor(out=tt_[:, 0:n], in0=xsrc[:, kc, 0:n], in1=rstd[:, 0:n], op=ALU.mult),
                    [xb_, bf("rstd")], [tt_b])
                for (c0, c1, ci) in cidx_of_col:
                    dve(lambda e, kc=kc, tt_=tt_, c0=c0, c1=c1, ci=ci: e.tensor_scalar(
                        out=dst[:, kc, c0:c1], in0=tt_[:, c0:c1], scalar1=Am[:, l, kc, ci:ci + 1],
                        scalar2=modT[:, l, kc, ci:ci + 1], op0=ALU.mult, op1=ALU.add),
                        [tt_b, bf("Am"), bf("modT")], [dstb])

        def rope_block(ps_ap, cc_ap, ss_ap, npart, want_f32):
            m1, m1b = t32()
            m2, m2b = t32()
            p3 = ps_ap.rearrange("p (h d) -> p h d", d=64)
            m13 = m1[0:npart, :].rearrange("p (h d) -> p h d", d=64)
            m23 = m2[0:npart, :].rearrange("p (h d) -> p h d", d=64)
            ccb = bass.AP(tensor=cc_ap.tensor, offset=cc_ap.offset, ap=[list(cc_ap.ap[0]), [0, 8], [1, 64]])
            ss1 = bass.AP(tensor=ss_ap.tensor, offset=ss_ap.offset, ap=[list(ss_ap.ap[0]), [0, 8], [1, 32]])
            ss2 = bass.AP(tensor=ss_ap.tensor, offset=ss_ap.offset + 32, ap=[list(ss_ap.ap[0]), [0, 8], [1, 32]])
            return m1, m1b, m2, m2b, p3, m13, m23, ccb, ss1, ss2

        cur_l = [-1]

        def prompt_group(b, l, tg):
            T0 = tg * 512
            ci = b
            if cur_l[0] != l:
                load_layer_vecs(l)
                cur_l[0] = l
            xTb = bf("xT")
            if l == 0:
                for ti in range(4):
                    xl = XL[ti % 2]
                    xlb = bf(f"XL{ti % 2}")
                    dma_in(xl[:], x_p[b, T0 + ti * 128:T0 + (ti + 1) * 128, :], [], [xlb])
                    for hf in range(2):
                        pi = 4 + hf

                        def tr(e, xl=xl, hf=hf, pi=pi):
                            ins = None
                            for k in range(4):
                                kc = hf * 4 + k
                                ins = e.transpose(out=PS[pi][:, k * 128:(k + 1) * 128], in_=xl[:, kc * 128:(kc + 1) * 128],
                                                  identity=ident_f[:])
                            return ins
                        pe(tr, [xlb, bf("ident_f")], [PSB[pi]])
                        act(lambda e, hf=hf, pi=pi, ti=ti: e.activation(
                            out=xT[:, hf * 4:hf * 4 + 4, ti * 128:(ti + 1) * 128],
                            in_=PS[pi][:, :].rearrange("p (k t) -> p k t", t=128), func=AF.Copy), [PSB[pi]], [xTb])
            else:
                dma_in(xT[:], XS[:, :, T0:T0 + 512].rearrange("k p t -> p k t"), [bf("XS")], [xTb])
            if DBG_STOP == 'X':
                return
            rms_rstd(xT[:], xTb, 512, ones_b[:])
            make_h(xT, xTb, 512, l, [(0, 512, ci)])
            dma_in(CC[:], c_cc[T0:T0 + 512, :].rearrange("(i p) d -> p i d", p=128), [], [bf("CC")])
            dma_in(SSn[:], c_ss[T0:T0 + 512, :].rearrange("(i p) d -> p i d", p=128), [], [bf("SS")])
            if DBG_STOP == 'H':
                return
            qk_pending = []
            for cb in range(6):
                isk = cb >= 3
                g = cb % 3
                W, Wb = wblock(WIN, l, 0, 8, OQ + cb * 512)
                for ti in range(4):
                    pi = ti % 4

                    def mm(e, W=W, ti=ti, pi=pi):
                        ins = None
                        for kc in range(8):
                            ins = e.matmul(PS[pi][:, :], lhsT=hT[:, kc, ti * 128:(ti + 1) * 128], rhs=W[:, kc, :],
                                           start=(kc == 0), stop=(kc == 7))
                        return ins
                    pe(mm, [Wb, bf("hT")], [PSB[pi]])
                    m1, m1b, m2, m2b, p3, m13, m23, ccb, ss1, ss2 = rope_block(PS[pi][:, :], CC[:, ti, :], SSn[:, ti, :], 128, isk)
                    dve(lambda e, m13=m13, p3=p3, ccb=ccb: e.tensor_tensor(out=m13, in0=p3, in1=ccb, op=ALU.mult),
                        [PSB[pi], bf("CC")], [m1b])
                    dve(lambda e, m23=m23, p3=p3, ss1=ss1: e.tensor_tensor(out=m23[:, :, 0:32], in0=p3[:, :, 32:64], in1=ss1, op=ALU.mult),
                        [PSB[pi], bf("SS")], [m2b])
                    dve(lambda e, m23=m23, p3=p3, ss2=ss2: e.tensor_tensor(out=m23[:, :, 32:64], in0=p3[:, :, 0:32], in1=ss2, op=ALU.mult),
                        [PSB[pi], bf("SS")], [m2b])
                    ob, obb = tb16()
                    if isk:
                        pool(lambda e, m1=m1, m2=m2: e.tensor_tensor(out=m1[:], in0=m1[:], in1=m2[:], op=ALU.add), [m1b, m2b], [m1b])
                        act(lambda e, ob=ob, m1=m1: e.activation(out=ob[:], in_=m1[:], func=AF.Copy), [m1b], [obb])
                        keep = DILS[g][0]
                        t_lo = T0 + ti * 128
                        if t_lo >= S - keep:
                            r0 = t_lo - (S - keep)
                            dma_g(kvp[g][l, b, r0:r0 + 128, 0, :], m1[:], [m1b], [])
                    else:
                        pool(lambda e, m1=m1, m2=m2, ob=ob: e.tensor_tensor(out=ob[:], in0=m1[:], in1=m2[:], op=ALU.add), [m1b, m2b], [obb])
                    pj = 4 + (ti % 2)

                    def tr_and_evac(ob=ob, obb=obb, pj=pj, isk=isk, g=g, ti=ti):
                        def tr(e):
                            ins = None
                            pv = PS[pj][:, :].bitcast(BF16)
                            for c in range(4):
                                ins = e.transpose(out=pv[:, c * 128:(c + 1) * 128], in_=ob[:, c * 128:(c + 1) * 128], identity=ident_b[:])
                            return ins
                        pe(tr, [obb, bf("ident_b")], [PSB[pj]])
                        dst = KT if isk else QT
                        dstb = bf("KT") if isk else bf("QT")
                        act(lambda e: e.activation(
                            out=dst[:, g * 4:g * 4 + 4, ti * 128:(ti + 1) * 128],
                            in_=PS[pj][:, :].bitcast(BF16)[:, 0:512].rearrange("p (c t) -> p c t", t=128), func=AF.Copy),
                            [PSB[pj]], [dstb])
                    qk_pending.append(tr_and_evac)
                    while len(qk_pending) > 2:
                        qk_pending.pop(0)()
            while qk_pending:
                qk_pending.pop(0)()
            dma_g(KH[:, :, T0:T0 + 512].rearrange("c p t -> p c t"), KT[:], [bf("KT")], [bf("KH")])
            if DBG_STOP == 'QK':
                return
            for g in range(3):
                W, Wb = wblock(WIN, l, 0, 8, OVAL + g * 512)
                for ti in range(4):
                    pi = ti % 4

                    def mm(e, W=W, ti=ti, pi=pi):
                        ins = None
                        for kc in range(8):
                            ins = e.matmul(PS[pi][:, :], lhsT=hT[:, kc, ti * 128:(ti + 1) * 128], rhs=W[:, kc, :],
                                           start=(kc == 0), stop=(kc == 7))
                        return ins
                    pe(mm, [Wb, bf("hT")], [PSB[pi]])
                    vf, vfb = t32()
                    act(lambda e, vf=vf, pi=pi: e.activation(out=vf[:], in_=PS[pi][:, :], func=AF.Copy), [PSB[pi]], [vfb])
                    t_lo = T0 + ti * 128
                    keep = DILS[g][0]
                    if t_lo >= S - keep:
                        r0 = t_lo - (S - keep)
                        dma_g(kvp[g][l, b, r0:r0 + 128, 1, :], vf[:], [vfb], [])
                    vi = vsti[0] % 2
                    vsti[0] += 1
                    vst = VST[vi]
                    vdst = bass.AP(tensor=vst[:].tensor, offset=vst[:].offset, ap=[list(vst[:].ap[0]), [192, 4], [128, 2], [1, 64]])
                    dve(lambda e, vdst=vdst, vf=vf: e.tensor_copy(out=vdst, in_=vf[:, :].rearrange("p (c h d) -> p c h d", h=2, d=64)),
                        [vfb], [bf(f"VST{vi}")])
                    dma_g(VH[t_lo:t_lo + 128, g, :, :], vst[:], [bf(f"VST{vi}")], [bf("VH")])
            if DBG_STOP == 'V':
                return
            lo0 = max(0, T0 - 128)
            lo1 = max(0, T0 - 512)

            def att_load_k(c):
                dma_in(KHt[0][:, 0:T0 + 512 - lo0], KH[0 * 4 + c, :, lo0:T0 + 512], [bf("KH")], [bf("KH0")])
                dma_in(KHt[1][:, 0:T0 + 512 - lo1], KH[1 * 4 + c, :, lo1:T0 + 512], [bf("KH")], [bf("KH1")])
                dma_in(KHt[2][:, 0:T0 + 512], KH[2 * 4 + c, :, 0:T0 + 512], [bf("KH")], [bf("KH2")])

            def att_load_v(c):
                VA = VAs[c % 2]
                vab = bf(f"VA{c % 2}")
                blk0 = max(0, tg * 4 - 1)
                nb = tg * 4 + 4 - blk0
                kb0 = blk0 - (tg * 4 - 1)
                src = bass.AP(tensor=VH.tensor, offset=VH[blk0 * 128, 0, c, 0].offset, ap=[[2304, 128], [128 * 2304, nb], [1, 192]])
                dma_in(VA[:, kb0:kb0 + nb, :], src, [bf("VH")], [vab])
                for bi in ((0, 1) if tg > 0 else (1,)):
                    src = bass.AP(tensor=VH.tensor, offset=VH[(tg - 1 + bi) * 512, 1, c, 0].offset, ap=[[4 * 2304, 128], [2304, 4], [1, 192]])
                    dma_in(VA[:, 5 + 4 * bi:9 + 4 * bi, :], src, [bf("VH")], [vab])
                np_ = 32 * (tg + 1)
                src = bass.AP(tensor=VH.tensor, offset=VH[0, 2, c, 0].offset, ap=[[16 * 2304, np_], [2304, 16], [1, 192]])
                dma_in(VA[0:np_, 13:29, :], src, [bf("VH")], [vab])

            sbanks = [6, 7, 2, 3]
            ucnt = [0]
            pending = []

            def flush(keep):
                while len(pending) > keep:
                    pending.pop(0)()

            def att_pair(c):
                VA = VAs[c % 2]
                vab = bf(f"VA{c % 2}")
                for hh in range(2):
                    rows = slice(hh * 64, hh * 64 + 64)
                    acc = PS[4 + hh]
                    accb = PSB[4 + hh]
                    vcol = slice(0, 128) if hh == 0 else slice(64, 192)
                    first = [True]

                    def pv_mm(e, slot, esrc, outcols, first=first, VA=VA, vcol=vcol):
                        ins = e.matmul(outcols, lhsT=VA[:, slot, vcol], rhs=esrc, start=first[0], stop=False, skip_group_check=True)
                        first[0] = False
                        return ins
                    for g in range(2):
                        for qp in range(2):
                            pi = sbanks[ucnt[0] % 4]
                            ucnt[0] += 1
                            units = []
                            for u in range(2):
                                qi = qp * 2 + u
                                if g == 0:
                                    qcols = QT[rows, 0 * 4 + c, qi * 128:(qi + 1) * 128]
                                    has_prev = (tg * 4 + qi) > 0
                                    kprev = KHt[0][rows, (T0 - lo0) + (qi - 1) * 128:(T0 - lo0) + qi * 128] if has_prev else None
                                    kcur = KHt[0][rows, (T0 - lo0) + qi * 128:(T0 - lo0) + (qi + 1) * 128]
                                    sprev, scur = qi, qi + 1
                                    ocols = acc[:, qi * 128:(qi + 1) * 128]
                                else:
                                    r = qi
                                    qcols = QT[rows, 1 * 4 + c, r:512:4]
                                    has_prev = tg > 0
                                    kprev = KHt[1][rows, (T0 - lo1) - 512 + r:(T0 - lo1):4] if has_prev else None
                                    kcur = KHt[1][rows, (T0 - lo1) + r:(T0 - lo1) + 512:4]
                                    sprev, scur = 5 + r, 9 + r
                                    ocols = acc[:, r:512:4]
                                units.append((qcols, has_prev, kprev, kcur, sprev, scur, ocols))

                            def smm(e, units=units, pi=pi):
                                ins = None
                                for u, (qcols, has_prev, kprev, kcur, sprev, scur, ocols) in enumerate(units):
                                    if has_prev:
                                        ins = e.matmul(PS[pi][:, u * 256:u * 256 + 128], lhsT=kprev, rhs=qcols, start=True, stop=True)
                                    ins = e.matmul(PS[pi][:, u * 256 + 128:u * 256 + 256], lhsT=kcur, rhs=qcols, start=True, stop=True)
                                return ins
                            pe(smm, [bf("QT"), bf(f"KH{g}")], [PSB[pi]])
                            c0 = 0 if units[0][1] else 128
                            E, Eb = tb16()
                            act(lambda e, E=E, pi=pi, c0=c0: e.activation(out=E[:, c0:512], in_=PS[pi][:, c0:512], func=AF.Exp, scale=0.125),
                                [PSB[pi]], [Eb])
                            dve(lambda e, E=E, c0=c0: e.tensor_tensor(out=E[:, c0:512], in0=E[:, c0:512], in1=MB[:, c0:512], op=ALU.mult),
                                [Eb, bf("MB")], [Eb])

                            def pmm(e, units=units, E=E, pv_mm=pv_mm):
                                ins = None
                                for u, (qcols, has_prev, kprev, kcur, sprev, scur, ocols) in enumerate(units):
                                    if has_prev:
                                        ins = pv_mm(e, sprev, E[:, u * 256:u * 256 + 128], ocols)
                                    ins = pv_mm(e, scur, E[:, u * 256 + 128:u * 256 + 256], ocols)
                                return ins
                            pending.append(lambda pmm=pmm, Eb=Eb, vab=vab, accb=accb: pe(pmm, [Eb, vab], [accb]))
                            flush(2)
                    pi = sbanks[ucnt[0] % 4]
                    ucnt[0] += 1

                    def smm2(e, pi=pi):
                        ins = None
                        for r in range(16):
                            ins = e.matmul(PS[pi][:, r * 32:(r + 1) * 32], lhsT=KHt[2][rows, r:2048:16],
                                           rhs=QT[rows, 2 * 4 + c, r:512:16], start=True, stop=True)
                        return ins
                    pe(smm2, [bf("QT"), bf("KH2")], [PSB[pi]])
                    E, Eb = tb16()
                    act(lambda e, E=E, pi=pi: e.activation(out=E[:], in_=PS[pi][:, :], func=AF.Exp, scale=0.125), [PSB[pi]], [Eb])
                    dve(lambda e, E=E: e.tensor_tensor(out=E[:], in0=E[:], in1=MG[:, tg, :], op=ALU.mult), [Eb, bf("MG")], [Eb])

                    def pmm2(e, E=E, pv_mm=pv_mm, acc=acc):
                        ins = None
                        for r in range(16):
                            ins = pv_mm(e, 13 + r, E[:, r * 32:(r + 1) * 32], acc[:, r:512:16])
                        return ins
                    pending.append(lambda pmm2=pmm2, Eb=Eb, vab=vab, accb=accb: pe(pmm2, [Eb, vab], [accb]))
                    urow = rows
                    zrow = slice(64, 128) if hh == 0 else slice(0, 64)

                    def norm(acc=acc, accb=accb, urow=urow, zrow=zrow, c=c):
                        rz, rzb = t32()
                        act(lambda e: e.activation(out=rz[urow, :], in_=acc[zrow, :], func=AF.Ln), [accb], [rzb])
                        act(lambda e: e.activation(out=rz[urow, :], in_=rz[urow, :], func=AF.Exp, scale=-1.0), [rzb], [rzb])
                        dve(lambda e: e.tensor_tensor(out=ybT[urow, c, :], in0=acc[urow, :], in1=rz[urow, :], op=ALU.mult),
                            [accb, rzb], [bf("ybT")])
                    pending.append(norm)

            branch_a1(l, 512, 4, hT, bf("hT"), None)
            att_load_v(0)
            att_load_k(0)
            branch_a2(l, 512, hT, bf("hT"), None)
            for c in range(4):
                if c + 1 < 4:
                    att_load_v(c + 1)
                att_pair(c)
                flush(0)
                if c + 1 < 4:
                    att_load_k(c + 1)
            if DBG_STOP == 'ATT':
                return
            merge_out(l, 512, hT, bf("hT"), xT, xTb, ybT, bf("ybT"), [(0, 512, ci)])
            if DBG_STOP == 'BM':
                return
            if l == 0:
                dma_g(XS[:, :, T0:T0 + 512].rearrange("k p t -> p k t"), xT[:], [xTb], [bf("XS")])
            else:
                final_out(xT, xTb, 512, lambda ti: y_p[b, T0 + ti * 128:T0 + (ti + 1) * 128, :], 4, 128)

        def final_out(xsrc, xb_, n, dst_of_tile, ntiles, npart):
            rms_rstd(xsrc[:, :, 0:n], xb_, n, ones_b[:])
            for kc in range(8):
                dve(lambda e, kc=kc: e.scalar_tensor_tensor(out=xsrc[:, kc, 0:n], in0=xsrc[:, kc, 0:n], scalar=SVT[:, 64 + kc:65 + kc],
                                                            in1=rstd[:, 0:n], op0=ALU.mult, op1=ALU.mult), [xb_, bf("rstd"), bf("SVT")], [xb_])
            for ti in range(ntiles):
                xl = XL[ti % 2]
                xlb = bf(f"XL{ti % 2}")
                for hf in range(2):
                    pi = 4 + hf

                    def tr(e, hf=hf, pi=pi, ti=ti):
                        ins = None
                        for k in range(4):
                            kc = hf * 4 + k
                            ins = e.transpose(out=PS[pi][0:npart, k * 128:(k + 1) * 128], in_=xsrc[:, kc, ti * npart:(ti + 1) * npart],
                                              identity=ident_f[:])
                        return ins
                    pe(tr, [xb_, bf("ident_f")], [PSB[pi]])
                    act(lambda e, hf=hf, pi=pi, xl=xl: e.activation(out=xl[0:npart, hf * 512:(hf + 1) * 512], in_=PS[pi][0:npart, :], func=AF.Copy),
                        [PSB[pi]], [xlb])
                dma_g(dst_of_tile(ti), xl[0:npart, :], [xlb], [])

        def branch_a_and_merge(l, n, ntile, hsrc, hb_, xsrc, xb_, ybsrc, ybb, cidx_of_col, sample):
            branch_a1(l, n, ntile, hsrc, hb_, sample)
            branch_a2(l, n, hsrc, hb_, sample)
            merge_out(l, n, hsrc, hb_, xsrc, xb_, ybsrc, ybb, cidx_of_col)

        def branch_a1(l, n, ntile, hsrc, hb_, sample):
            npart = 128 if sample is None else n
            Ws = [wblock(WIN, l, 0, 8, OV + hf * 512) for hf in range(2)]
            for t0 in range(0, ntile, 2):
                tiles = [t for t in (t0, t0 + 1) if t < ntile]
                k = len(tiles)
                for ti in tiles:
                    gv = XL[ti % 2]
                    gvb = bf(f"XL{ti % 2}")
                    for hf in range(2):
                        W, Wb = Ws[hf]
                        pi = (ti % 2) * 2 + hf

                        def mm(e, W=W, ti=ti, pi=pi):
                            ins = None
                            for kc in range(8):
                                ins = e.matmul(PS[pi][0:npart, :], lhsT=hsrc[:, kc, ti * npart:(ti + 1) * npart], rhs=W[:, kc, :],
                                               start=(kc == 0), stop=(kc == 7))
                            return ins
                        pe(mm, [Wb, hb_], [PSB[pi]])
                        act(lambda e, gv=gv, hf=hf, pi=pi: e.activation(out=gv[0:npart, hf * 512:(hf + 1) * 512], in_=PS[pi][0:npart, :],
                                                                       func=AF.Gelu_apprx_tanh), [PSB[pi]], [gvb])
                        dve(lambda e, gv=gv, hf=hf, ti=ti: e.bn_stats(out=st6[0:npart, ti, hf, :], in_=gv[0:npart, hf * 512:(hf + 1) * 512]),
                            [gvb], [bf("st6")])
                    dve(lambda e, ti=ti: e.bn_aggr(out=mv[0:npart, ti, :], in_=st6[0:npart, ti, :, :]), [bf("st6")], [bf("mv")])
                dve(lambda e: e.tensor_scalar(out=lnr[0:npart, t0:t0 + k], in0=mv[0:npart, t0:t0 + k, 1], scalar1=1e-5, scalar2=None, op0=ALU.add),
                    [bf("mv")], [bf("lnr")])
                act(lambda e: e.activation(out=lnr[0:npart, t0:t0 + k], in_=lnr[0:npart, t0:t0 + k], func=AF.Ln), [bf("lnr")], [bf("lnr")])
                act(lambda e: e.activation(out=lnr[0:npart, t0:t0 + k], in_=lnr[0:npart, t0:t0 + k], func=AF.Exp, scale=-0.5),
                    [bf("lnr")], [bf("lnr")])
                for ti in tiles:
                    gv = XL[ti % 2]
                    gvb = bf(f"XL{ti % 2}")
                    dve(lambda e, gv=gv, ti=ti: e.tensor_scalar(out=gv[0:npart, :], in0=gv[0:npart, :], scalar1=mv[0:npart, ti, 0:1],
                                                                scalar2=lnr[0:npart, ti:ti + 1], op0=ALU.subtract, op1=ALU.mult),
                        [gvb, bf("mv"), bf("lnr")], [gvb])
                    pool(lambda e, gv=gv: e.tensor_tensor(out=gv[0:npart, :], in0=gv[0:npart, :], in1=LG[0:npart, :], op=ALU.mult), [gvb, bf("LG")], [gvb])
                    if sample is None:
                        pool(lambda e, gv=gv, ti=ti: e.tensor_tensor(out=vn[:, ti, :], in0=gv[:, :], in1=LB[:, :], op=ALU.add), [gvb, bf("LB")], [bf("vn")])
                    else:
                        pool(lambda e, gv=gv: e.tensor_tensor(out=gv[0:npart, :], in0=gv[0:npart, :], in1=LB[0:npart, :], op=ALU.add), [gvb, bf("LB")], [gvb])
                        dma_g(gmv[l, :, :], gv[0:npart, :], [gvb], [])
                        act(lambda e, gv=gv: e.activation(out=vn[0:npart, 0, :], in_=gv[0:npart, :], func=AF.Copy), [gvb], [bf("vn")])

        def branch_a2(l, n, hsrc, hb_, sample):
            yab = bf("BIG")
            Wu = None
            for g in range(8):
                if g % 4 == 0:
                    Wu = wblock(WIN, l, 0, 8, OU + (g // 4) * 512)
                    Wz = wblock(WIN, l, 0, 8, OZA + (g // 4) * 512)
                ch = g % 4
                bu, bz, bs_ = (0, 1, 2) if g % 2 == 0 else (3, 6, 7)

                def mmf(e, Wt, pi, ch=ch):
                    ins = None
                    for kc in range(8):
                        ins = e.matmul(PS[pi][:, 0:n], lhsT=Wt[:, kc, ch * 128:(ch + 1) * 128], rhs=hsrc[:, kc, 0:n],
                                       start=(kc == 0), stop=(kc == 7))
                    return ins
                pe(lambda e, W=Wu[0]: mmf(e, W, bu), [Wu[1], hb_], [PSB[bu]])
                pe(lambda e, W=Wz[0]: mmf(e, W, bz), [Wz[1], hb_], [PSB[bz]])

                def spm(e, g=g):
                    ins = None
                    if sample is None:
                        for ti in range(4):
                            ins = e.matmul(PS[bs_][:, ti * 128:(ti + 1) * 128], lhsT=vn[:, ti, g * 128:(g + 1) * 128], rhs=wmT[:, l, g, :],
                                           start=True, stop=True)
                    else:
                        ins = e.matmul(PS[bs_][:, 0:n], lhsT=vn[0:n, 0, g * 128:(g + 1) * 128], rhs=WS[0:n, g, 0:n], start=True, stop=True)
                    return ins
                pe(spm, [bf("vn"), bf("wmT"), bf("WS")], [PSB[bs_]])
                gu, gub = t32()
                tz, tzb = t32()
                spb, spbb = t32()
                act(lambda e, gu=gu: e.activation(out=gu[:, 0:n], in_=PS[bu][:, 0:n], func=AF.Gelu_apprx_tanh), [PSB[bu]], [gub])
                act(lambda e, tz=tz: e.activation(out=tz[:, 0:n], in_=PS[bz][:, 0:n], func=AF.Tanh, scale=0.5), [PSB[bz]], [tzb])
                dve(lambda e, tz=tz: e.scalar_tensor_tensor(out=tz[:, 0:n], in0=tz[:, 0:n], scalar=1.0, in1=PS[bz][:, 0:n], op0=ALU.add, op1=ALU.mult),
                    [tzb, PSB[bz]], [tzb])
                if sample is None:
                    bsv = bass.AP(tensor=BSR[:].tensor, offset=BSR[0, g * 128].offset, ap=[list(BSR[:].ap[0]), [0, 4], [1, 128]])
                    dve(lambda e, spb=spb, bsv=bsv: e.tensor_tensor(out=spb[:, :].rearrange("p (a t) -> p a t", t=128),
                                                                    in0=PS[bs_][:, :].rearrange("p (a t) -> p a t", t=128), in1=bsv, op=ALU.add),
                        [PSB[bs_], bf("BSR")], [spbb])
                else:
                    bsv = bass.AP(tensor=BSR[:].tensor, offset=BSR[0, g * 128].offset, ap=[list(BSR[:].ap[0]), [0, n // T], [1, T]])
                    dve(lambda e, spb=spb, bsv=bsv: e.tensor_tensor(out=spb[:, 0:n].rearrange("p (a t) -> p a t", t=T),
                                                                    in0=PS[bs_][:, 0:n].rearrange("p (a t) -> p a t", t=T), in1=bsv, op=ALU.add),
                        [PSB[bs_], bf("BSR")], [spbb])
                dve(lambda e, gu=gu, spb=spb: e.tensor_tensor(out=gu[:, 0:n], in0=gu[:, 0:n], in1=spb[:, 0:n], op=ALU.mult), [gub, spbb], [gub])
                dve(lambda e, gu=gu, tz=tz, g=g: e.tensor_tensor(out=BIG[:, g, 0:n], in0=gu[:, 0:n], in1=tz[:, 0:n], op=ALU.mult), [gub, tzb], [yab])

        def merge_out(l, n, hsrc, hb_, xsrc, xb_, ybsrc, ybb, cidx_of_col):
            yab = bf("BIG")
            Wzb = wblock(WIN, l, 0, 8, OZB)
            for c in range(4):
                bq = c % 2
                pe(lambda e, c=c, bq=bq: mmf_generic(e, Wzb[0], bq, c, hsrc, n, 8), [Wzb[1], hb_], [PSB[bq]])
                tz, tzb = t32()
                act(lambda e, tz=tz, bq=bq: e.activation(out=tz[:, 0:n], in_=PS[bq][:, 0:n], func=AF.Tanh, scale=0.5), [PSB[bq]], [tzb])
                dve(lambda e, tz=tz, bq=bq: e.scalar_tensor_tensor(out=tz[:, 0:n], in0=tz[:, 0:n], scalar=1.0, in1=PS[bq][:, 0:n], op0=ALU.add, op1=ALU.mult),
                    [tzb, PSB[bq]], [tzb])
                dve(lambda e, tz=tz, c=c: e.tensor_tensor(out=ybsrc[:, c, 0:n], in0=ybsrc[:, c, 0:n], in1=tz[:, 0:n], op=ALU.mult), [tzb, ybb], [ybb])
            for oc in range(8):
                if oc % 4 == 0:
                    Wga = wblock(WIN, l, 0, 8, OGA + (oc // 4) * 512)
                    Wgb = wblock(WIN, l, 0, 8, OGB + (oc // 4) * 512)
                ch = oc % 4
                Wg1 = wblock(WGM, l, 0, 8, oc * 128, 128)
                Wa1 = wblock(WATT, l, 0, 4, oc * 128, 128)
                b0 = 0 if oc % 2 == 0 else 4
                pe(lambda e, ch=ch, W=Wga[0], b0=b0: mmf_generic(e, W, b0, ch, hsrc, n, 8), [Wga[1], hb_], [PSB[b0]])
                pe(lambda e, ch=ch, W=Wgb[0], b0=b0: mmf_generic(e, W, b0 + 1, ch, hsrc, n, 8), [Wgb[1], hb_], [PSB[b0 + 1]])
                pe(lambda e, W=Wg1[0], b0=b0: mmf_generic(e, W, b0 + 2, 0, BIG, n, 8), [Wg1[1], yab], [PSB[b0 + 2]])
                pe(lambda e, W=Wa1[0], b0=b0: mmf_generic(e, W, b0 + 3, 0, ybsrc, n, 4), [Wa1[1], ybb], [PSB[b0 + 3]])
                ta, tab_ = t32()
                tb_, tbb_ = t32()
                act(lambda e, ta=ta, b0=b0: e.activation(out=ta[:, 0:n], in_=PS[b0][:, 0:n], func=AF.Tanh, scale=0.5), [PSB[b0]], [tab_])
                act(lambda e, tb_=tb_, b0=b0: e.activation(out=tb_[:, 0:n], in_=PS[b0 + 1][:, 0:n], func=AF.Tanh, scale=0.5), [PSB[b0 + 1]], [tbb_])
                dve(lambda e, ta=ta, b0=b0: e.scalar_tensor_tensor(out=ta[:, 0:n], in0=ta[:, 0:n], scalar=1.0, in1=PS[b0 + 2][:, 0:n], op0=ALU.add, op1=ALU.mult),
                    [tab_, PSB[b0 + 2]], [tab_])
                dve(lambda e, tb_=tb_, b0=b0: e.scalar_tensor_tensor(out=tb_[:, 0:n], in0=tb_[:, 0:n], scalar=1.0, in1=PS[b0 + 3][:, 0:n], op0=ALU.add, op1=ALU.mult),
                    [tbb_, PSB[b0 + 3]], [tbb_])
                dve(lambda e, ta=ta, tb_=tb_, oc=oc: e.tensor_tensor(out=MGT[:, oc, 0:n], in0=ta[:, 0:n], in1=tb_[:, 0:n], op=ALU.add),
                     [tab_, tbb_], [bf("MGT")])
            for oc in range(8):
                Wo1 = wblock(WO, l, 0, 8, oc * 128, 128)
                bo = oc % 4
                pe(lambda e, W=Wo1[0], bo=bo: mmf_generic(e, W, bo, 0, MGT, n, 8), [Wo1[1], bf("MGT")], [PSB[bo]])
                for (c0, c1, ci) in cidx_of_col:
                    dve(lambda e, oc=oc, c0=c0, c1=c1, ci=ci, bo=bo: e.scalar_tensor_tensor(
                        out=xsrc[:, oc, c0:c1], in0=PS[bo][:, c0:c1], scalar=G4[:, l, oc, ci:ci + 1], in1=xsrc[:, oc, c0:c1],
                        op0=ALU.mult, op1=ALU.add), [PSB[bo], bf("G4"), xb_], [xb_])

        def mmf_generic(e, Wt, pi, ch, src, n, nk):
            ins = None
            for kc in range(nk):
                ins = e.matmul(PS[pi][:, 0:n], lhsT=Wt[:, kc, ch * 128:(ch + 1) * 128], rhs=src[:, kc, 0:n],
                               start=(kc == 0), stop=(kc == nk - 1))
            return ins

        def sample_layer(l):
            n = TS
            xb_ = bf("xsT")
            if cur_l[0] != l:
                load_layer_vecs(l)
                cur_l[0] = l
            if l == 0:
                xl = XL[0]
                xlb = bf("XL0")
                dma_in(xl[0:n, :], x_s[:, :], [], [xlb])
                for hf in range(2):
                    pi = 4 + hf

                    def tr(e, hf=hf, pi=pi):
                        ins = None
                        for k in range(4):
                            kc = hf * 4 + k
                            ins = e.transpose(out=PS[pi][:, k * 128:k * 128 + n], in_=xl[0:n, kc * 128:(kc + 1) * 128], identity=ident_f[0:n, 0:n])
                        return ins
                    pe(tr, [xlb, bf("ident_f")], [PSB[pi]])
                    act(lambda e, hf=hf, pi=pi: e.activation(out=xsT[:, hf * 4:hf * 4 + 4, :],
                                                            in_=PS[pi][:, :].rearrange("p (k t) -> p k t", t=128)[:, :, 0:n], func=AF.Copy),
                        [PSB[pi]], [xb_])
            for bs in range(NBS):
                dma_in(WS[bs * T:(bs + 1) * T, :, bs * T:(bs + 1) * T], WMT[l, :, 0:T, 0:T].rearrange("g s t -> s g t"), [bf("WMT")], [bf("WS")])
            if DBG_SSTOP == 'SX':
                return
            cols = [(bs * T, (bs + 1) * T, NBP + bs) for bs in range(NBS)]
            rms_rstd(xsT[:], xb_, n, ones_b[:])
            make_h(xsT, xb_, n, l, cols)
            hb_ = bf("hsT")
            if DBG_SSTOP == 'SH':
                return
            for cb in range(6):
                isk = cb >= 3
                g = cb % 3
                W, Wb = wblock(WIN, l, 0, 8, OQ + cb * 512)

                def mm(e, W=W):
                    ins = None
                    for kc in range(8):
                        ins = e.matmul(PS[0][0:n, :], lhsT=hsT[:, kc, :], rhs=W[:, kc, :], start=(kc == 0), stop=(kc == 7))
                    return ins
                pe(mm, [Wb, hb_], [PSB[0]])
                m1, m1b, m2, m2b, p3, m13, m23, ccb, ss1, ss2 = rope_block(PS[0][0:n, :], CCs[0:n, :], SSs[0:n, :], n, isk)
                dve(lambda e, m13=m13, p3=p3, ccb=ccb: e.tensor_tensor(out=m13, in0=p3, in1=ccb, op=ALU.mult), [PSB[0], bf("CCs")], [m1b])
                dve(lambda e, m23=m23, p3=p3, ss1=ss1: e.tensor_tensor(out=m23[:, :, 0:32], in0=p3[:, :, 32:64], in1=ss1, op=ALU.mult), [PSB[0], bf("SSs")], [m2b])
                dve(lambda e, m23=m23, p3=p3, ss2=ss2: e.tensor_tensor(out=m23[:, :, 32:64], in0=p3[:, :, 0:32], in1=ss2, op=ALU.mult), [PSB[0], bf("SSs")], [m2b])
                ob, obb = tb16()
                dve(lambda e, m1=m1, m2=m2: e.tensor_tensor(out=m1[0:n, :], in0=m1[0:n, :], in1=m2[0:n, :], op=ALU.add), [m1b, m2b], [m1b])
                act(lambda e, ob=ob, m1=m1: e.activation(out=ob[0:n, :], in_=m1[0:n, :], func=AF.Copy), [m1b], [obb])
                if isk:
                    dma_g(kvs[g][l, :, 0, :], m1[0:n, :], [m1b], [])

                def tr(e, ob=ob):
                    ins = None
                    pv = PS[4][:, :].bitcast(BF16)
                    for c in range(4):
                        ins = e.transpose(out=pv[:, c * 128:c * 128 + n], in_=ob[0:n, c * 128:(c + 1) * 128], identity=ident_b[0:n, 0:n])
                    return ins
                pe(tr, [obb, bf("ident_b")], [PSB[4]])
                dst = KTs if isk else QTs
                dstb = bf("KTs") if isk else bf("QTs")
                act(lambda e, dst=dst, g=g: e.activation(out=dst[:, g * 4:g * 4 + 4, :],
                                                         in_=PS[4][:, :].bitcast(BF16)[:, 0:512].rearrange("p (c t) -> p c t", t=128)[:, :, 0:n],
                                                         func=AF.Copy), [PSB[4]], [dstb])
            if DBG_SSTOP == 'SQK':
                return
            for g in range(3):
                W, Wb = wblock(WIN, l, 0, 8, OVAL + g * 512)

                def mm(e, W=W):
                    ins = None
                    for kc in range(8):
                        ins = e.matmul(PS[0][0:n, :], lhsT=hsT[:, kc, :], rhs=W[:, kc, :], start=(kc == 0), stop=(kc == 7))
                    return ins
                pe(mm, [Wb, hb_], [PSB[0]])
                vf, vfb = t32()
                act(lambda e, vf=vf: e.activation(out=vf[0:n, :], in_=PS[0][0:n, :], func=AF.Copy), [PSB[0]], [vfb])
                dma_g(kvs[g][l, :, 1, :], vf[0:n, :], [vfb], [])
            if DBG_SSTOP == 'SV':
                return
            for bs in range(NBS):
                sample_attention(l, bs)
            if DBG_SSTOP == 'SATT':
                return
            branch_a_and_merge(l, n, 1, hsT, hb_, xsT, xb_, ybs, bf("ybs"), cols, True)
            if l == DEPTH - 1:
                final_out(xsT, xb_, n, lambda ti: y_s[:, :], 1, n)


        def sample_attention(l, bs):
            n = TS
            hb_ = bf("hsT")
            for g in range(3):
                W, Wb = wblock(WIN, l, 0, 8, OVAL + g * 512)

                def mm2(e, W=W):
                    ins = None
                    for kc in range(8):
                        ins = e.matmul(PS[1][0:T, :], lhsT=hsT[:, kc, bs * T:(bs + 1) * T], rhs=W[:, kc, :], start=(kc == 0), stop=(kc == 7))
                    return ins
                pe(mm2, [Wb, hb_], [PSB[1]])
                vdst = bass.AP(tensor=VAnB1[:].tensor, offset=VAnB1[0, g, 0, 0].offset, ap=[list(VAnB1[:].ap[0]), [192, 4], [128, 2], [1, 64]])
                act(lambda e, vdst=vdst: e.activation(out=vdst, in_=PS[1][0:T, :].rearrange("p (c h d) -> p c h d", h=2, d=64), func=AF.Copy),
                    [PSB[1]], [bf("VAnB")])
            acc = PS[5]
            accb = PSB[5]
            first = [True]
            tiles = [(0, 0, 0)] + [(1, r, 1 + r) for r in range(4)] + [(2, r, 5 + r) for r in range(8)]
            spend = []
            for idx, (g, r, mi) in enumerate(tiles):
                par = idx % 2
                d = DILS[g][1]
                ct = CT[par]
                ctb = bf(f"CT{par}")
                ktc = (KTc, KTcB)[par]
                vac = (VAc, VAcB)[par]
                es = (Es, EsB)[par]
                ktb, vab_, esb = bf(f"KTc{par}"), bf(f"VAc{par}"), bf(f"Es{par}")
                pt = 2 + par
                be, bo = (7, 6) if par == 0 else (1, 0)
                src = bass.AP(tensor=ck[g].tensor, offset=ck[g][l, bs, r, 0, 0].offset, ap=[[d * 1024, 128], [1, 1024]])
                dma_in(ct[:], src, [], [ctb])

                kb, kbb = tb16()
                dve(lambda e, kb=kb, ct=ct: e.tensor_copy(out=kb[:], in_=ct[:, 0:512]), [ctb], [kbb])

                def tr(e, kb=kb, pt=pt):
                    ins = None
                    pv = PS[pt][:, :].bitcast(BF16)
                    for c in range(4):
                        ins = e.transpose(out=pv[:, c * 128:(c + 1) * 128], in_=kb[:, c * 128:(c + 1) * 128], identity=ident_b[:])
                    return ins
                pe(tr, [kbb, bf("ident_b")], [PSB[pt]])
                act(lambda e, ktc=ktc, pt=pt: e.activation(out=ktc[:], in_=PS[pt][:, :].bitcast(BF16)[:, 0:512].rearrange("p (c t) -> p c t", t=128),
                                                           func=AF.Copy), [PSB[pt]], [ktb])
                vdst = bass.AP(tensor=vac[:].tensor, offset=vac[0, 0, 0].offset, ap=[list(vac[:].ap[0]), [192, 4], [128, 2], [1, 64]])
                dve(lambda e, ct=ct, vdst=vdst: e.tensor_copy(out=vdst, in_=ct[:, 512:1024].rearrange("p (c h d) -> p c h d", h=2, d=64)),
                    [ctb], [vab_])

                def rest(g=g, mi=mi, ktc=ktc, vac=vac, es=es, ktb=ktb, vab_=vab_, esb=esb, be=be, bo=bo):
                    def smm(e):
                        ins = None
                        for h in (0, 2, 4, 6, 1, 3, 5, 7):
                            rows = slice((h % 2) * 64, (h % 2) * 64 + 64)
                            ins = e.matmul(PS[be if h % 2 == 0 else bo][:, (h // 2) * T:(h // 2 + 1) * T], lhsT=ktc[rows, h // 2, :],
                                           rhs=QTs[rows, g * 4 + h // 2, bs * T:(bs + 1) * T], start=True, stop=True)
                        return ins
                    pe(smm, [ktb, bf("QTs")], [PSB[be], PSB[bo]])
                    es4 = es[:, :].rearrange("p (c h t) -> p c h t", h=2, t=T)
                    for hh in range(2):
                        bk = be if hh == 0 else bo
                        act(lambda e, hh=hh, bk=bk: e.activation(out=es4[:, :, hh, :], in_=PS[bk][:, 0:4 * T].rearrange("p (c t) -> p c t", t=T),
                                                                 func=AF.Exp, scale=0.125), [PSB[bk]], [esb])
                    msk = bass.AP(tensor=MS[:].tensor, offset=MS[0, mi, 0].offset, ap=[list(MS[:].ap[0]), [0, 8], [1, T]])
                    dve(lambda e: e.tensor_tensor(out=es[:, :].rearrange("p (h t) -> p h t", t=T), in0=es[:, :].rearrange("p (h t) -> p h t", t=T),
                                                  in1=msk, op=ALU.mult), [esb, bf("MS")], [esb])

                    def pmm(e):
                        ins = None
                        for h in range(8):
                            vcol = slice(0, 128) if h % 2 == 0 else slice(64, 192)
                            ins = e.matmul(acc[:, h * T:(h + 1) * T], lhsT=vac[:, h // 2, vcol], rhs=es[:, h * T:(h + 1) * T], start=first[0], stop=False,
                                           skip_group_check=True)
                            first[0] = False
                        return ins
                    pe(pmm, [vab_, esb], [accb])
                spend.append(rest)
                while len(spend) > 1:
                    spend.pop(0)()
            while spend:
                spend.pop(0)()
            if DBG_SA <= 3:
                return
            for g in range(3):
                def smm(e, g=g):
                    ins = None
                    for h in (0, 2, 4, 6, 1, 3, 5, 7):
                        rows = slice((h % 2) * 64, (h % 2) * 64 + 64)
                        ins = e.matmul(PS[7 - (h % 2)][0:T, (h // 2) * T:(h // 2 + 1) * T], lhsT=KTs[rows, g * 4 + h // 2, bs * T:(bs + 1) * T],
                                       rhs=QTs[rows, g * 4 + h // 2, bs * T:(bs + 1) * T], start=True, stop=True)
                    return ins
                pe(smm, [bf("KTs"), bf("QTs")], [PSB[7], PSB[6]])
                if DBG_SA == 35:
                    continue
                Esn4 = Esn[:, :].rearrange("p (c h t) -> p c h t", h=2, t=T)
                for hh in range(2):
                    act(lambda e, hh=hh: e.activation(out=Esn4[:, :, hh, :], in_=PS[7 - hh][0:T, 0:4 * T].rearrange("p (c t) -> p c t", t=T),
                                                      func=AF.Exp, scale=0.125), [PSB[7 - hh]], [bf("Esn")])
                msk = bass.AP(tensor=MN[:].tensor, offset=MN[0, g, 0].offset, ap=[list(MN[:].ap[0]), [0, 8], [1, T]])
                dve(lambda e, msk=msk: e.tensor_tensor(out=Esn[:, :].rearrange("p (h t) -> p h t", t=T), in0=Esn[:, :].rearrange("p (h t) -> p h t", t=T),
                                                       in1=msk, op=ALU.mult), [bf("Esn"), bf("MN")], [bf("Esn")])
                if DBG_SA == 36:
                    continue

                def pmm(e, g=g):
                    ins = None
                    for h in range(8):
                        vcol = slice(0, 128) if h % 2 == 0 else slice(64, 192)
                        ins = e.matmul(acc[:, h * T:(h + 1) * T], lhsT=VAnB[bs][:, g, h // 2, vcol], rhs=Esn[:, h * T:(h + 1) * T], start=False, stop=False,
                                       skip_group_check=True)
                    return ins
                pe(pmm, [bf("VAnB"), bf("Esn")], [accb])
            if DBG_SA <= 4 or DBG_SA in (35, 36):
                return
            rz, rzb = t32()
            for hh in range(2):
                urow = slice(hh * 64, hh * 64 + 64)
                zrow = slice(64, 128) if hh == 0 else slice(0, 64)
                a3 = acc[:, 0:64].rearrange("p (c h t) -> p c h t", h=2, t=T)
                r3 = rz[:, 0:32].rearrange("p (c t) -> p c t", t=T)
                act(lambda e, urow=urow, zrow=zrow, hh=hh: e.activation(out=r3[urow], in_=a3[zrow, :, hh, :], func=AF.Ln), [accb], [rzb])
                act(lambda e, urow=urow: e.activation(out=r3[urow], in_=r3[urow], func=AF.Exp, scale=-1.0), [rzb], [rzb])
                dve(lambda e, urow=urow, hh=hh: e.tensor_tensor(out=ybs[urow, :, bs * T:(bs + 1) * T], in0=a3[urow, :, hh, :], in1=r3[urow], op=ALU.mult),
                    [accb, rzb], [bf("ybs")])

        ng = 0
        for b in range(NBP):
            for l in range(DEPTH):
                for tg in range(4):
                    if ng < DBG_NG:
                        prompt_group(b, l, tg)
                    ng += 1
        if NBS > 0 and DBG_SAMPLE:
            for l in range(DEPTH):
                sample_layer(l)
        sc.finish()

    return nc


_NC_CACHE = {}


def _run(inputs, NBP, NBS, ncores):
    key = (NBP, NBS)
    if key not in _NC_CACHE:
        _NC_CACHE[key] = build(NBP, NBS)
    nc = _NC_CACHE[key]
    f = lambda a: np.ascontiguousarray(np.asarray(a, dtype=np.float32))
    cst = _consts()
    shared = {k: f(inputs[k]) for k in ("w_ada", "b_ada", "norm_g", "w_in", "gm_ln_g", "gm_ln_b", "gm_ws", "gm_bs",
                                        "w_gm_out", "w_att_out", "w_o", "final_g")}
    shared.update(cst)
    xp, xs = f(inputs["x_prompt"]), f(inputs["x_sample"])
    cp, cs = f(inputs["c_prompt"]), f(inputs["c_sample"])
    caches = [f(inputs["cache_kv_w128"]), f(inputs["cache_kv_w512"]), f(inputs["cache_kv_w2048"])]
    in_maps = []
    for i in range(ncores):
        m = dict(shared)
        m["x_p"] = np.ascontiguousarray(xp[i * NBP:(i + 1) * NBP])
        m["x_s"] = np.ascontiguousarray(xs[i * NBS:(i + 1) * NBS].reshape(NBS * T, D))
        m["c_all"] = np.ascontiguousarray(np.concatenate([cp[i * NBP:(i + 1) * NBP], cs[i * NBS:(i + 1) * NBS]], 0))
        for g in range(3):
            cg = caches[g][:, i * NBS:(i + 1) * NBS]
            m[f"ck{g}"] = np.ascontiguousarray(cg.reshape(DEPTH, NBS, cg.shape[2], 2, 512))
        in_maps.append(m)
    res = run_bass_kernel_spmd(nc, in_maps, core_ids=list(range(ncores)))
    R = res.results
    y_p = np.concatenate([r["y_p"] for r in R], 0)
    y_s = np.concatenate([r["y_s"].reshape(NBS, T, D) for r in R], 0)
    outs = [y_p, y_s]
    for g in range(3):
        keep = DILS[g][0]
        outs.append(np.concatenate([r[f"kvp{g}"].reshape(DEPTH, NBP, keep, 2, 8, 64) for r in R], 1))
    for g in range(3):
        outs.append(np.concatenate([r[f"kvs{g}"].reshape(DEPTH, NBS, T, 2, 8, 64) for r in R], 1))
    outs.append(np.concatenate([r["gmv"].reshape(DEPTH, NBS, T, D) for r in R], 1))
    return tuple(np.ascontiguousarray(o.astype(np.float32)) for o in outs)


def kernel(**inputs):
    return _run(inputs, 2, 4, NCORES)
```

```python
import numpy as np
from contextlib import ExitStack
import concourse.bass as bass
import concourse.mybir as mybir
from concourse.bass_utils import run_bass_kernel_spmd

F32, BF16 = mybir.dt.float32, mybir.dt.bfloat16
AF = mybir.ActivationFunctionType
ALU = mybir.AluOpType

D = 1024
S = 2048
DEPTH = 2
T = 8
PAST = 16384
NCORES = 8
INW = 10240
OU, OV, OZA, OQ, OK_, OVAL, OZB, OGA, OGB = 0, 1024, 2048, 3072, 4608, 6144, 7680, 8192, 9216
DILS = ((128, 1), (512, 4), (2048, 16))
ENG = ("pe", "act", "dve", "pool", "sp")
EPOCH = 30000
NDS = 24


class Buf:
    __slots__ = ("w", "r")

    def __init__(self):
        self.w = None
        self.r = {}


class Sched:
    def __init__(self, nc, es):
        self.nc = nc
        self.eh = {"pe": nc.tensor, "act": nc.scalar, "dve": nc.vector, "pool": nc.gpsimd, "sp": nc.sync}
        self.cnt = {e: 0 for e in ENG}
        self.sems = {e: [es.enter_context(nc.semaphore(f"s_{e}{i}")) for i in range(3)] for e in ENG}
        self.dsem = [es.enter_context(nc.semaphore(f"d{i}")) for i in range(2 * NDS)]
        self.dtgt = [0] * (2 * NDS)
        self.rr = {"sp": 0, "pool": 0}
        self.waited = {}

    def _waits(self, eng, deps):
        out = []
        for tok in deps:
            if tok[0] == "e":
                key = (eng, "e", tok[1])
                val = (tok[2], tok[3])
                sem = self.sems[tok[1]][tok[2]]
                v = tok[3]
            else:
                key = (eng, "d", tok[1])
                val = (0, tok[2])
                sem = self.dsem[tok[1]]
                v = tok[2]
            if self.waited.get(key, (-1, -1)) >= val:
                continue
            self.waited[key] = val
            out.append((sem, v))
        return out

    def op(self, eng, fn, reads=(), writes=(), dma=False):
        deps = []
        for b in reads:
            if b.w is not None:
                deps.append(b.w)
        for b in writes:
            if b.w is not None:
                deps.append(b.w)
            deps.extend(b.r.values())
        if dma:
            k = self.rr[eng] % NDS + (NDS if eng == "pool" else 0)
            self.rr[eng] += 1
            old = self.dtgt[k]
            self.dtgt[k] += 16
            if old > 0:
                deps.append(("d", k, old))
            tok = ("d", k, self.dtgt[k])
            sem, inc = self.dsem[k], 16
        else:
            c = self.cnt[eng]
            self.cnt[eng] += 1
            ep, v = divmod(c, EPOCH)
            tok = ("e", eng, ep, v + 1)
            sem, inc = self.sems[eng][ep], 1
        waits = self._waits(eng, deps)

        e = self.eh[eng]
        for s_, v_ in waits:
            e.wait_ge(s_, v_)
        fn(e).then_inc(sem, inc)
        rk = ("e", eng) if not dma else ("d", tok[1])
        for b in reads:
            b.r[rk] = tok
        for b in writes:
            b.w = tok
            b.r = {}
        return tok

    def finish(self):
        deps = [("d", k, self.dtgt[k]) for k in range(2 * NDS) if self.dtgt[k] > 0]
        for e in ENG:
            if e != "sp" and self.cnt[e] > 0:
                ep, v = divmod(self.cnt[e] - 1, EPOCH)
                deps.append(("e", e, ep, v + 1))
        waits = self._waits("sp", deps)

        for s_, v_ in waits:
            self.eh["sp"].wait_ge(s_, v_)


def _consts():
    half = 32
    inv = (10000.0 ** (-np.arange(half, dtype=np.float32) / np.float32(half))).astype(np.float32)

    def tab(pos):
        ang = (pos.astype(np.float32)[:, None] * inv[None, :]).astype(np.float32)
        c = np.cos(ang.astype(np.float64)).astype(np.float32)
        s = np.sin(ang.astype(np.float64)).astype(np.float32)
        return np.concatenate([c, c], 1), np.concatenate([-s, s], 1)

    cc, ss = tab(np.arange(S))
    ccs, sss = tab(PAST + np.arange(T))
    k = np.arange(128)[:, None]
    q = np.arange(128)[None, :]
    prev = (k >= q).astype(np.float32)
    cur = (k <= q).astype(np.float32)
    mb = np.concatenate([prev, cur, prev, cur], 1)
    mg = np.zeros((4, 128, 512), np.float32)
    for tq in range(4):
        m = (np.arange(128)[:, None] <= (32 * tq + np.arange(32))[None, :]).astype(np.float32)
        mg[tq] = np.tile(m, (1, 16))
    tri = (np.arange(128)[:, None] >= np.arange(128)[None, :]).astype(np.float32)
    tt = np.arange(T)[None, :]
    rows = np.arange(128)[:, None]
    ms = np.zeros((13, 128, T), np.float32)
    ms[0] = (rows >= tt)
    for r in range(4):
        ms[1 + r] = ((tt % 4) == r) & ((tt < 4) | (rows >= 1))
    for r in range(8):
        ms[5 + r] = (tt == r) & (rows >= 0)
    tk = np.arange(T)[:, None]
    mn = np.zeros((3, T, T), np.float32)
    mn[0] = (tk <= tt)
    mn[1] = (tk <= tt) & (((tt - tk) % 4) == 0)
    mn[2] = (tk == tt)
    return dict(
        c_cc=np.ascontiguousarray(cc), c_ss=np.ascontiguousarray(ss),
        c_ccs=np.ascontiguousarray(np.tile(ccs, (4, 1))), c_sss=np.ascontiguousarray(np.tile(sss, (4, 1))),
        c_mb=mb, c_mg=mg, c_tri=tri, c_id=np.eye(128, dtype=np.float32),
        c_ms=np.ascontiguousarray(ms), c_mn=np.ascontiguousarray(mn),
    )


DBG_STOP = ''
DBG_NG = 99
DBG_SAMPLE = True
DBG_SSTOP = ''
DBG_SA = 9


def build(NBP=2, NBS=4):
    nc = bass.Bass("TRN2", target_bir_lowering=False)
    NC6 = NBP + NBS
    TS = NBS * T

    def din(name, shape, dt=F32):
        return nc.dram_tensor(name, list(shape), dt, kind="ExternalInput").ap()

    def dout(name, shape):
        return nc.dram_tensor(name, list(shape), F32, kind="ExternalOutput").ap()

    def dint(name, shape, dt):
        return nc.dram_tensor(name, list(shape), dt, kind="Internal").ap()

    x_p = din("x_p", [NBP, S, D]); x_s = din("x_s", [TS, D])
    c_all = din("c_all", [NC6, D])
    ck = [din("ck0", [DEPTH, NBS, 128, 2, 512]), din("ck1", [DEPTH, NBS, 512, 2, 512]),
          din("ck2", [DEPTH, NBS, 2048, 2, 512])]
    w_ada = din("w_ada", [DEPTH, D, 3 * D]); b_ada = din("b_ada", [DEPTH, 3 * D])
    norm_g = din("norm_g", [DEPTH, D]); w_in = din("w_in", [DEPTH, D, INW])
    gm_ln_g = din("gm_ln_g", [DEPTH, D]); gm_ln_b = din("gm_ln_b", [DEPTH, D])
    gm_ws = din("gm_ws", [DEPTH, 8, 128, 128]); gm_bs = din("gm_bs", [DEPTH, 8, 128])
    w_gm = din("w_gm_out", [DEPTH, D, D]); w_att = din("w_att_out", [DEPTH, 512, D])
    w_o = din("w_o", [DEPTH, D, D]); final_g = din("final_g", [D])
    c_cc = din("c_cc", [S, 64]); c_ss = din("c_ss", [S, 64])
    c_ccs = din("c_ccs", [32, 64]); c_sss = din("c_sss", [32, 64])
    c_mb = din("c_mb", [128, 512]); c_mg = din("c_mg", [4, 128, 512]); c_tri = din("c_tri", [128, 128])
    c_id = din("c_id", [128, 128]); c_ms = din("c_ms", [13, 128, T]); c_mn = din("c_mn", [3, T, T])

    y_p = dout("y_p", [NBP, S, D]); y_s = dout("y_s", [TS, D])
    kvp = [dout("kvp0", [DEPTH, NBP, 128, 2, 512]), dout("kvp1", [DEPTH, NBP, 512, 2, 512]),
           dout("kvp2", [DEPTH, NBP, 2048, 2, 512])]
    kvs = [dout(f"kvs{g}", [DEPTH, TS, 2, 512]) for g in range(3)]
    gmv = dout("gmv", [DEPTH, TS, D])

    WIN = dint("WIN", [DEPTH, D, INW], BF16)
    WGM = dint("WGM", [DEPTH, 8, 128, 8, 128], BF16)
    WATT = dint("WATT", [DEPTH, 8, 128, 4, 128], BF16)
    WO = dint("WO", [DEPTH, 8, 128, 8, 128], BF16)
    XS = dint("XS", [8, 128, S], F32)
    KH = dint("KH", [12, 128, S], BF16)
    VH = dint("VH", [S, 3, 4, 192], BF16)
    WMT = dint("WMT", [DEPTH, 8, 128, 128], BF16)

    es = ExitStack()
    with es:
        def sb(name, shape, dt=F32):
            return es.enter_context(nc.sbuf_tensor(name, list(shape), dt))

        sc = Sched(nc, es)
        PS = [es.enter_context(nc.psum_tensor(f"ps{i}", [128, 512], F32)) for i in range(8)]
        PSB = [Buf() for _ in range(8)]

        ident_f = sb("ident_f", [128, 128]); ident_b = sb("ident_b", [128, 128], BF16)
        MB = sb("MB", [128, 512], BF16); MG = sb("MG", [128, 4, 512], BF16)
        TRI = sb("TRI", [128, 128])
        wmT = sb("wmT", [128, DEPTH, 8, 128], BF16)
        SV = sb("SV", [128, 128]); SVT = sb("SVT", [128, 72])
        modT = sb("modT", [128, DEPTH, 24, NC6])
        Am = sb("Am", [128, DEPTH, 8, NC6]); G4 = sb("G4", [128, DEPTH, 8, NC6])
        cT = sb("cT", [128, 8, NC6]); scT = sb("scT", [128, 8, NC6], BF16)
        LG = sb("LG", [128, D]); LB = sb("LB", [128, D]); BSR = sb("BSR", [128, D])
        xT = sb("xT", [128, 8, 512]); hT = sb("hT", [128, 8, 512], BF16)
        rstd = sb("rstd", [128, 512])
        NWB = 4
        WB = [sb(f"WB{i}", [128, 8, 512], BF16) for i in range(NWB)]
        NWS = 5
        WBS = [sb(f"WBS{i}", [128, 8, 128], BF16) for i in range(NWS)]
        NT = 8
        T32 = [sb(f"T32_{i}", [128, 512]) for i in range(NT)]
        NTB = 5
        TB = [sb(f"TB_{i}", [128, 512], BF16) for i in range(NTB)]
        XL = [sb(f"XL{i}", [128, D]) for i in range(2)]
        QT = sb("QT", [128, 12, 512], BF16); KT = sb("KT", [128, 12, 512], BF16)
        MGT = QT[:, 0:8, :]
        BIG = KT[:, 0:8, :]
        MBf = XL[0][:, 0:512]
        call_t = XL[1][0:NC6, :]
        KHt = [sb("KH0", [128, 640], BF16), sb("KH1", [128, 1024], BF16), sb("KH2", [128, 2048], BF16)]
        VAs = [sb(f"VA{i}", [128, 29, 192], BF16) for i in range(2)]
        vn = sb("vn", [128, 4, D], BF16)
        VST = [sb(f"VST{i}", [128, 4, 192], BF16) for i in range(2)]
        ybT = sb("ybT", [128, 4, 512], BF16)
        CC = sb("CC", [128, 4, 64]); SSn = sb("SSn", [128, 4, 64])
        st6 = sb("st6", [128, 4, 2, 6]); mv = sb("mv", [128, 4, 2]); lnr = sb("lnr", [128, 4])
        xsT = sb("xsT", [128, 8, TS]); hsT = sb("hsT", [128, 8, TS], BF16)
        QTs = sb("QTs", [128, 12, TS], BF16); KTs = sb("KTs", [128, 12, TS], BF16)
        CT = XL
        KTc = sb("KTc", [128, 4, 128], BF16); VAc = sb("VAc", [128, 4, 192], BF16)
        KTcB = sb("KTcB", [128, 4, 128], BF16); VAcB = sb("VAcB", [128, 4, 192], BF16); EsB = sb("EsB", [128, 64], BF16)
        MS = sb("MS", [128, 13, T], BF16); MN = sb("MN", [T, 3, T], BF16)
        MSf = sb("MSf", [128, 13, T]); MNf = sb("MNf", [T, 3, T])
        CCs = sb("CCs", [32, 64]); SSs = sb("SSs", [32, 64])
        WS = sb("WS", [32, 8, 32], BF16)
        ybs = sb("ybs", [128, 4, TS], BF16)
        Es = sb("Es", [128, 64], BF16); Esn = sb("Esn", [T, 64], BF16)

        B = {}

        ALIAS = {"MGT": "QT", "BIG": "KT", "MBf": "XL0", "call": "XL1", "CT0": "XL0", "CT1": "XL1"}

        def bf(name):
            name = ALIAS.get(name, name)
            if name not in B:
                B[name] = Buf()
            return B[name]

        t32b = [Buf() for _ in range(NT)]
        tbb = [Buf() for _ in range(NTB)]
        t32i = [0]
        tbi = [0]

        def t32():
            i = t32i[0] % NT
            t32i[0] += 1
            return T32[i], t32b[i]

        def tb16():
            i = tbi[0] % NTB
            tbi[0] += 1
            return TB[i], tbb[i]

        wbb = [Buf() for _ in range(NWB)]
        wbi = [0]
        wsb = [Buf() for _ in range(NWS)]
        wsi = [0]

        def pe(fn, r, w):
            return sc.op("pe", fn, r, w)

        def act(fn, r, w):
            return sc.op("act", fn, r, w)

        def dve(fn, r, w):
            return sc.op("dve", fn, r, w)

        def pool(fn, r, w):
            return sc.op("pool", fn, r, w)

        def dma_in(out, in_, r, w):
            return sc.op("sp", lambda e: e.dma_start(out=out, in_=in_), r, w, dma=True)

        def dma_g(out, in_, r, w):
            return sc.op("pool", lambda e: e.dma_start(out=out, in_=in_), r, w, dma=True)

        def bcast_rows(ap1d, n):
            return bass.AP(tensor=ap1d.tensor, offset=ap1d.offset, ap=[[0, 128], [1, n]])

        dma_in(ident_f[:], c_id[:, :], [], [bf("ident_f")])
        pool(lambda e: e.tensor_copy(out=ident_b[:], in_=ident_f[:]), [bf("ident_f")], [bf("ident_b")])
        dma_in(MBf, c_mb[:, :], [], [bf("MBf")])
        pool(lambda e: e.tensor_copy(out=MB[:], in_=MBf), [bf("MBf")], [bf("MB")])
        for tq in range(4):
            dma_in(MBf, c_mg[tq], [], [bf("MBf")])
            pool(lambda e, tq=tq: e.tensor_copy(out=MG[:, tq, :], in_=MBf), [bf("MBf")], [bf("MG")])
        dma_in(TRI[:], c_tri[:, :], [], [bf("TRI")])
        dma_in(MSf[:], c_ms.rearrange("k p t -> p k t"), [], [bf("MSf")])
        pool(lambda e: e.tensor_copy(out=MS[:], in_=MSf[:]), [bf("MSf")], [bf("MS")])
        dma_in(MNf[:], c_mn.rearrange("k p t -> p k t"), [], [bf("MNf")])
        pool(lambda e: e.tensor_copy(out=MN[:], in_=MNf[:]), [bf("MNf")], [bf("MN")])
        dma_in(CCs[:], c_ccs[:, :], [], [bf("CCs")])
        dma_in(SSs[:], c_sss[:, :], [], [bf("SSs")])
        for i in range(2):
            pool(lambda e, i=i: e.memset(VAs[i][:], 1.0), [], [bf(f"VA{i}")])
            pool(lambda e, i=i: e.memset(VST[i][:], 1.0), [], [bf(f"VST{i}")])
        vsti = [0]
        pool(lambda e: e.memset(VAc[:], 1.0), [], [bf("VAc0")])
        pool(lambda e: e.memset(VAcB[:], 1.0), [], [bf("VAc1")])
        VAnB1 = sb("VAnB", [T, 3, 4, 192], BF16)
        VAnB = [VAnB1 for _ in range(max(NBS, 1))]
        pool(lambda e: e.memset(VAnB1[:], 1.0), [], [bf("VAnB")])
        pool(lambda e: e.memset(WS[:], 0.0), [], [bf("WS")])
        pool(lambda e: e.memset(KHt[2][:], 0.0), [], [bf("KH2")])

        wconv = bf("wconv")
        for l in range(DEPTH):
            for kc in range(8):
                dma_g(WIN[l, kc * 128:(kc + 1) * 128, :], w_in[l, kc * 128:(kc + 1) * 128, :], [], [])
                dma_g(bass.AP(tensor=WGM.tensor, offset=WGM[l, 0, 0, kc, 0].offset, ap=[[8 * 128, 128], [128 * 8 * 128, 8], [1, 128]]),
                      w_gm[l, kc * 128:(kc + 1) * 128, :].rearrange("p (o j) -> p o j", j=128), [], [])
                dma_g(bass.AP(tensor=WO.tensor, offset=WO[l, 0, 0, kc, 0].offset, ap=[[8 * 128, 128], [128 * 8 * 128, 8], [1, 128]]),
                      w_o[l, kc * 128:(kc + 1) * 128, :].rearrange("p (o j) -> p o j", j=128), [], [])
            for kc in range(4):
                dma_g(bass.AP(tensor=WATT.tensor, offset=WATT[l, 0, 0, kc, 0].offset, ap=[[4 * 128, 128], [128 * 4 * 128, 8], [1, 128]]),
                      w_att[l, kc * 128:(kc + 1) * 128, :].rearrange("p (o j) -> p o j", j=128), [], [])
        conv_deps = [("d", k, sc.dtgt[k]) for k in range(2 * NDS) if sc.dtgt[k] > 0]
        wts = sc._waits("pool", conv_deps)

        for s_, v_ in wts:
            nc.gpsimd.wait_ge(s_, v_)
        pool(lambda e: e.memset(Es[:], 0.0), [], [wconv])

        dma_in(SV[0:48, :], b_ada.rearrange("l (j p) -> (l j) p", p=128), [], [bf("SV")])
        dma_in(SV[48:64, :], norm_g.rearrange("l (j p) -> (l j) p", p=128), [], [bf("SV")])
        dma_in(SV[64:72, :], final_g.rearrange("(j p) -> j p", p=128), [], [bf("SV")])
        pe(lambda e: e.transpose(out=PS[0][:, 0:72], in_=SV[0:72, :], identity=ident_f[0:72, 0:72]),
           [bf("SV"), bf("ident_f")], [PSB[0]])
        dve(lambda e: e.tensor_copy(out=SVT[:], in_=PS[0][:, 0:72]), [PSB[0]], [bf("SVT")])

        for l in range(DEPTH):
            for g in range(8):
                tt_, tt_b = t32()
                dma_in(tt_[:, 0:128], gm_ws[l, g], [], [tt_b])
                dve(lambda e, tt_=tt_: e.tensor_tensor(out=tt_[:, 128:256], in0=tt_[:, 0:128], in1=TRI[:], op=ALU.mult),
                    [tt_b, bf("TRI")], [tt_b])
                pe(lambda e, tt_=tt_: e.transpose(out=PS[1][:, 0:128], in_=tt_[:, 128:256], identity=ident_f[:]),
                   [tt_b, bf("ident_f")], [PSB[1]])
                dve(lambda e, l=l, g=g: e.tensor_copy(out=wmT[:, l, g, :], in_=PS[1][:, 0:128]), [PSB[1]], [bf("wmT")])
        dma_g(WMT.rearrange("l g s t -> s l g t"), wmT[:], [bf("wmT")], [bf("WMT")])

        dma_in(call_t, c_all[:, :], [], [bf("call")])
        for kc in range(8):
            pe(lambda e, kc=kc: e.transpose(out=PS[0][:, kc * 8:kc * 8 + NC6], in_=call_t[:, kc * 128:(kc + 1) * 128],
                                            identity=ident_f[0:NC6, 0:NC6]), [bf("call"), bf("ident_f")], [PSB[0]])
        ps0v = PS[0][:, 0:64].rearrange("p (k i) -> p k i", i=8)[:, :, 0:NC6]
        dve(lambda e: e.tensor_copy(out=cT[:], in_=ps0v), [PSB[0]], [bf("cT")])
        tt_, tt_b = t32()
        ttv = tt_[:, 0:8 * NC6].rearrange("p (k i) -> p k i", i=NC6)
        act(lambda e: e.activation(out=ttv, in_=cT[:], func=AF.Tanh, scale=0.5), [bf("cT")], [tt_b])
        dve(lambda e: e.scalar_tensor_tensor(out=ttv, in0=ttv, scalar=1.0, in1=cT[:], op0=ALU.add, op1=ALU.mult),
            [tt_b, bf("cT")], [tt_b])
        dve(lambda e: e.tensor_scalar(out=scT[:], in0=ttv, scalar1=0.5, scalar2=None, op0=ALU.mult), [tt_b], [bf("scT")])
        for l in range(DEPTH):
            for cb in range(6):
                i = wbi[0] % NWB
                wbi[0] += 1
                src = bass.AP(tensor=w_ada.tensor, offset=w_ada[l, 0, cb * 512].offset,
                              ap=[[3 * D, 128], [128 * 3 * D, 8], [1, 512]])
                dma_g(WB[i][:], src, [], [wbb[i]])
                for ch in range(4):
                    j = cb * 4 + ch

                    def mm(e, i=i, ch=ch):
                        ins = None
                        for kc in range(8):
                            ins = e.matmul(PS[2][:, 0:NC6], lhsT=WB[i][:, kc, ch * 128:(ch + 1) * 128], rhs=scT[:, kc, :],
                                           start=(kc == 0), stop=(kc == 7))
                        return ins
                    pe(mm, [wbb[i], bf("scT")], [PSB[2]])
                    dve(lambda e, l=l, j=j: e.tensor_scalar(out=modT[:, l, j, :], in0=PS[2][:, 0:NC6],
                                                            scalar1=SVT[:, l * 24 + j:l * 24 + j + 1], scalar2=None, op0=ALU.add),
                        [PSB[2], bf("SVT")], [bf("modT")])
            for kc in range(8):
                dve(lambda e, l=l, kc=kc: e.tensor_scalar(out=Am[:, l, kc, :], in0=modT[:, l, 8 + kc, :], scalar1=1.0,
                                                          scalar2=SVT[:, 48 + l * 8 + kc:48 + l * 8 + kc + 1],
                                                          op0=ALU.add, op1=ALU.mult), [bf("modT"), bf("SVT")], [bf("Am")])
                dve(lambda e, l=l, kc=kc: e.tensor_scalar(out=G4[:, l, kc, :], in0=modT[:, l, 16 + kc, :], scalar1=0.25,
                                                          scalar2=None, op0=ALU.mult), [bf("modT")], [bf("G4")])

        def wblock(src3, l, k0, nk, col0, ncols=512):
            if ncols == 128:
                i = wsi[0] % NWS
                wsi[0] += 1
                dma_in(WBS[i][:, 0:nk, :], src3[l, col0 // 128], [wconv], [wsb[i]])
                return WBS[i], wsb[i]
            ncol_total = src3.shape[2]
            src = bass.AP(tensor=src3.tensor, offset=src3[l, k0 * 128, col0].offset,
                          ap=[[ncol_total, 128], [128 * ncol_total, nk], [1, ncols]])
            i = wbi[0] % NWB
            wbi[0] += 1
            dma_in(WB[i][:, 0:nk, 0:ncols], src, [wconv], [wbb[i]])
            return WB[i], wbb[i]

        def rms_rstd(src, srcb, n, gsel):
            act(lambda e: e.activation(out=BIG[:, :, 0:n], in_=src, func=AF.Square), [srcb], [bf("BIG")])

            def mm(e):
                ins = None
                for kc in range(8):
                    ins = e.matmul(PS[7][:, 0:n], lhsT=gsel, rhs=BIG[:, kc, 0:n], start=(kc == 0), stop=(kc == 7))
                return ins
            pe(mm, [bf("BIG"), bf("ones")], [PSB[7]])
            dve(lambda e: e.tensor_scalar(out=rstd[:, 0:n], in0=PS[7][:, 0:n], scalar1=1.0 / D, scalar2=1e-6,
                                          op0=ALU.mult, op1=ALU.add), [PSB[7]], [bf("rstd")])
            act(lambda e: e.activation(out=rstd[:, 0:n], in_=rstd[:, 0:n], func=AF.Ln), [bf("rstd")], [bf("rstd")])
            act(lambda e: e.activation(out=rstd[:, 0:n], in_=rstd[:, 0:n], func=AF.Exp, scale=-0.5), [bf("rstd")], [bf("rstd")])

        ones_b = sb("ones_b", [128, 128], BF16)
        pool(lambda e: e.memset(ones_b[:], 1.0), [], [bf("ones")])
        for tb_ in range(S // 128):
            dma_g(bass.AP(tensor=VH.tensor, offset=VH[tb_ * 128, 0, 0, 64].offset, ap=[[2304, 128], [192, 12], [1, 64]]),
                  bass.AP(tensor=ones_b[:].tensor, offset=ones_b[:].offset, ap=[list(ones_b[:].ap[0]), [0, 12], [1, 64]]),
                  [bf("ones")], [bf("VH")])

        def load_layer_vecs(l):
            dma_in(LG[:], bcast_rows(gm_ln_g[l], D), [], [bf("LG")])
            dma_in(LB[:], bcast_rows(gm_ln_b[l], D), [], [bf("LB")])
            dma_in(BSR[:], bcast_rows(gm_bs[l].rearrange("g t -> (g t)"), D), [], [bf("BSR")])

        def make_h(xsrc, xb_, n, l, cidx_of_col):
            dst = hT if n == 512 else hsT
            dstb = bf("hT") if n == 512 else bf("hsT")
            for kc in range(8):
                tt_, tt_b = t32()
                dve(lambda e, kc=kc, tt_=tt_: e.tensor_tensor(out=tt_[:, 0:n], in0=xsrc[:, kc, 0:n], in1=rstd[:, 0:n], op=ALU.mult),
                    [xb_, bf("rstd")], [tt_b])
                for (c0, c1, ci) in cidx_of_col:
                    dve(lambda e, kc=kc, tt_=tt_, c0=c0, c1=c1, ci=ci: e.tensor_scalar(
                        out=dst[:, kc, c0:c1], in0=tt_[:, c0:c1], scalar1=Am[:, l, kc, ci:ci + 1],
                        scalar2=modT[:, l, kc, ci:ci + 1], op0=ALU.mult, op1=ALU.add),
                        [tt_b, bf("Am"), bf("modT")], [dstb])

        def rope_block(ps_ap, cc_ap, ss_ap, npart, want_f32):
            m1, m1b = t32()
            m2, m2b = t32()
            p3 = ps_ap.rearrange("p (h d) -> p h d", d=64)
            m13 = m1[0:npart, :].rearrange("p (h d) -> p h d", d=64)
            m23 = m2[0:npart, :].rearrange("p (h d) -> p h d", d=64)
            ccb = bass.AP(tensor=cc_ap.tensor, offset=cc_ap.offset, ap=[list(cc_ap.ap[0]), [0, 8], [1, 64]])
            ss1 = bass.AP(tensor=ss_ap.tensor, offset=ss_ap.offset, ap=[list(ss_ap.ap[0]), [0, 8], [1, 32]])
            ss2 = bass.AP(tensor=ss_ap.tensor, offset=ss_ap.offset + 32, ap=[list(ss_ap.ap[0]), [0, 8], [1, 32]])
            return m1, m1b, m2, m2b, p3, m13, m23, ccb, ss1, ss2

        cur_l = [-1]

        def prompt_group(b, l, tg):
            T0 = tg * 512
            ci = b
            if cur_l[0] != l:
                load_layer_vecs(l)
                cur_l[0] = l
            xTb = bf("xT")
            if l == 0:
                for ti in range(4):
                    xl = XL[ti % 2]
                    xlb = bf(f"XL{ti % 2}")
                    dma_in(xl[:], x_p[b, T0 + ti * 128:T0 + (ti + 1) * 128, :], [], [xlb])
                    for hf in range(2):
                        pi = 4 + hf

                        def tr(e, xl=xl, hf=hf, pi=pi):
                            ins = None
                            for k in range(4):
                                kc = hf * 4 + k
                                ins = e.transpose(out=PS[pi][:, k * 128:(k + 1) * 128], in_=xl[:, kc * 128:(kc + 1) * 128],
                                                  identity=ident_f[:])
                            return ins
                        pe(tr, [xlb, bf("ident_f")], [PSB[pi]])
                        act(lambda e, hf=hf, pi=pi, ti=ti: e.activation(
                            out=xT[:, hf * 4:hf * 4 + 4, ti * 128:(ti + 1) * 128],
                            in_=PS[pi][:, :].rearrange("p (k t) -> p k t", t=128), func=AF.Copy), [PSB[pi]], [xTb])
            else:
                dma_in(xT[:], XS[:, :, T0:T0 + 512].rearrange("k p t -> p k t"), [bf("XS")], [xTb])
            if DBG_STOP == 'X':
                return
            rms_rstd(xT[:], xTb, 512, ones_b[:])
            make_h(xT, xTb, 512, l, [(0, 512, ci)])
            dma_in(CC[:], c_cc[T0:T0 + 512, :].rearrange("(i p) d -> p i d", p=128), [], [bf("CC")])
            dma_in(SSn[:], c_ss[T0:T0 + 512, :].rearrange("(i p) d -> p i d", p=128), [], [bf("SS")])
            if DBG_STOP == 'H':
                return
            qk_pending = []
            for cb in range(6):
                isk = cb >= 3
                g = cb % 3
                W, Wb = wblock(WIN, l, 0, 8, OQ + cb * 512)
                for ti in range(4):
                    pi = ti % 4

                    def mm(e, W=W, ti=ti, pi=pi):
                        ins = None
                        for kc in range(8):
                            ins = e.matmul(PS[pi][:, :], lhsT=hT[:, kc, ti * 128:(ti + 1) * 128], rhs=W[:, kc, :],
                                           start=(kc == 0), stop=(kc == 7))
                        return ins
                    pe(mm, [Wb, bf("hT")], [PSB[pi]])
                    m1, m1b, m2, m2b, p3, m13, m23, ccb, ss1, ss2 = rope_block(PS[pi][:, :], CC[:, ti, :], SSn[:, ti, :], 128, isk)
                    dve(lambda e, m13=m13, p3=p3, ccb=ccb: e.tensor_tensor(out=m13, in0=p3, in1=ccb, op=ALU.mult),
                        [PSB[pi], bf("CC")], [m1b])
                    dve(lambda e, m23=m23, p3=p3, ss1=ss1: e.tensor_tensor(out=m23[:, :, 0:32], in0=p3[:, :, 32:64], in1=ss1, op=ALU.mult),
                        [PSB[pi], bf("SS")], [m2b])
                    dve(lambda e, m23=m23, p3=p3, ss2=ss2: e.tensor_tensor(out=m23[:, :, 32:64], in0=p3[:, :, 0:32], in1=ss2, op=ALU.mult),
                        [PSB[pi], bf("SS")], [m2b])
                    ob, obb = tb16()
                    if isk:
                        pool(lambda e, m1=m1, m2=m2: e.tensor_tensor(out=m1[:], in0=m1[:], in1=m2[:], op=ALU.add), [m1b, m2b], [m1b])
                        act(lambda e, ob=ob, m1=m1: e.activation(out=ob[:], in_=m1[:], func=AF.Copy), [m1b], [obb])
                        keep = DILS[g][0]
                        t_lo = T0 + ti * 128
                        if t_lo >= S - keep:
                            r0 = t_lo - (S - keep)
                            dma_g(kvp[g][l, b, r0:r0 + 128, 0, :], m1[:], [m1b], [])
                    else:
                        pool(lambda e, m1=m1, m2=m2, ob=ob: e.tensor_tensor(out=ob[:], in0=m1[:], in1=m2[:], op=ALU.add), [m1b, m2b], [obb])
                    pj = 4 + (ti % 2)

                    def tr_and_evac(ob=ob, obb=obb, pj=pj, isk=isk, g=g, ti=ti):
                        def tr(e):
                            ins = None
                            pv = PS[pj][:, :].bitcast(BF16)
                            for c in range(4):
                                ins = e.transpose(out=pv[:, c * 128:(c + 1) * 128], in_=ob[:, c * 128:(c + 1) * 128], identity=ident_b[:])
                            return ins
                        pe(tr, [obb, bf("ident_b")], [PSB[pj]])
                        dst = KT if isk else QT
                        dstb = bf("KT") if isk else bf("QT")
                        act(lambda e: e.activation(
                            out=dst[:, g * 4:g * 4 + 4, ti * 128:(ti + 1) * 128],
                            in_=PS[pj][:, :].bitcast(BF16)[:, 0:512].rearrange("p (c t) -> p c t", t=128), func=AF.Copy),
                            [PSB[pj]], [dstb])
                    qk_pending.append(tr_and_evac)
                    while len(qk_pending) > 3:
                        qk_pending.pop(0)()
            while qk_pending:
                qk_pending.pop(0)()
            dma_g(KH[:, :, T0:T0 + 512].rearrange("c p t -> p c t"), KT[:], [bf("KT")], [bf("KH")])
            if DBG_STOP == 'QK':
                return
            for g in range(3):
                W, Wb = wblock(WIN, l, 0, 8, OVAL + g * 512)
                for ti in range(4):
                    pi = ti % 4

                    def mm(e, W=W, ti=ti, pi=pi):
                        ins = None
                        for kc in range(8):
                            ins = e.matmul(PS[pi][:, :], lhsT=hT[:, kc, ti * 128:(ti + 1) * 128], rhs=W[:, kc, :],
                                           start=(kc == 0), stop=(kc == 7))
                        return ins
                    pe(mm, [Wb, bf("hT")], [PSB[pi]])
                    vf, vfb = t32()
                    act(lambda e, vf=vf, pi=pi: e.activation(out=vf[:], in_=PS[pi][:, :], func=AF.Copy), [PSB[pi]], [vfb])
                    t_lo = T0 + ti * 128
                    keep = DILS[g][0]
                    if t_lo >= S - keep:
                        r0 = t_lo - (S - keep)
                        dma_g(kvp[g][l, b, r0:r0 + 128, 1, :], vf[:], [vfb], [])
                    vi = vsti[0] % 2
                    vsti[0] += 1
                    vst = VST[vi]
                    vdst = bass.AP(tensor=vst[:].tensor, offset=vst[:].offset, ap=[list(vst[:].ap[0]), [192, 4], [128, 2], [1, 64]])
                    dve(lambda e, vdst=vdst, vf=vf: e.tensor_copy(out=vdst, in_=vf[:, :].rearrange("p (c h d) -> p c h d", h=2, d=64)),
                        [vfb], [bf(f"VST{vi}")])
                    dma_g(VH[t_lo:t_lo + 128, g, :, :], vst[:], [bf(f"VST{vi}")], [bf("VH")])
            if DBG_STOP == 'V':
                return
            lo0 = max(0, T0 - 128)
            lo1 = max(0, T0 - 512)

            def att_load_k(c):
                dma_in(KHt[0][:, 0:T0 + 512 - lo0], KH[0 * 4 + c, :, lo0:T0 + 512], [bf("KH")], [bf("KH0")])
                dma_in(KHt[1][:, 0:T0 + 512 - lo1], KH[1 * 4 + c, :, lo1:T0 + 512], [bf("KH")], [bf("KH1")])
                dma_in(KHt[2][:, 0:T0 + 512], KH[2 * 4 + c, :, 0:T0 + 512], [bf("KH")], [bf("KH2")])

            def att_load_v(c):
                VA = VAs[c % 2]
                vab = bf(f"VA{c % 2}")
                blk0 = max(0, tg * 4 - 1)
                nb = tg * 4 + 4 - blk0
                kb0 = blk0 - (tg * 4 - 1)
                src = bass.AP(tensor=VH.tensor, offset=VH[blk0 * 128, 0, c, 0].offset, ap=[[2304, 128], [128 * 2304, nb], [1, 192]])
                dma_in(VA[:, kb0:kb0 + nb, :], src, [bf("VH")], [vab])
                for bi in ((0, 1) if tg > 0 else (1,)):
                    src = bass.AP(tensor=VH.tensor, offset=VH[(tg - 1 + bi) * 512, 1, c, 0].offset, ap=[[4 * 2304, 128], [2304, 4], [1, 192]])
                    dma_in(VA[:, 5 + 4 * bi:9 + 4 * bi, :], src, [bf("VH")], [vab])
                np_ = 32 * (tg + 1)
                src = bass.AP(tensor=VH.tensor, offset=VH[0, 2, c, 0].offset, ap=[[16 * 2304, np_], [2304, 16], [1, 192]])
                dma_in(VA[0:np_, 13:29, :], src, [bf("VH")], [vab])

            sbanks = [6, 7, 2, 3]
            ucnt = [0]
            pending = []

            def flush(keep):
                while len(pending) > keep:
                    pending.pop(0)()

            def att_pair(c):
                VA = VAs[c % 2]
                vab = bf(f"VA{c % 2}")
                for hh in range(2):
                    rows = slice(hh * 64, hh * 64 + 64)
                    acc = PS[4 + hh]
                    accb = PSB[4 + hh]
                    vcol = slice(0, 128) if hh == 0 else slice(64, 192)
                    first = [True]

                    def pv_mm(e, slot, esrc, outcols, first=first, VA=VA, vcol=vcol):
                        ins = e.matmul(outcols, lhsT=VA[:, slot, vcol], rhs=esrc, start=first[0], stop=False, skip_group_check=True)
                        first[0] = False
                        return ins
                    for g in range(2):
                        for qp in range(2):
                            pi = sbanks[ucnt[0] % 4]
                            ucnt[0] += 1
                            units = []
                            for u in range(2):
                                qi = qp * 2 + u
                                if g == 0:
                                    qcols = QT[rows, 0 * 4 + c, qi * 128:(qi + 1) * 128]
                                    has_prev = (tg * 4 + qi) > 0
                                    kprev = KHt[0][rows, (T0 - lo0) + (qi - 1) * 128:(T0 - lo0) + qi * 128] if has_prev else None
                                    kcur = KHt[0][rows, (T0 - lo0) + qi * 128:(T0 - lo0) + (qi + 1) * 128]
                                    sprev, scur = qi, qi + 1
                                    ocols = acc[:, qi * 128:(qi + 1) * 128]
                                else:
                                    r = qi
                                    qcols = QT[rows, 1 * 4 + c, r:512:4]
                                    has_prev = tg > 0
                                    kprev = KHt[1][rows, (T0 - lo1) - 512 + r:(T0 - lo1):4] if has_prev else None
                                    kcur = KHt[1][rows, (T0 - lo1) + r:(T0 - lo1) + 512:4]
                                    sprev, scur = 5 + r, 9 + r
                                    ocols = acc[:, r:512:4]
                                units.append((qcols, has_prev, kprev, kcur, sprev, scur, ocols))

                            def smm(e, units=units, pi=pi):
                                ins = None
                                for u, (qcols, has_prev, kprev, kcur, sprev, scur, ocols) in enumerate(units):
                                    if has_prev:
                                        ins = e.matmul(PS[pi][:, u * 256:u * 256 + 128], lhsT=kprev, rhs=qcols, start=True, stop=True)
                                    ins = e.matmul(PS[pi][:, u * 256 + 128:u * 256 + 256], lhsT=kcur, rhs=qcols, start=True, stop=True)
                                return ins
                            pe(smm, [bf("QT"), bf(f"KH{g}")], [PSB[pi]])
                            c0 = 0 if units[0][1] else 128
                            E, Eb = tb16()
                            act(lambda e, E=E, pi=pi, c0=c0: e.activation(out=E[:, c0:512], in_=PS[pi][:, c0:512], func=AF.Exp, scale=0.125),
                                [PSB[pi]], [Eb])
                            dve(lambda e, E=E, c0=c0: e.tensor_tensor(out=E[:, c0:512], in0=E[:, c0:512], in1=MB[:, c0:512], op=ALU.mult),
                                [Eb, bf("MB")], [Eb])

                            def pmm(e, units=units, E=E, pv_mm=pv_mm):
                                ins = None
                                for u, (qcols, has_prev, kprev, kcur, sprev, scur, ocols) in enumerate(units):
                                    if has_prev:
                                        ins = pv_mm(e, sprev, E[:, u * 256:u * 256 + 128], ocols)
                                    ins = pv_mm(e, scur, E[:, u * 256 + 128:u * 256 + 256], ocols)
                                return ins
                            pending.append(lambda pmm=pmm, Eb=Eb, vab=vab, accb=accb: pe(pmm, [Eb, vab], [accb]))
                            flush(3)
                    pi = sbanks[ucnt[0] % 4]
                    ucnt[0] += 1

                    def smm2(e, pi=pi):
                        ins = None
                        for r in range(16):
                            ins = e.matmul(PS[pi][:, r * 32:(r + 1) * 32], lhsT=KHt[2][rows, r:2048:16],
                                           rhs=QT[rows, 2 * 4 + c, r:512:16], start=True, stop=True)
                        return ins
                    pe(smm2, [bf("QT"), bf("KH2")], [PSB[pi]])
                    E, Eb = tb16()
                    act(lambda e, E=E, pi=pi: e.activation(out=E[:], in_=PS[pi][:, :], func=AF.Exp, scale=0.125), [PSB[pi]], [Eb])
                    dve(lambda e, E=E: e.tensor_tensor(out=E[:], in0=E[:], in1=MG[:, tg, :], op=ALU.mult), [Eb, bf("MG")], [Eb])

                    def pmm2(e, E=E, pv_mm=pv_mm, acc=acc):
                        ins = None
                        for r in range(16):
                            ins = pv_mm(e, 13 + r, E[:, r * 32:(r + 1) * 32], acc[:, r:512:16])
                        return ins
                    pending.append(lambda pmm2=pmm2, Eb=Eb, vab=vab, accb=accb: pe(pmm2, [Eb, vab], [accb]))
                    urow = rows
                    zrow = slice(64, 128) if hh == 0 else slice(0, 64)

                    def norm(acc=acc, accb=accb, urow=urow, zrow=zrow, c=c):
                        rz, rzb = t32()
                        act(lambda e: e.activation(out=rz[urow, :], in_=acc[zrow, :], func=AF.Ln), [accb], [rzb])
                        act(lambda e: e.activation(out=rz[urow, :], in_=rz[urow, :], func=AF.Exp, scale=-1.0), [rzb], [rzb])
                        dve(lambda e: e.tensor_tensor(out=ybT[urow, c, :], in0=acc[urow, :], in1=rz[urow, :], op=ALU.mult),
                            [accb, rzb], [bf("ybT")])
                    pending.append(norm)

            branch_a1(l, 512, 4, hT, bf("hT"), None)
            att_load_v(0)
            att_load_k(0)
            branch_a2(l, 512, hT, bf("hT"), None)
            for c in range(4):
                if c + 1 < 4:
                    att_load_v(c + 1)
                att_pair(c)
                flush(0)
                if c + 1 < 4:
                    att_load_k(c + 1)
            if DBG_STOP == 'ATT':
                return
            merge_out(l, 512, hT, bf("hT"), xT, xTb, ybT, bf("ybT"), [(0, 512, ci)])
            if DBG_STOP == 'BM':
                return
            if l == 0:
                dma_g(XS[:, :, T0:T0 + 512].rearrange("k p t -> p k t"), xT[:], [xTb], [bf("XS")])
            else:
                final_out(xT, xTb, 512, lambda ti: y_p[b, T0 + ti * 128:T0 + (ti + 1) * 128, :], 4, 128)

        def final_out(xsrc, xb_, n, dst_of_tile, ntiles, npart):
            rms_rstd(xsrc[:, :, 0:n], xb_, n, ones_b[:])
            for kc in range(8):
                dve(lambda e, kc=kc: e.scalar_tensor_tensor(out=xsrc[:, kc, 0:n], in0=xsrc[:, kc, 0:n], scalar=SVT[:, 64 + kc:65 + kc],
                                                            in1=rstd[:, 0:n], op0=ALU.mult, op1=ALU.mult), [xb_, bf("rstd"), bf("SVT")], [xb_])
            for ti in range(ntiles):
                xl = XL[ti % 2]
                xlb = bf(f"XL{ti % 2}")
                for hf in range(2):
                    pi = 4 + hf

                    def tr(e, hf=hf, pi=pi, ti=ti):
                        ins = None
                        for k in range(4):
                            kc = hf * 4 + k
                            ins = e.transpose(out=PS[pi][0:npart, k * 128:(k + 1) * 128], in_=xsrc[:, kc, ti * npart:(ti + 1) * npart],
                                              identity=ident_f[:])
                        return ins
                    pe(tr, [xb_, bf("ident_f")], [PSB[pi]])
                    act(lambda e, hf=hf, pi=pi, xl=xl: e.activation(out=xl[0:npart, hf * 512:(hf + 1) * 512], in_=PS[pi][0:npart, :], func=AF.Copy),
                        [PSB[pi]], [xlb])
                dma_g(dst_of_tile(ti), xl[0:npart, :], [xlb], [])

        def branch_a_and_merge(l, n, ntile, hsrc, hb_, xsrc, xb_, ybsrc, ybb, cidx_of_col, sample):
            branch_a1(l, n, ntile, hsrc, hb_, sample)
            branch_a2(l, n, hsrc, hb_, sample)
            merge_out(l, n, hsrc, hb_, xsrc, xb_, ybsrc, ybb, cidx_of_col)

        def branch_a1(l, n, ntile, hsrc, hb_, sample):
            npart = 128 if sample is None else n
            Ws = [wblock(WIN, l, 0, 8, OV + hf * 512) for hf in range(2)]
            for t0 in range(0, ntile, 2):
                tiles = [t for t in (t0, t0 + 1) if t < ntile]
                k = len(tiles)
                for ti in tiles:
                    gv = XL[ti % 2]
                    gvb = bf(f"XL{ti % 2}")
                    for hf in range(2):
                        W, Wb = Ws[hf]
                        pi = (ti % 2) * 2 + hf

                        def mm(e, W=W, ti=ti, pi=pi):
                            ins = None
                            for kc in range(8):
                                ins = e.matmul(PS[pi][0:npart, :], lhsT=hsrc[:, kc, ti * npart:(ti + 1) * npart], rhs=W[:, kc, :],
                                               start=(kc == 0), stop=(kc == 7))
                            return ins
                        pe(mm, [Wb, hb_], [PSB[pi]])
                        act(lambda e, gv=gv, hf=hf, pi=pi: e.activation(out=gv[0:npart, hf * 512:(hf + 1) * 512], in_=PS[pi][0:npart, :],
                                                                       func=AF.Gelu_apprx_tanh), [PSB[pi]], [gvb])
                        dve(lambda e, gv=gv, hf=hf, ti=ti: e.bn_stats(out=st6[0:npart, ti, hf, :], in_=gv[0:npart, hf * 512:(hf + 1) * 512]),
                            [gvb], [bf("st6")])
                    dve(lambda e, ti=ti: e.bn_aggr(out=mv[0:npart, ti, :], in_=st6[0:npart, ti, :, :]), [bf("st6")], [bf("mv")])
                dve(lambda e: e.tensor_scalar(out=lnr[0:npart, t0:t0 + k], in0=mv[0:npart, t0:t0 + k, 1], scalar1=1e-5, scalar2=None, op0=ALU.add),
                    [bf("mv")], [bf("lnr")])
                act(lambda e: e.activation(out=lnr[0:npart, t0:t0 + k], in_=lnr[0:npart, t0:t0 + k], func=AF.Ln), [bf("lnr")], [bf("lnr")])
                act(lambda e: e.activation(out=lnr[0:npart, t0:t0 + k], in_=lnr[0:npart, t0:t0 + k], func=AF.Exp, scale=-0.5),
                    [bf("lnr")], [bf("lnr")])
                for ti in tiles:
                    gv = XL[ti % 2]
                    gvb = bf(f"XL{ti % 2}")
                    dve(lambda e, gv=gv, ti=ti: e.tensor_scalar(out=gv[0:npart, :], in0=gv[0:npart, :], scalar1=mv[0:npart, ti, 0:1],
                                                                scalar2=lnr[0:npart, ti:ti + 1], op0=ALU.subtract, op1=ALU.mult),
                        [gvb, bf("mv"), bf("lnr")], [gvb])
                    pool(lambda e, gv=gv: e.tensor_tensor(out=gv[0:npart, :], in0=gv[0:npart, :], in1=LG[0:npart, :], op=ALU.mult), [gvb, bf("LG")], [gvb])
                    if sample is None:
                        pool(lambda e, gv=gv, ti=ti: e.tensor_tensor(out=vn[:, ti, :], in0=gv[:, :], in1=LB[:, :], op=ALU.add), [gvb, bf("LB")], [bf("vn")])
                    else:
                        pool(lambda e, gv=gv: e.tensor_tensor(out=gv[0:npart, :], in0=gv[0:npart, :], in1=LB[0:npart, :], op=ALU.add), [gvb, bf("LB")], [gvb])
                        dma_g(gmv[l, :, :], gv[0:npart, :], [gvb], [])
                        act(lambda e, gv=gv: e.activation(out=vn[0:npart, 0, :], in_=gv[0:npart, :], func=AF.Copy), [gvb], [bf("vn")])

        def branch_a2(l, n, hsrc, hb_, sample):
            yab = bf("BIG")
            Wu = None
            for g in range(8):
                if g % 4 == 0:
                    Wu = wblock(WIN, l, 0, 8, OU + (g // 4) * 512)
                    Wz = wblock(WIN, l, 0, 8, OZA + (g // 4) * 512)
                ch = g % 4
                bu, bz, bs_ = (0, 1, 2) if g % 2 == 0 else (3, 6, 7)

                def mmf(e, Wt, pi, ch=ch):
                    ins = None
                    for kc in range(8):
                        ins = e.matmul(PS[pi][:, 0:n], lhsT=Wt[:, kc, ch * 128:(ch + 1) * 128], rhs=hsrc[:, kc, 0:n],
                                       start=(kc == 0), stop=(kc == 7))
                    return ins
                pe(lambda e, W=Wu[0]: mmf(e, W, bu), [Wu[1], hb_], [PSB[bu]])
                pe(lambda e, W=Wz[0]: mmf(e, W, bz), [Wz[1], hb_], [PSB[bz]])

                def spm(e, g=g):
                    ins = None
                    if sample is None:
                        for ti in range(4):
                            ins = e.matmul(PS[bs_][:, ti * 128:(ti + 1) * 128], lhsT=vn[:, ti, g * 128:(g + 1) * 128], rhs=wmT[:, l, g, :],
                                           start=True, stop=True)
                    else:
                        ins = e.matmul(PS[bs_][:, 0:n], lhsT=vn[0:n, 0, g * 128:(g + 1) * 128], rhs=WS[0:n, g, 0:n], start=True, stop=True)
                    return ins
                pe(spm, [bf("vn"), bf("wmT"), bf("WS")], [PSB[bs_]])
                gu, gub = t32()
                tz, tzb = t32()
                spb, spbb = t32()
                act(lambda e, gu=gu: e.activation(out=gu[:, 0:n], in_=PS[bu][:, 0:n], func=AF.Gelu_apprx_tanh), [PSB[bu]], [gub])
                act(lambda e, tz=tz: e.activation(out=tz[:, 0:n], in_=PS[bz][:, 0:n], func=AF.Tanh, scale=0.5), [PSB[bz]], [tzb])
                dve(lambda e, tz=tz: e.scalar_tensor_tensor(out=tz[:, 0:n], in0=tz[:, 0:n], scalar=1.0, in1=PS[bz][:, 0:n], op0=ALU.add, op1=ALU.mult),
                    [tzb, PSB[bz]], [tzb])
                if sample is None:
                    bsv = bass.AP(tensor=BSR[:].tensor, offset=BSR[0, g * 128].offset, ap=[list(BSR[:].ap[0]), [0, 4], [1, 128]])
                    dve(lambda e, spb=spb, bsv=bsv: e.tensor_tensor(out=spb[:, :].rearrange("p (a t) -> p a t", t=128),
                                                                    in0=PS[bs_][:, :].rearrange("p (a t) -> p a t", t=128), in1=bsv, op=ALU.add),
                        [PSB[bs_], bf("BSR")], [spbb])
                else:
                    bsv = bass.AP(tensor=BSR[:].tensor, offset=BSR[0, g * 128].offset, ap=[list(BSR[:].ap[0]), [0, n // T], [1, T]])
                    dve(lambda e, spb=spb, bsv=bsv: e.tensor_tensor(out=spb[:, 0:n].rearrange("p (a t) -> p a t", t=T),
                                                                    in0=PS[bs_][:, 0:n].rearrange("p (a t) -> p a t", t=T), in1=bsv, op=ALU.add),
                        [PSB[bs_], bf("BSR")], [spbb])
                dve(lambda e, gu=gu, spb=spb: e.tensor_tensor(out=gu[:, 0:n], in0=gu[:, 0:n], in1=spb[:, 0:n], op=ALU.mult), [gub, spbb], [gub])
                dve(lambda e, gu=gu, tz=tz, g=g: e.tensor_tensor(out=BIG[:, g, 0:n], in0=gu[:, 0:n], in1=tz[:, 0:n], op=ALU.mult), [gub, tzb], [yab])

        def merge_out(l, n, hsrc, hb_, xsrc, xb_, ybsrc, ybb, cidx_of_col):
            yab = bf("BIG")
            Wzb = wblock(WIN, l, 0, 8, OZB)
            for c in range(4):
                bq = c % 2
                pe(lambda e, c=c, bq=bq: mmf_generic(e, Wzb[0], bq, c, hsrc, n, 8), [Wzb[1], hb_], [PSB[bq]])
                tz, tzb = t32()
                act(lambda e, tz=tz, bq=bq: e.activation(out=tz[:, 0:n], in_=PS[bq][:, 0:n], func=AF.Tanh, scale=0.5), [PSB[bq]], [tzb])
                dve(lambda e, tz=tz, bq=bq: e.scalar_tensor_tensor(out=tz[:, 0:n], in0=tz[:, 0:n], scalar=1.0, in1=PS[bq][:, 0:n], op0=ALU.add, op1=ALU.mult),
                    [tzb, PSB[bq]], [tzb])
                dve(lambda e, tz=tz, c=c: e.tensor_tensor(out=ybsrc[:, c, 0:n], in0=ybsrc[:, c, 0:n], in1=tz[:, 0:n], op=ALU.mult), [tzb, ybb], [ybb])
            for oc in range(8):
                if oc % 4 == 0:
                    Wga = wblock(WIN, l, 0, 8, OGA + (oc // 4) * 512)
                    Wgb = wblock(WIN, l, 0, 8, OGB + (oc // 4) * 512)
                ch = oc % 4
                Wg1 = wblock(WGM, l, 0, 8, oc * 128, 128)
                Wa1 = wblock(WATT, l, 0, 4, oc * 128, 128)
                b0 = 0 if oc % 2 == 0 else 4
                pe(lambda e, ch=ch, W=Wga[0], b0=b0: mmf_generic(e, W, b0, ch, hsrc, n, 8), [Wga[1], hb_], [PSB[b0]])
                pe(lambda e, ch=ch, W=Wgb[0], b0=b0: mmf_generic(e, W, b0 + 1, ch, hsrc, n, 8), [Wgb[1], hb_], [PSB[b0 + 1]])
                pe(lambda e, W=Wg1[0], b0=b0: mmf_generic(e, W, b0 + 2, 0, BIG, n, 8), [Wg1[1], yab], [PSB[b0 + 2]])
                pe(lambda e, W=Wa1[0], b0=b0: mmf_generic(e, W, b0 + 3, 0, ybsrc, n, 4), [Wa1[1], ybb], [PSB[b0 + 3]])
                ta, tab_ = t32()
                tb_, tbb_ = t32()
                act(lambda e, ta=ta, b0=b0: e.activation(out=ta[:, 0:n], in_=PS[b0][:, 0:n], func=AF.Tanh, scale=0.5), [PSB[b0]], [tab_])
                act(lambda e, tb_=tb_, b0=b0: e.activation(out=tb_[:, 0:n], in_=PS[b0 + 1][:, 0:n], func=AF.Tanh, scale=0.5), [PSB[b0 + 1]], [tbb_])
                dve(lambda e, ta=ta, b0=b0: e.scalar_tensor_tensor(out=ta[:, 0:n], in0=ta[:, 0:n], scalar=1.0, in1=PS[b0 + 2][:, 0:n], op0=ALU.add, op1=ALU.mult),
                    [tab_, PSB[b0 + 2]], [tab_])
                dve(lambda e, tb_=tb_, b0=b0: e.scalar_tensor_tensor(out=tb_[:, 0:n], in0=tb_[:, 0:n], scalar=1.0, in1=PS[b0 + 3][:, 0:n], op0=ALU.add, op1=ALU.mult),
                    [tbb_, PSB[b0 + 3]], [tbb_])
                dve(lambda e, ta=ta, tb_=tb_, oc=oc: e.tensor_tensor(out=MGT[:, oc, 0:n], in0=ta[:, 0:n], in1=tb_[:, 0:n], op=ALU.add),
                     [tab_, tbb_], [bf("MGT")])
            for oc in range(8):
                Wo1 = wblock(WO, l, 0, 8, oc * 128, 128)
                bo = oc % 4
                pe(lambda e, W=Wo1[0], bo=bo: mmf_generic(e, W, bo, 0, MGT, n, 8), [Wo1[1], bf("MGT")], [PSB[bo]])
                for (c0, c1, ci) in cidx_of_col:
                    dve(lambda e, oc=oc, c0=c0, c1=c1, ci=ci, bo=bo: e.scalar_tensor_tensor(
                        out=xsrc[:, oc, c0:c1], in0=PS[bo][:, c0:c1], scalar=G4[:, l, oc, ci:ci + 1], in1=xsrc[:, oc, c0:c1],
                        op0=ALU.mult, op1=ALU.add), [PSB[bo], bf("G4"), xb_], [xb_])

        def mmf_generic(e, Wt, pi, ch, src, n, nk):
            ins = None
            for kc in range(nk):
                ins = e.matmul(PS[pi][:, 0:n], lhsT=Wt[:, kc, ch * 128:(ch + 1) * 128], rhs=src[:, kc, 0:n],
                               start=(kc == 0), stop=(kc == nk - 1))
            return ins

        def sample_layer(l):
            n = TS
            xb_ = bf("xsT")
            if cur_l[0] != l:
                load_layer_vecs(l)
                cur_l[0] = l
            if l == 0:
                xl = XL[0]
                xlb = bf("XL0")
                dma_in(xl[0:n, :], x_s[:, :], [], [xlb])
                for hf in range(2):
                    pi = 4 + hf

                    def tr(e, hf=hf, pi=pi):
                        ins = None
                        for k in range(4):
                            kc = hf * 4 + k
                            ins = e.transpose(out=PS[pi][:, k * 128:k * 128 + n], in_=xl[0:n, kc * 128:(kc + 1) * 128], identity=ident_f[0:n, 0:n])
                        return ins
                    pe(tr, [xlb, bf("ident_f")], [PSB[pi]])
                    act(lambda e, hf=hf, pi=pi: e.activation(out=xsT[:, hf * 4:hf * 4 + 4, :],
                                                            in_=PS[pi][:, :].rearrange("p (k t) -> p k t", t=128)[:, :, 0:n], func=AF.Copy),
                        [PSB[pi]], [xb_])
            for bs in range(NBS):
                dma_in(WS[bs * T:(bs + 1) * T, :, bs * T:(bs + 1) * T], WMT[l, :, 0:T, 0:T].rearrange("g s t -> s g t"), [bf("WMT")], [bf("WS")])
            if DBG_SSTOP == 'SX':
                return
            cols = [(bs * T, (bs + 1) * T, NBP + bs) for bs in range(NBS)]
            rms_rstd(xsT[:], xb_, n, ones_b[:])
            make_h(xsT, xb_, n, l, cols)
            hb_ = bf("hsT")
            if DBG_SSTOP == 'SH':
                return
            for cb in range(6):
                isk = cb >= 3
                g = cb % 3
                W, Wb = wblock(WIN, l, 0, 8, OQ + cb * 512)

                def mm(e, W=W):
                    ins = None
                    for kc in range(8):
                        ins = e.matmul(PS[0][0:n, :], lhsT=hsT[:, kc, :], rhs=W[:, kc, :], start=(kc == 0), stop=(kc == 7))
                    return ins
                pe(mm, [Wb, hb_], [PSB[0]])
                m1, m1b, m2, m2b, p3, m13, m23, ccb, ss1, ss2 = rope_block(PS[0][0:n, :], CCs[0:n, :], SSs[0:n, :], n, isk)
                dve(lambda e, m13=m13, p3=p3, ccb=ccb: e.tensor_tensor(out=m13, in0=p3, in1=ccb, op=ALU.mult), [PSB[0], bf("CCs")], [m1b])
                dve(lambda e, m23=m23, p3=p3, ss1=ss1: e.tensor_tensor(out=m23[:, :, 0:32], in0=p3[:, :, 32:64], in1=ss1, op=ALU.mult), [PSB[0], bf("SSs")], [m2b])
                dve(lambda e, m23=m23, p3=p3, ss2=ss2: e.tensor_tensor(out=m23[:, :, 32:64], in0=p3[:, :, 0:32], in1=ss2, op=ALU.mult), [PSB[0], bf("SSs")], [m2b])
                ob, obb = tb16()
                dve(lambda e, m1=m1, m2=m2: e.tensor_tensor(out=m1[0:n, :], in0=m1[0:n, :], in1=m2[0:n, :], op=ALU.add), [m1b, m2b], [m1b])
                act(lambda e, ob=ob, m1=m1: e.activation(out=ob[0:n, :], in_=m1[0:n, :], func=AF.Copy), [m1b], [obb])
                if isk:
                    dma_g(kvs[g][l, :, 0, :], m1[0:n, :], [m1b], [])

                def tr(e, ob=ob):
                    ins = None
                    pv = PS[4][:, :].bitcast(BF16)
                    for c in range(4):
                        ins = e.transpose(out=pv[:, c * 128:c * 128 + n], in_=ob[0:n, c * 128:(c + 1) * 128], identity=ident_b[0:n, 0:n])
                    return ins
                pe(tr, [obb, bf("ident_b")], [PSB[4]])
                dst = KTs if isk else QTs
                dstb = bf("KTs") if isk else bf("QTs")
                act(lambda e, dst=dst, g=g: e.activation(out=dst[:, g * 4:g * 4 + 4, :],
                                                         in_=PS[4][:, :].bitcast(BF16)[:, 0:512].rearrange("p (c t) -> p c t", t=128)[:, :, 0:n],
                                                         func=AF.Copy), [PSB[4]], [dstb])
            if DBG_SSTOP == 'SQK':
                return
            for g in range(3):
                W, Wb = wblock(WIN, l, 0, 8, OVAL + g * 512)

                def mm(e, W=W):
                    ins = None
                    for kc in range(8):
                        ins = e.matmul(PS[0][0:n, :], lhsT=hsT[:, kc, :], rhs=W[:, kc, :], start=(kc == 0), stop=(kc == 7))
                    return ins
                pe(mm, [Wb, hb_], [PSB[0]])
                vf, vfb = t32()
                act(lambda e, vf=vf: e.activation(out=vf[0:n, :], in_=PS[0][0:n, :], func=AF.Copy), [PSB[0]], [vfb])
                dma_g(kvs[g][l, :, 1, :], vf[0:n, :], [vfb], [])
            if DBG_SSTOP == 'SV':
                return
            for bs in range(NBS):
                sample_attention(l, bs)
            if DBG_SSTOP == 'SATT':
                return
            branch_a_and_merge(l, n, 1, hsT, hb_, xsT, xb_, ybs, bf("ybs"), cols, True)
            if l == DEPTH - 1:
                final_out(xsT, xb_, n, lambda ti: y_s[:, :], 1, n)


        def sample_attention(l, bs):
            n = TS
            hb_ = bf("hsT")
            for g in range(3):
                W, Wb = wblock(WIN, l, 0, 8, OVAL + g * 512)

                def mm2(e, W=W):
                    ins = None
                    for kc in range(8):
                        ins = e.matmul(PS[1][0:T, :], lhsT=hsT[:, kc, bs * T:(bs + 1) * T], rhs=W[:, kc, :], start=(kc == 0), stop=(kc == 7))
                    return ins
                pe(mm2, [Wb, hb_], [PSB[1]])
                vdst = bass.AP(tensor=VAnB1[:].tensor, offset=VAnB1[0, g, 0, 0].offset, ap=[list(VAnB1[:].ap[0]), [192, 4], [128, 2], [1, 64]])
                act(lambda e, vdst=vdst: e.activation(out=vdst, in_=PS[1][0:T, :].rearrange("p (c h d) -> p c h d", h=2, d=64), func=AF.Copy),
                    [PSB[1]], [bf("VAnB")])
            acc = PS[5]
            accb = PSB[5]
            first = [True]
            tiles = [(0, 0, 0)] + [(1, r, 1 + r) for r in range(4)] + [(2, r, 5 + r) for r in range(8)]
            spend = []
            for idx, (g, r, mi) in enumerate(tiles):
                par = idx % 2
                d = DILS[g][1]
                ct = CT[par]
                ctb = bf(f"CT{par}")
                ktc = (KTc, KTcB)[par]
                vac = (VAc, VAcB)[par]
                es = (Es, EsB)[par]
                ktb, vab_, esb = bf(f"KTc{par}"), bf(f"VAc{par}"), bf(f"Es{par}")
                pt = 2 + par
                be, bo = (7, 6) if par == 0 else (1, 0)
                src = bass.AP(tensor=ck[g].tensor, offset=ck[g][l, bs, r, 0, 0].offset, ap=[[d * 1024, 128], [1, 1024]])
                dma_in(ct[:], src, [], [ctb])

                kb, kbb = tb16()
                dve(lambda e, kb=kb, ct=ct: e.tensor_copy(out=kb[:], in_=ct[:, 0:512]), [ctb], [kbb])

                def tr(e, kb=kb, pt=pt):
                    ins = None
                    pv = PS[pt][:, :].bitcast(BF16)
                    for c in range(4):
                        ins = e.transpose(out=pv[:, c * 128:(c + 1) * 128], in_=kb[:, c * 128:(c + 1) * 128], identity=ident_b[:])
                    return ins
                pe(tr, [kbb, bf("ident_b")], [PSB[pt]])
                act(lambda e, ktc=ktc, pt=pt: e.activation(out=ktc[:], in_=PS[pt][:, :].bitcast(BF16)[:, 0:512].rearrange("p (c t) -> p c t", t=128),
                                                           func=AF.Copy), [PSB[pt]], [ktb])
                vdst = bass.AP(tensor=vac[:].tensor, offset=vac[0, 0, 0].offset, ap=[list(vac[:].ap[0]), [192, 4], [128, 2], [1, 64]])
                dve(lambda e, ct=ct, vdst=vdst: e.tensor_copy(out=vdst, in_=ct[:, 512:1024].rearrange("p (c h d) -> p c h d", h=2, d=64)),
                    [ctb], [vab_])

                def rest(g=g, mi=mi, ktc=ktc, vac=vac, es=es, ktb=ktb, vab_=vab_, esb=esb, be=be, bo=bo):
                    def smm(e):
                        ins = None
                        for h in (0, 2, 4, 6, 1, 3, 5, 7):
                            rows = slice((h % 2) * 64, (h % 2) * 64 + 64)
                            ins = e.matmul(PS[be if h % 2 == 0 else bo][:, (h // 2) * T:(h // 2 + 1) * T], lhsT=ktc[rows, h // 2, :],
                                           rhs=QTs[rows, g * 4 + h // 2, bs * T:(bs + 1) * T], start=True, stop=True)
                        return ins
                    pe(smm, [ktb, bf("QTs")], [PSB[be], PSB[bo]])
                    es4 = es[:, :].rearrange("p (c h t) -> p c h t", h=2, t=T)
                    for hh in range(2):
                        bk = be if hh == 0 else bo
                        act(lambda e, hh=hh, bk=bk: e.activation(out=es4[:, :, hh, :], in_=PS[bk][:, 0:4 * T].rearrange("p (c t) -> p c t", t=T),
                                                                 func=AF.Exp, scale=0.125), [PSB[bk]], [esb])
                    msk = bass.AP(tensor=MS[:].tensor, offset=MS[0, mi, 0].offset, ap=[list(MS[:].ap[0]), [0, 8], [1, T]])
                    dve(lambda e: e.tensor_tensor(out=es[:, :].rearrange("p (h t) -> p h t", t=T), in0=es[:, :].rearrange("p (h t) -> p h t", t=T),
                                                  in1=msk, op=ALU.mult), [esb, bf("MS")], [esb])

                    def pmm(e):
                        ins = None
                        for h in range(8):
                            vcol = slice(0, 128) if h % 2 == 0 else slice(64, 192)
                            ins = e.matmul(acc[:, h * T:(h + 1) * T], lhsT=vac[:, h // 2, vcol], rhs=es[:, h * T:(h + 1) * T], start=first[0], stop=False,
                                           skip_group_check=True)
                            first[0] = False
                        return ins
                    pe(pmm, [vab_, esb], [accb])
                spend.append(rest)
                while len(spend) > 1:
                    spend.pop(0)()
            while spend:
                spend.pop(0)()
            if DBG_SA <= 3:
                return
            for g in range(3):
                def smm(e, g=g):
                    ins = None
                    for h in (0, 2, 4, 6, 1, 3, 5, 7):
                        rows = slice((h % 2) * 64, (h % 2) * 64 + 64)
                        ins = e.matmul(PS[7 - (h % 2)][0:T, (h // 2) * T:(h // 2 + 1) * T], lhsT=KTs[rows, g * 4 + h // 2, bs * T:(bs + 1) * T],
                                       rhs=QTs[rows, g * 4 + h // 2, bs * T:(bs + 1) * T], start=True, stop=True)
                    return ins
                pe(smm, [bf("KTs"), bf("QTs")], [PSB[7], PSB[6]])
                if DBG_SA == 35:
                    continue
                Esn4 = Esn[:, :].rearrange("p (c h t) -> p c h t", h=2, t=T)
                for hh in range(2):
                    act(lambda e, hh=hh: e.activation(out=Esn4[:, :, hh, :], in_=PS[7 - hh][0:T, 0:4 * T].rearrange("p (c t) -> p c t", t=T),
                                                      func=AF.Exp, scale=0.125), [PSB[7 - hh]], [bf("Esn")])
                msk = bass.AP(tensor=MN[:].tensor, offset=MN[0, g, 0].offset, ap=[list(MN[:].ap[0]), [0, 8], [1, T]])
                dve(lambda e, msk=msk: e.tensor_tensor(out=Esn[:, :].rearrange("p (h t) -> p h t", t=T), in0=Esn[:, :].rearrange("p (h t) -> p h t", t=T),
                                                       in1=msk, op=ALU.mult), [bf("Esn"), bf("MN")], [bf("Esn")])
                if DBG_SA == 36:
                    continue

                def pmm(e, g=g):
                    ins = None
                    for h in range(8):
                        vcol = slice(0, 128) if h % 2 == 0 else slice(64, 192)
                        ins = e.matmul(acc[:, h * T:(h + 1) * T], lhsT=VAnB[bs][:, g, h // 2, vcol], rhs=Esn[:, h * T:(h + 1) * T], start=False, stop=False,
                                       skip_group_check=True)
                    return ins
                pe(pmm, [bf("VAnB"), bf("Esn")], [accb])
            if DBG_SA <= 4 or DBG_SA in (35, 36):
                return
            rz, rzb = t32()
            for hh in range(2):
                urow = slice(hh * 64, hh * 64 + 64)
                zrow = slice(64, 128) if hh == 0 else slice(0, 64)
                a3 = acc[:, 0:64].rearrange("p (c h t) -> p c h t", h=2, t=T)
                r3 = rz[:, 0:32].rearrange("p (c t) -> p c t", t=T)
                act(lambda e, urow=urow, zrow=zrow, hh=hh: e.activation(out=r3[urow], in_=a3[zrow, :, hh, :], func=AF.Ln), [accb], [rzb])
                act(lambda e, urow=urow: e.activation(out=r3[urow], in_=r3[urow], func=AF.Exp, scale=-1.0), [rzb], [rzb])
                dve(lambda e, urow=urow, hh=hh: e.tensor_tensor(out=ybs[urow, :, bs * T:(bs + 1) * T], in0=a3[urow, :, hh, :], in1=r3[urow], op=ALU.mult),
                    [accb, rzb], [bf("ybs")])

        ng = 0
        for b in range(NBP):
            for l in range(DEPTH):
                for tg in range(4):
                    if ng < DBG_NG:
                        prompt_group(b, l, tg)
                    ng += 1
        if NBS > 0 and DBG_SAMPLE:
            for l in range(DEPTH):
                sample_layer(l)
        sc.finish()

    return nc


_NC_CACHE = {}


def _run(inputs, NBP, NBS, ncores):
    key = (NBP, NBS)
    if key not in _NC_CACHE:
        _NC_CACHE[key] = build(NBP, NBS)
    nc = _NC_CACHE[key]
    f = lambda a: np.ascontiguousarray(np.asarray(a, dtype=np.float32))
    cst = _consts()
    shared = {k: f(inputs[k]) for k in ("w_ada", "b_ada", "norm_g", "w_in", "gm_ln_g", "gm_ln_b", "gm_ws", "gm_bs",
                                        "w_gm_out", "w_att_out", "w_o", "final_g")}
    shared.update(cst)
    xp, xs = f(inputs["x_prompt"]), f(inputs["x_sample"])
    cp, cs = f(inputs["c_prompt"]), f(inputs["c_sample"])
    caches = [f(inputs["cache_kv_w128"]), f(inputs["cache_kv_w512"]), f(inputs["cache_kv_w2048"])]
    in_maps = []
    for i in range(ncores):
        m = dict(shared)
        m["x_p"] = np.ascontiguousarray(xp[i * NBP:(i + 1) * NBP])
        m["x_s"] = np.ascontiguousarray(xs[i * NBS:(i + 1) * NBS].reshape(NBS * T, D))
        m["c_all"] = np.ascontiguousarray(np.concatenate([cp[i * NBP:(i + 1) * NBP], cs[i * NBS:(i + 1) * NBS]], 0))
        for g in range(3):
            cg = caches[g][:, i * NBS:(i + 1) * NBS]
            m[f"ck{g}"] = np.ascontiguousarray(cg.reshape(DEPTH, NBS, cg.shape[2], 2, 512))
        in_maps.append(m)
    res = run_bass_kernel_spmd(nc, in_maps, core_ids=list(range(ncores)))
    R = res.results
    y_p = np.concatenate([r["y_p"] for r in R], 0)
    y_s = np.concatenate([r["y_s"].reshape(NBS, T, D) for r in R], 0)
    outs = [y_p, y_s]
    for g in range(3):
        keep = DILS[g][0]
        outs.append(np.concatenate([r[f"kvp{g}"].reshape(DEPTH, NBP, keep, 2, 8, 64) for r in R], 1))
    for g in range(3):
        outs.append(np.concatenate([r[f"kvs{g}"].reshape(DEPTH, NBS, T, 2, 8, 64) for r in R], 1))
    outs.append(np.concatenate([r["gmv"].reshape(DEPTH, NBS, T, D) for r in R], 1))
    return tuple(np.ascontiguousarray(o.astype(np.float32)) for o in outs)


def kernel(**inputs):
    return _run(inputs, 2, 4, NCORES)
```
